# Optimizing a Trainium2 kernel written in Bass

```python
import jax, jax.numpy as jnp
from jax import lax
import numpy as np

D_MODEL = 1024
BATCH = 8
SEQ = 2048
DEPTH = 4
DEC_BATCH = 8
DEC_SEQ = 64
PAST_LEN = 2048

CHUNK = 64
Q_BLOCK = 128
N_EVEN = (DEPTH + 1) // 2
N_ODD = DEPTH // 2
WA = D_MODEL // 2
HA = 8
BWA = WA // HA
CONV_W = 4
LRU_C = 8.0
HB = 8
DHB = (D_MODEL // 2) // HB
WB = HB * DHB
AB_IN = 2 * WA + 3 * WB
AB_OUT = WA + WB
HC = D_MODEL // 256
DKC = D_MODEL // HC
DVC = 2 * DKC
WC_V = HC * DVC
C_IN = 2 * D_MODEL + 2 * WC_V
ROPE_BASE = 10000.0
D_FF = 2816
P_DIM = 256
N_NORMS = 8
EPS = 1e-6
F32 = jnp.float32

kernel_name = 'hybrid_lru_stickbreak_retention_stream_step'


def rms_norm(x, g):
    xf = x.astype(F32)
    y = xf * lax.rsqrt(jnp.mean(xf * xf, axis=-1, keepdims=True) + EPS)
    return (y * g.astype(F32)).astype(x.dtype)


def swiglu(x, w_gate, w_up, w_down):
    return (jax.nn.silu(x @ w_gate) * (x @ w_up)) @ w_down


def causal_conv(x, buf, w, b):
    L = x.shape[1]
    xp = jnp.concatenate([buf.astype(x.dtype), x], axis=1)
    y = b
    for tap in range(CONV_W):
        y = y + xp[:, tap:tap + L] * w[tap]
    return y, xp[:, -(CONV_W - 1):]


def rg_lru(xc, h0, lam, w_r, b_r, w_i, b_i):
    B, L, _ = xc.shape
    xf = xc.astype(F32)
    xh = xf.reshape(B, L, HA, BWA)
    rg = jax.nn.sigmoid(jnp.einsum('blhi,hij->blhj', xh, w_r.astype(F32)).reshape(B, L, WA) + b_r.astype(F32))
    ig = jax.nn.sigmoid(jnp.einsum('blhi,hij->blhj', xh, w_i.astype(F32)).reshape(B, L, WA) + b_i.astype(F32))
    log_a = -LRU_C * rg * jax.nn.softplus(-lam.astype(F32))
    a = jnp.exp(log_a)
    u = jnp.sqrt(-jnp.expm1(2.0 * log_a)) * ig * xf
    u = u.at[:, 0].add(a[:, 0] * h0.astype(F32))

    def combine(left, right):
        a1, b1 = left
        a2, b2 = right
        return a1 * a2, a2 * b1 + b2

    _, h = lax.associative_scan(combine, (a, u), axis=1)
    return h, h[:, -1]


def stick_breaking_block(q, k, v, q_pos0):
    z = jnp.einsum('bqhd,bkhd->bhqk', q.astype(F32), k.astype(F32)) * (DHB ** -0.5)
    t = q_pos0 + jnp.arange(q.shape[1])
    s = jnp.arange(k.shape[1])
    mask = (s[None, :] < t[:, None])[None, None]
    log_1mb = jnp.where(mask, jax.nn.log_sigmoid(-z), 0.0)
    tail = lax.cumsum(log_1mb, axis=3, reverse=True) - log_1mb
    att = jnp.where(mask, jnp.exp(jax.nn.log_sigmoid(z) + tail), 0.0)
    return jnp.einsum('bhqk,bkhd->bqhd', att, v.astype(F32)).astype(q.dtype)


def stick_breaking(q, k_all, v_all):
    L = q.shape[1]
    P = k_all.shape[1] - L
    outs = []
    for s0 in range(0, L, Q_BLOCK):
        e = min(s0 + Q_BLOCK, L)
        outs.append(stick_breaking_block(q[:, s0:e], k_all[:, :P + e], v_all[:, :P + e], P + s0))
    return jnp.concatenate(outs, axis=1)


def mixer_ab(h, conv_buf, lru_h0, past_k, past_v, w_in, w_out, conv_w, conv_b, w_r, b_r, w_i, b_i, lam):
    B, L, _ = h.shape
    xa, ga, q, k, v = jnp.split(h @ w_in, [WA, 2 * WA, 2 * WA + WB, 2 * WA + 2 * WB], axis=-1)
    xc, new_buf = causal_conv(xa, conv_buf, conv_w, conv_b)
    hs, h_last = rg_lru(xc, lru_h0, lam, w_r, b_r, w_i, b_i)
    out_a = jax.nn.gelu(ga) * hs.astype(h.dtype)
    q = q.reshape(B, L, HB, DHB)
    k = k.reshape(B, L, HB, DHB)
    v = v.reshape(B, L, HB, DHB)
    k_all = jnp.concatenate([past_k.astype(k.dtype), k], axis=1)
    v_all = jnp.concatenate([past_v.astype(v.dtype), v], axis=1)
    out_b = stick_breaking(q, k_all, v_all).reshape(B, L, WB)
    y = jnp.concatenate([out_a, out_b], axis=-1) @ w_out
    return y, h_last.astype(h.dtype), new_buf, k, v


def rotary(x, pos):
    half = x.shape[-1] // 2
    freq = ROPE_BASE ** (-jnp.arange(half, dtype=F32) / half)
    ang = pos.astype(F32)[:, None] * freq[None, :]
    cos = jnp.cos(ang)[None, :, None, :]
    sin = jnp.sin(ang)[None, :, None, :]
    x1, x2 = x[..., :half], x[..., half:]
    return jnp.concatenate([x1 * cos - x2 * sin, x2 * cos + x1 * sin], axis=-1)


def retention_chunk(q, k, v, R, log_g):
    L = q.shape[1]
    idx = jnp.arange(L, dtype=F32)
    dist = jnp.abs(idx[:, None] - idx[None, :])
    decay = jnp.exp(log_g[:, None, None] * dist)
    scores = jnp.einsum('blhd,bmhd->bhlm', q, k) * decay[None]
    o = jnp.einsum('bhlm,bmhe->blhe', scores, v)
    xi = jnp.exp(log_g[None, :] * (idx[:, None] + 1.0))
    o = o + jnp.einsum('blhd,bhde->blhe', q, R) * xi[None, :, :, None]
    zeta = jnp.exp(log_g[None, :] * (L - 1.0 - idx[:, None]))
    R = jnp.exp(log_g * L)[None, :, None, None] * R + jnp.einsum('blhd,blhe->bhde', k * zeta[None, :, :, None], v)
    return o, R


def mixer_c(h, R0, pos0, w_in, w_out):
    B, L, _ = h.shape
    q, k, v, g = jnp.split(h @ w_in, [D_MODEL, 2 * D_MODEL, 2 * D_MODEL + WC_V], axis=-1)
    pos = pos0 + jnp.arange(L)
    q = rotary(q.reshape(B, L, HC, DKC).astype(F32), pos)
    k = rotary(k.reshape(B, L, HC, DKC).astype(F32), pos) * (DKC ** -0.5)
    v = v.reshape(B, L, HC, DVC).astype(F32)
    log_g = jnp.log(1.0 - 2.0 ** (-5.0 - jnp.arange(HC, dtype=F32)))
    cl = min(L, CHUNK)
    n = L // cl

    def to_chunks(t):
        return t.reshape(B, n, cl, *t.shape[2:]).swapaxes(0, 1)

    def step(R, qkv):
        o, R = retention_chunk(qkv[0], qkv[1], qkv[2], R, log_g)
        return R, o

    R_last, o = lax.scan(step, R0.astype(F32), (to_chunks(q), to_chunks(k), to_chunks(v)))
    o = o.swapaxes(0, 1).reshape(B, L, HC, DVC)
    mu = jnp.mean(o, axis=-1, keepdims=True)
    var = jnp.mean(jnp.square(o - mu), axis=-1, keepdims=True)
    o = ((o - mu) * lax.rsqrt(var + EPS)).reshape(B, L, WC_V)
    y = (jax.nn.silu(g.astype(F32)) * o).astype(h.dtype) @ w_out
    return y, R_last.astype(h.dtype)


def run_group(x, p, conv_state, lru_state, past_k, past_v, ret_state,
              norm_g, ffn_w_gate, ffn_w_up, ffn_w_down, ple_w_in, ple_w_gate,
              ab_w_in, ab_w_out, lru_conv_w, lru_conv_b, lru_w_r, lru_b_r, lru_w_i, lru_b_i, lru_lambda,
              ret_w_in, ret_w_out):
    pos0 = past_k.shape[2]
    new_h, new_conv, new_k, new_v, new_ret = [], [], [], [], []
    for i in range(DEPTH):
        g = norm_g[i]
        j = i // 2
        x = x + 0.5 * rms_norm(swiglu(rms_norm(x, g[0]), ffn_w_gate[i, 0], ffn_w_up[i, 0], ffn_w_down[i, 0]), g[1])
        hn = rms_norm(x, g[2])
        if i % 2 == 0:
            mix, h_last, buf, k_rows, v_rows = mixer_ab(
                hn, conv_state[j], lru_state[j], past_k[j], past_v[j], ab_w_in[j], ab_w_out[j],
                lru_conv_w[j], lru_conv_b[j], lru_w_r[j], lru_b_r[j], lru_w_i[j], lru_b_i[j], lru_lambda[j])
            new_h.append(h_last)
            new_conv.append(buf)
            new_k.append(k_rows)
            new_v.append(v_rows)
        else:
            mix, R = mixer_c(hn, ret_state[j], pos0, ret_w_in[j], ret_w_out[j])
            new_ret.append(R)
        x = x + rms_norm(mix, g[3])
        x = x + 0.5 * rms_norm(swiglu(rms_norm(x, g[4]), ffn_w_gate[i, 1], ffn_w_up[i, 1], ffn_w_down[i, 1]), g[5])
        gate = jax.nn.sigmoid(rms_norm(x, g[6]) @ ple_w_gate[i])
        x = x + rms_norm(gate * (p[i] @ ple_w_in[i]), g[7])
    return x, jnp.stack(new_h), jnp.stack(new_conv), jnp.stack(new_k), jnp.stack(new_v), jnp.stack(new_ret)


def setup_inputs(seed: int = 0) -> dict:
    key = jax.random.key(seed)
    ks = jax.random.split(key, 32)
    nrm = jax.random.normal
    u = jax.random.uniform(ks[25], (N_EVEN, WA), F32, 0.9, 0.999)
    a = u ** (1.0 / LRU_C)
    lam = jnp.log(a) - jnp.log1p(-a)
    return {
        'x_prompt': nrm(ks[0], (BATCH, SEQ, D_MODEL), F32),
        'x_sample': nrm(ks[1], (DEC_BATCH, DEC_SEQ, D_MODEL), F32),
        'p_prompt': nrm(ks[2], (DEPTH, BATCH, SEQ, P_DIM), F32),
        'p_sample': nrm(ks[3], (DEPTH, DEC_BATCH, DEC_SEQ, P_DIM), F32),
        'state_lru_h': 0.5 * nrm(ks[4], (N_EVEN, DEC_BATCH, WA), F32),
        'state_conv': nrm(ks[5], (N_EVEN, DEC_BATCH, CONV_W - 1, WA), F32),
        'cache_sb_k': nrm(ks[6], (N_EVEN, DEC_BATCH, PAST_LEN, HB, DHB), F32),
        'cache_sb_v': nrm(ks[7], (N_EVEN, DEC_BATCH, PAST_LEN, HB, DHB), F32),
        'state_ret': 0.1 * nrm(ks[8], (N_ODD, DEC_BATCH, HC, DKC, DVC), F32),
        'norm_g': 1.0 + 0.05 * nrm(ks[9], (DEPTH, N_NORMS, D_MODEL), F32),
        'ffn_w_gate': nrm(ks[10], (DEPTH, 2, D_MODEL, D_FF), F32) * D_MODEL ** -0.5,
        'ffn_w_up': nrm(ks[11], (DEPTH, 2, D_MODEL, D_FF), F32) * D_MODEL ** -0.5,
        'ffn_w_down': nrm(ks[12], (DEPTH, 2, D_FF, D_MODEL), F32) * D_FF ** -0.5,
        'ple_w_in': nrm(ks[13], (DEPTH, P_DIM, D_MODEL), F32) * P_DIM ** -0.5,
        'ple_w_gate': nrm(ks[14], (DEPTH, D_MODEL, D_MODEL), F32) * D_MODEL ** -0.5,
        'ab_w_in': nrm(ks[15], (N_EVEN, D_MODEL, AB_IN), F32) * D_MODEL ** -0.5,
        'ab_w_out': nrm(ks[16], (N_EVEN, AB_OUT, D_MODEL), F32) * AB_OUT ** -0.5,
        'lru_conv_w': nrm(ks[17], (N_EVEN, CONV_W, WA), F32) * CONV_W ** -0.5,
        'lru_conv_b': 0.01 * nrm(ks[18], (N_EVEN, WA), F32),
        'lru_w_r': nrm(ks[19], (N_EVEN, HA, BWA, BWA), F32) * BWA ** -0.5,
        'lru_b_r': 0.01 * nrm(ks[20], (N_EVEN, WA), F32),
        'lru_w_i': nrm(ks[21], (N_EVEN, HA, BWA, BWA), F32) * BWA ** -0.5,
        'lru_b_i': 0.01 * nrm(ks[22], (N_EVEN, WA), F32),
        'lru_lambda': lam,
        'ret_w_in': nrm(ks[23], (N_ODD, D_MODEL, C_IN), F32) * D_MODEL ** -0.5,
        'ret_w_out': nrm(ks[24], (N_ODD, WC_V, D_MODEL), F32) * WC_V ** -0.5,
    }


def reference(x_prompt, x_sample, p_prompt, p_sample, state_lru_h, state_conv, cache_sb_k, cache_sb_v, state_ret,
              norm_g, ffn_w_gate, ffn_w_up, ffn_w_down, ple_w_in, ple_w_gate,
              ab_w_in, ab_w_out, lru_conv_w, lru_conv_b, lru_w_r, lru_b_r, lru_w_i, lru_b_i, lru_lambda,
              ret_w_in, ret_w_out):
    B = x_prompt.shape[0]
    dt = x_prompt.dtype
    conv0 = jnp.zeros((N_EVEN, B, CONV_W - 1, WA), dt)
    h0 = jnp.zeros((N_EVEN, B, WA), dt)
    kv0 = jnp.zeros((N_EVEN, B, 0, HB, DHB), dt)
    r0 = jnp.zeros((N_ODD, B, HC, DKC, DVC), dt)
    y_p, h_p, c_p, k_p, v_p, r_p = run_group(
        x_prompt, p_prompt, conv0, h0, kv0, kv0, r0,
        norm_g, ffn_w_gate, ffn_w_up, ffn_w_down, ple_w_in, ple_w_gate,
        ab_w_in, ab_w_out, lru_conv_w, lru_conv_b, lru_w_r, lru_b_r, lru_w_i, lru_b_i, lru_lambda,
        ret_w_in, ret_w_out)
    y_s, h_s, c_s, k_s, v_s, r_s = run_group(
        x_sample, p_sample, state_conv, state_lru_h, cache_sb_k, cache_sb_v, state_ret,
        norm_g, ffn_w_gate, ffn_w_up, ffn_w_down, ple_w_in, ple_w_gate,
        ab_w_in, ab_w_out, lru_conv_w, lru_conv_b, lru_w_r, lru_b_r, lru_w_i, lru_b_i, lru_lambda,
        ret_w_in, ret_w_out)
    return (y_p, y_s, h_p, c_p, k_p, v_p, r_p, h_s, c_s, k_s, v_s, r_s)
```

```python
import numpy as np
from contextlib import ExitStack
import concourse.bass as bass
import concourse.mybir as mybir
from concourse.bass_utils import run_bass_kernel_spmd

F32 = mybir.dt.float32
F32R = mybir.dt.float32r
AF = mybir.ActivationFunctionType
ALU = mybir.AluOpType

D = 1024
DEPTH = 4
SEQ = 2048
DSEQ = 64
NCOL = SEQ + DSEQ
DFF = 2816
NFC = 22
EPS = 1e-6
SAME_ENGINE_SYNC = True
NSLOT = 8
PE_DRAIN = True
POOL_ADD = True
ATT_SKEW = 1
NRT = 36
NZT = 10
NFT = 22
PREFETCH = 4
SEG_T = 512


def pass_specs(depth=DEPTH, en_ab=True, en_c=True):
    sp = []
    for i in range(depth):
        j = i // 2
        for w in range(2):
            if w == 1:
                if i % 2 == 0 and en_ab:
                    for q in range(4):
                        sp.append(('abv', j, q))
                    for c in range(4):
                        sp.append(('abxa', j, c))
                        sp.append(('abga', j, c))
                    for h in range(8):
                        sp.append(('abqk', j, h))
                    for c in range(8):
                        sp.append(('abo', j, c))
                if i % 2 == 1 and en_c:
                    def _A(h):
                        for cc in range(2):
                            sp.append(('cq', j, h, cc))
                        for cc in range(2):
                            sp.append(('ck', j, h, cc))
                        for q in range(4):
                            sp.append(('cv', j, h, q))
                    _A(0)
                    for h in range(4):
                        if h + 1 < 4:
                            _A(h + 1)
                        for dc in range(4):
                            sp.append(('cg', j, h, dc))
                        for c in range(8):
                            sp.append(('co', j, h, c))
            for f in range(NFC):
                sp.append(('gate', i, w, f))
                sp.append(('up', i, w, f))
            for c in range(8):
                for part in range(3):
                    sp.append(('down', i, w, c, part))
        for c in range(8):
            sp.append(('pleg', i, c))
            sp.append(('plep', i, c))
    return sp


def kpiece(W, r0, nk, c0, ncol):
    a = W[r0:r0 + nk * 128, c0:c0 + ncol].reshape(nk, 128, ncol).transpose(1, 0, 2).reshape(128, nk * ncol)
    return a


def piece_array(spec, I):
    out = np.zeros((128, 1024), np.float32)
    k = spec[0]
    if k in ('gate', 'up'):
        W = I['ffn_w_gate' if k == 'gate' else 'ffn_w_up'][spec[1], spec[2]]
        a = kpiece(W, 0, 8, spec[3] * 128, 128)
    elif k == 'down':
        W = I['ffn_w_down'][spec[1], spec[2]]
        f0 = spec[4] * 8
        nf = min(8, NFC - f0)
        a = kpiece(W, f0 * 128, nf, spec[3] * 128, 128)
    elif k == 'pleg':
        a = kpiece(I['ple_w_gate'][spec[1]], 0, 8, spec[2] * 128, 128)
    elif k == 'plep':
        a = kpiece(I['ple_w_in'][spec[1]], 0, 2, spec[2] * 128, 128)
    elif k == 'abv':
        a = kpiece(I['ab_w_in'][spec[1]], spec[2] * 256, 2, 2048, 512)
    elif k == 'ablru':
        j = spec[1]
        a = np.zeros((128, 1024), np.float32)
        for gi, nm in enumerate(('lru_w_r', 'lru_w_i')):
            for c in range(4):
                for hh in range(2):
                    a[hh * 64:(hh + 1) * 64, gi * 512 + c * 128 + hh * 64: gi * 512 + c * 128 + (hh + 1) * 64] = I[nm][j, 2 * c + hh]
    elif k == 'abxa':
        a = kpiece(I['ab_w_in'][spec[1]], 0, 8, spec[2] * 128, 128)
    elif k == 'abga':
        a = kpiece(I['ab_w_in'][spec[1]], 0, 8, 512 + spec[2] * 128, 128)
    elif k == 'abqk':
        W = I['ab_w_in'][spec[1]]
        a = np.concatenate([kpiece(W, 0, 8, 1024 + spec[2] * 64, 64), kpiece(W, 0, 8, 1536 + spec[2] * 64, 64)], axis=1)
    elif k == 'abo':
        a = kpiece(I['ab_w_out'][spec[1]], 0, 8, spec[2] * 128, 128)
    elif k == 'cq':
        a = kpiece(I['ret_w_in'][spec[1]], 0, 8, spec[2] * 256 + spec[3] * 128, 128)
    elif k == 'ck':
        a = kpiece(I['ret_w_in'][spec[1]], 0, 8, 1024 + spec[2] * 256 + spec[3] * 128, 128)
    elif k == 'cv':
        a = kpiece(I['ret_w_in'][spec[1]], spec[3] * 256, 2, 2048 + spec[2] * 512, 512)
    elif k == 'cg':
        a = kpiece(I['ret_w_in'][spec[1]], 0, 8, 4096 + spec[2] * 512 + spec[3] * 128, 128)
    elif k == 'co':
        a = kpiece(I['ret_w_out'][spec[1]], spec[2] * 512, 4, spec[3] * 128, 128)
    else:
        raise ValueError(spec)
    out[:a.shape[0], :a.shape[1]] = a
    return out


def piece_cols(spec):
    k = spec[0]
    if k == 'down':
        return min(8, NFC - spec[4] * 8) * 128
    if k == 'plep':
        return 256
    if k in ('co',):
        return 512
    return 1024


def vcol_norm(i, n, c):
    return (i * 8 + n) * 8 + c
VC_CONVW = 256
VC_CONVB = 288
VC_BR = 296
VC_BI = 304
VC_LAM = 312
NVEC = 320


def build_vec(I):
    v = np.zeros((128, NVEC), np.float32)
    ng = I['norm_g']
    for i in range(DEPTH):
        for n in range(8):
            v[:, vcol_norm(i, n, 0):vcol_norm(i, n, 0) + 8] = ng[i, n].reshape(8, 128).T
    for j in range(2):
        for tap in range(4):
            v[:, VC_CONVW + (j * 4 + tap) * 4: VC_CONVW + (j * 4 + tap) * 4 + 4] = I['lru_conv_w'][j, tap].reshape(4, 128).T
        v[:, VC_CONVB + j * 4: VC_CONVB + j * 4 + 4] = I['lru_conv_b'][j].reshape(4, 128).T
        v[:, VC_BR + j * 4: VC_BR + j * 4 + 4] = I['lru_b_r'][j].reshape(4, 128).T
        v[:, VC_BI + j * 4: VC_BI + j * 4 + 4] = I['lru_b_i'][j].reshape(4, 128).T
        v[:, VC_LAM + j * 4: VC_LAM + j * 4 + 4] = I['lru_lambda'][j].reshape(4, 128).T
    return v


def build_consts():
    C = {}
    C['ones'] = np.ones((128, 128), np.float32)
    C['ident'] = np.eye(128, dtype=np.float32)
    jj = np.arange(128)
    C['tri'] = (jj[:, None] >= jj[None, :]).astype(np.float32)
    q = np.arange(512)
    C['masks'] = np.stack([((128 * kb + jj)[:, None] < q[None, :]).astype(np.float32) for kb in range(4)])
    half = 128
    freq = (np.float32(10000.0) ** (-np.arange(half, dtype=np.float32) / np.float32(half))).astype(np.float32)
    pos = np.arange(NCOL, dtype=np.float32)
    ang = (pos[None, :] * freq[:, None]).astype(np.float32)
    cs = np.cos(ang).astype(np.float32)
    sn = np.sin(ang).astype(np.float32)
    C['rot'] = np.stack([cs, sn, cs * np.float32(1.0 / 16), sn * np.float32(1.0 / 16)]).astype(np.float32)
    log_g = np.log(np.float32(1.0) - np.float32(2.0) ** (-5.0 - np.arange(4, dtype=np.float32))).astype(np.float32)
    m = np.arange(512)
    l = np.arange(512)
    dT = np.zeros((4, 512, 512), np.float32)
    cm = m[:, None] // 64
    cl = l[None, :] // 64
    for h in range(4):
        same = np.exp(log_g[h] * np.abs(l[None, :] - m[:, None]).astype(np.float32))
        prev = np.exp(log_g[h] * (l[None, :] - m[:, None]).astype(np.float32))
        dT[h] = np.where(cm == cl, same, np.where(cm < cl, prev, 0.0))
    C['decayT'] = dT.astype(np.float32)
    xi = np.stack([np.exp(log_g[h] * (l.astype(np.float32) + 1.0)) for h in range(4)]).astype(np.float32)
    C['xi'] = np.broadcast_to(xi[:, None, :], (4, 128, 512)).copy()
    z = np.zeros((128, 20), np.float32)
    for h in range(4):
        for blk in range(4):
            z[:, h * 5 + blk] = np.exp(log_g[h] * (511.0 - (blk * 128 + jj)).astype(np.float32))
        z[:64, h * 5 + 4] = np.exp(log_g[h] * (63.0 - jj[:64]).astype(np.float32))
    C['zeta'] = z
    C['gT'] = {512: [float(np.exp(log_g[h] * np.float32(512.0))) for h in range(4)],
               64: [float(np.exp(log_g[h] * np.float32(64.0))) for h in range(4)]}
    return C


class Buf:
    __slots__ = ('w', 'r', 'wm')

    def __init__(self, multi=False):
        self.w = None
        self.r = {}
        self.wm = {} if multi else None


class Tile:
    def __init__(self, t, shape):
        self.t = t
        self.b = Buf()
        self.shape = shape

    def f(self, p=None, a=0, b=None):
        p = self.shape[0] if p is None else p
        b = self.shape[1] if b is None else b
        return self.t[0:p, a:b]

    def r(self, p=None, a=0, b=None):
        return self.f(p, a, b).bitcast(F32R)


def _cls(n):
    return 32 if n <= 32 else (64 if n <= 64 else 128)


class PEProxy:
    def __init__(self, h):
        self.h = h
        self.last = None
        self.ndrain = 0

    def matmul(self, out, lhsT, rhs, **kw):
        c = (_cls(lhsT.shape[0]), _cls(lhsT.shape[-1]))
        if PE_DRAIN and self.last is not None and c != self.last:
            self.h.drain()
            self.ndrain += 1
        self.last = c
        return self.h.matmul(out, lhsT=lhsT, rhs=rhs, **kw)

    def wait_ge(self, *a, **k):
        return self.h.wait_ge(*a, **k)


class Eng:
    def __init__(self, name, h, sem):
        self.name = name
        self.h = h
        self.sem = sem
        self.cnt = 0
        self.seen = {}


class Builder:
    def __init__(self, depth=DEPTH, en_ab=True, en_c=True, segs=None):
        self.depth = depth
        self.en_ab = en_ab
        self.en_c = en_c
        self.C = build_consts()
        self.specs = pass_specs(depth, en_ab, en_c)
        self.npp = len(self.specs)
        if segs is None:
            segs = [dict(kind='p', s0=s, T=SEG_T, col0=s) for s in range(0, SEQ, SEG_T)] + [dict(kind='s', s0=SEQ, T=DSEQ, col0=SEQ)]
        self.segs = segs
        self.nc = bass.Bass("TRN2", target_bir_lowering=False)
        self.es = ExitStack()
        self.build()

    def dram_in(self, name, shape):
        return self.nc.dram_tensor(name, list(shape), F32, kind="ExternalInput").ap()

    def dram_out(self, name, shape):
        return self.nc.dram_tensor(name, list(shape), F32, kind="ExternalOutput").ap()

    def sb(self, name, shape):
        t = self.es.enter_context(self.nc.sbuf_tensor("sb_" + name, list(shape), F32))
        return Tile(t, shape)

    def setup(self):
        nc, es = self.nc, self.es
        self.E = {}
        self.sem = {}
        for name, h in (('pe', PEProxy(nc.tensor)), ('act', nc.scalar), ('dve', nc.vector), ('pool', nc.gpsimd), ('sp', nc.sync)):
            s = es.enter_context(nc.semaphore("s_" + name))
            self.E[name] = Eng(name, h, s)
            self.sem[name] = s
        self.dring = {}
        for q, n in (('pool', 12), ('sp', 24)):
            sems = []
            for i in range(n):
                s = es.enter_context(nc.semaphore(f"d_{q}{i}"))
                self.sem[(q, i)] = s
                sems.append(s)
            self.dring[q] = dict(n=0, val=[0] * n, size=n)
        self.psb = []
        for i in range(8):
            t = es.enter_context(nc.psum_tensor(f"ps{i}", [128, 512], F32))
            self.psb.append(Tile(t, [128, 512]))
        self.ps_free = list(self.psb)
        self.rt_free = [self.sb(f"rt{i}", [128, 512]) for i in range(NRT)]
        self.ft_free = [self.sb(f"ft{i}", [128, 512]) for i in range(NFT)]
        self.zt_free = [self.sb(f"zt{i}", [128, 512]) for i in range(NZT)]
        self.slots = [self.sb(f"ws{i}", [128, 1024]) for i in range(NSLOT)]

    def mark(self, label):
        if not hasattr(self, 'marks'):
            self.marks = []
        self.marks.append((label, self.E['dve'].cnt))

    def ps_get(self):
        assert self.ps_free, "out of PSUM banks"
        return self.ps_free.pop(0)

    def ps_put(self, p):
        self.ps_free.append(p)

    def rget(self):
        assert self.rt_free, "out of R tiles"
        return self.rt_free.pop(0)

    def rput(self, *ts):
        for t in ts:
            self.rt_free.append(t)

    def zget(self):
        assert self.zt_free, "out of Z tiles"
        return self.zt_free.pop(0)

    def zput(self, *ts):
        for t in ts:
            self.zt_free.append(t)

    def fget(self):
        assert self.ft_free, "out of F tiles"
        return self.ft_free.pop(0)

    def fput(self, *ts):
        for t in ts:
            self.ft_free.append(t)

    def _waits(self, eng, r, w):
        need = {}
        for b in list(r) + list(w):
            if b.wm:
                for k, c in b.wm.items():
                    if need.get(k, 0) < c:
                        need[k] = c
        for b in r:
            if b.w is not None:
                k, c = b.w
                if need.get(k, 0) < c:
                    need[k] = c
        for b in w:
            if b.w is not None:
                k, c = b.w
                if need.get(k, 0) < c:
                    need[k] = c
            for k, c in b.r.items():
                if need.get(k, 0) < c:
                    need[k] = c
        for k, c in need.items():
            if k == eng.name and (not SAME_ENGINE_SYNC or k == 'pe'):
                continue
            if eng.seen.get(k, 0) < c:
                eng.h.wait_ge(self.sem[k], c)
                eng.seen[k] = c

    def op(self, e, fn, r=(), w=()):
        eng = self.E[e]
        r = [x.b if isinstance(x, Tile) else x for x in r]
        w = [x.b if isinstance(x, Tile) else x for x in w]
        self._waits(eng, r, w)
        ins = fn(eng.h)
        eng.cnt += 1
        ins.then_inc(eng.sem, 1)
        for b in r:
            if b.r.get(e, 0) < eng.cnt:
                b.r[e] = eng.cnt
        for b in w:
            b.w = (e, eng.cnt)
            b.r = {}
        return ins

    def dma(self, q, out_ap, in_ap, r=(), w=()):
        eng = self.E[q]
        ring = self.dring[q]
        idx = ring['n'] % ring['size']
        ring['n'] += 1
        key = (q, idx)
        prev = ring['val'][idx]
        if prev > 0 and eng.seen.get(key, 0) < prev:
            eng.h.wait_ge(self.sem[key], prev)
            eng.seen[key] = prev
        r = [x.b if isinstance(x, Tile) else x for x in r]
        w = [x.b if isinstance(x, Tile) else x for x in w]
        self._waits(eng, r, w)
        ins = eng.h.dma_start(out=out_ap, in_=in_ap)
        val = prev + 16
        ins.then_inc(self.sem[key], 16)
        ring['val'][idx] = val
        for b in r:
            b.r[key] = val
        for b in w:
            if b.wm is not None:
                b.wm[key] = val
            else:
                b.w = (key, val)
                b.r = {}

    def ws_issue(self, gidx):
        pidx = gidx % self.npp
        spec = self.specs[pidx]
        slot = self.slots[gidx % NSLOT]
        ncol = piece_cols(spec)
        self.dma('pool', slot.r(128, 0, ncol), self.wpack[pidx, :, 0:ncol], r=(), w=[slot])

    def ws_next(self, spec):
        g = self.ws_pos
        assert self.specs[g % self.npp] == spec, (self.specs[g % self.npp], spec)
        while self.ws_issued < min(g + PREFETCH, self.ws_total):
            self.ws_issue(self.ws_issued)
            self.ws_issued += 1
        self.ws_pos += 1
        return self.slots[g % NSLOT]

    def vc(self, col, p=128):
        return self.vec.t[0:p, col:col + 1]

    def linear(self, slot, ins, T, kw=128, M=128, coff=0, ps=None, start=True, stop=True, extra_r=(), per_k=False):
        if ps is None:
            ps = self.ps_get()
        n = len(ins)
        if per_k:
            for k in range(n):
                self.op('pe', lambda h: h.matmul(ps.t[0:M, 0:T], lhsT=slot.r(128, coff + k * kw, coff + k * kw + M), rhs=ins[k].r(128, 0, T),
                                                 start=(start and k == 0), stop=(stop and k == n - 1)), r=[slot, ins[k]], w=[ps])
            return ps

        def fn(h):
            last = None
            for k in range(n):
                last = h.matmul(ps.t[0:M, 0:T], lhsT=slot.r(128, coff + k * kw, coff + k * kw + M), rhs=ins[k].r(128, 0, T),
                                start=(start and k == 0), stop=(stop and k == n - 1))
            return last
        self.op('pe', fn, r=[slot] + list(ins) + list(extra_r), w=[ps])
        return ps

    def evac(self, ps, dst_ap, P, T, dst_tile, eng='act', scale=None):
        if eng == 'act':
            if scale is None:
                self.op('act', lambda h: h.activation(out=dst_ap, in_=ps.t[0:P, 0:T], func=AF.Copy), r=[ps], w=[dst_tile])
            else:
                self.op('act', lambda h: h.activation(out=dst_ap, in_=ps.t[0:P, 0:T], func=AF.Copy, scale=scale), r=[ps], w=[dst_tile])
        else:
            self.op('dve', lambda h: h.tensor_copy(out=dst_ap, in_=ps.t[0:P, 0:T]), r=[ps], w=[dst_tile])

    def rstd_from(self, tiles, T, nfeat, P=128):
        ps = self.ps_get()
        n = len(tiles)
        for c in range(n):
            sq = self.rget()
            self.op('act', lambda h: h.activation(out=sq.r(P, 0, T), in_=tiles[c].f(P, 0, T), func=AF.Square), r=[tiles[c]], w=[sq])
            self.op('pe', lambda h: h.matmul(ps.t[0:128, 0:T], lhsT=self.ones.r(P, 0, 128), rhs=sq.r(P, 0, T), start=(c == 0), stop=(c == n - 1)),
                    r=[sq, self.ones], w=[ps])
            self.rput(sq)
        rstd = self.fget()
        self.op('act', lambda h: h.activation(out=rstd.f(128, 0, T), in_=ps.t[0:128, 0:T], func=AF.Ln, scale=1.0 / nfeat, bias=self.epsc.t[0:128, 0:1]),
                r=[ps, self.epsc], w=[rstd])
        self.ps_put(ps)
        self.op('act', lambda h: h.activation(out=rstd.f(128, 0, T), in_=rstd.f(128, 0, T), func=AF.Exp, scale=-0.5), r=[rstd], w=[rstd])
        return rstd

    def rmsnorm(self, gcol, T):
        rstd = self.rstd_from(self.x, T, D)
        hn = [self.rget() for _ in range(8)]
        for c in range(8):
            self.op('dve', lambda h: h.scalar_tensor_tensor(out=hn[c].r(128, 0, T), in0=self.x[c].f(128, 0, T), scalar=self.vc(gcol + c),
                                                           in1=rstd.f(128, 0, T), op0=ALU.mult, op1=ALU.mult),
                    r=[self.x[c], rstd, self.vec], w=[hn[c]])
        self.fput(rstd)
        return hn

    def postnorm_add(self, y, gcol, half, T):
        rstd = self.rstd_from(y, T, D)
        vt = self.vech if half else self.vec
        for c in range(8):
            self.op('dve', lambda h: h.scalar_tensor_tensor(out=y[c].f(128, 0, T), in0=y[c].f(128, 0, T), scalar=vt.t[0:128, gcol + c:gcol + c + 1],
                                                           in1=rstd.f(128, 0, T), op0=ALU.mult, op1=ALU.mult),
                    r=[y[c], rstd, vt], w=[y[c]])
            self.op('pool' if (POOL_ADD and c % 2 == 1) else 'dve', lambda h: h.tensor_tensor(out=self.x[c].f(128, 0, T), in0=self.x[c].f(128, 0, T), in1=y[c].f(128, 0, T), op=ALU.add),
                    r=[self.x[c], y[c]], w=[self.x[c]])
        self.fput(rstd)

    def ffn(self, i, w, T):
        hn = self.rmsnorm(vcol_norm(i, 0 if w == 0 else 4, 0), T)
        hh = []
        for f in range(NFC):
            wg = self.ws_next(('gate', i, w, f))
            wu = self.ws_next(('up', i, w, f))
            psg = self.linear(wg, hn, T, per_k=(f == 0))
            psu = self.linear(wu, hn, T)
            s = self.fget()
            self.op('act', lambda h: h.activation(out=s.f(128, 0, T), in_=psg.t[0:128, 0:T], func=AF.Silu), r=[psg], w=[s])
            self.ps_put(psg)
            ht = self.rget()
            self.op('dve', lambda h: h.tensor_tensor(out=ht.r(128, 0, T), in0=psu.t[0:128, 0:T], in1=s.f(128, 0, T), op=ALU.mult), r=[psu, s], w=[ht])
            self.ps_put(psu)
            self.fput(s)
            hh.append(ht)
        self.rput(*hn)
        y = [self.fget() for _ in range(8)]
        for c in range(8):
            psy = self.ps_get()
            for part in range(3):
                wd = self.ws_next(('down', i, w, c, part))
                f0 = part * 8
                f1 = min(NFC, f0 + 8)
                self.linear(wd, hh[f0:f1], T, ps=psy, start=(part == 0), stop=(part == 2))
            self.evac(psy, y[c].f(128, 0, T), 128, T, y[c])
            self.ps_put(psy)
        self.rput(*hh)
        self.postnorm_add(y, vcol_norm(i, 1 if w == 0 else 5, 0), True, T)
        self.fput(*y)

    def ple(self, i, seg):
        T, col0 = seg['T'], seg['col0']
        hn = self.rmsnorm(vcol_norm(i, 6, 0), T)
        pt = [self.rget() for _ in range(2)]
        for k in range(2):
            self.dma('pool', pt[k].r(128, 0, T), self.pT[i, k, :, col0:col0 + T], w=[pt[k]])
        z = [self.fget() for _ in range(8)]
        for c in range(8):
            wg = self.ws_next(('pleg', i, c))
            wp = self.ws_next(('plep', i, c))
            psg = self.linear(wg, hn, T)
            psp = self.linear(wp, pt, T)
            sg = self.fget()
            self.op('act', lambda h: h.activation(out=sg.f(128, 0, T), in_=psg.t[0:128, 0:T], func=AF.Sigmoid), r=[psg], w=[sg])
            self.ps_put(psg)
            self.op('dve', lambda h: h.tensor_tensor(out=z[c].f(128, 0, T), in0=psp.t[0:128, 0:T], in1=sg.f(128, 0, T), op=ALU.mult), r=[psp, sg], w=[z[c]])
            self.ps_put(psp)
            self.fput(sg)
        self.rput(*hn)
        self.rput(*pt)
        self.postnorm_add(z, vcol_norm(i, 7, 0), False, T)
        self.fput(*z)

    def build(self):
        nc = self.nc
        self.wpack = self.dram_in("wpack", [self.npp, 128, 1024])
        self.xT = self.dram_in("xT", [8, 128, NCOL])
        self.pT = self.dram_in("pT", [DEPTH, 2, 128, NCOL])
        self.vec_d = self.dram_in("vec", [128, NVEC])
        self.lrupack = self.dram_in("lrupack", [2, 128, 1024])
        self.c_ones = self.dram_in("c_ones", [128, 128])
        self.c_ident = self.dram_in("c_ident", [128, 128])
        self.c_tri = self.dram_in("c_tri", [128, 128])
        self.c_masks = self.dram_in("c_masks", [4, 128, 512])
        self.c_rot = self.dram_in("c_rot", [4, 128, NCOL])
        self.c_decayT = self.dram_in("c_decayT", [4, 512, 512])
        self.c_xi = self.dram_in("c_xi", [4, 128, 512])
        self.c_zeta = self.dram_in("c_zeta", [128, 20])
        self.s_lruh = self.dram_in("s_lruh", [2, 128, 4])
        self.s_conv = self.dram_in("s_conv", [2, 128, 12])
        self.s_kT = self.dram_in("s_kT", [2, 8, 64, SEQ])
        self.s_v = self.dram_in("s_v", [2, SEQ, 512])
        self.s_R = self.dram_in("s_R", [2, 4, 2, 128, 512])
        self.o_y = self.dram_out("o_y", [8, 128, NCOL])
        self.o_lruh = self.dram_out("o_lruh", [2, 2, 128, 4])
        self.o_conv = self.dram_out("o_conv", [2, 2, 128, 12])
        self.o_kT = self.dram_out("o_kT", [2, 8, 64, NCOL])
        self.o_v = self.dram_out("o_v", [2, NCOL, 512])
        self.o_R = self.dram_out("o_R", [2, 2, 4, 2, 128, 512])
        self.setup()
        self.x = [self.sb(f"x{c}", [128, 512]) for c in range(8)]
        self.vec = self.sb("vec", [128, NVEC])
        self.vech = self.sb("vech", [128, NVEC])
        self.ones = self.sb("ones", [128, 128])
        self.ident = self.sb("ident", [128, 128])
        self.tri = self.sb("tri", [128, 128])
        self.masks = [self.sb(f"mask{k}", [128, 512]) for k in range(4)]
        self.zeta = self.sb("zeta", [128, 20])
        self.epsc = self.sb("epsc", [128, 1])
        self.lruh = [self.sb(f"lruh{j}", [128, 4]) for j in range(2)]
        self.carry = [self.sb(f"carry{j}", [128, 12]) for j in range(2)]
        self.m8c = self.sb("m8c", [128, 8])
        self.wl = self.sb("wl", [128, 1024])
        self.hb_kv = [Buf(True) for _ in range(2)]
        self.hb_kv_seen = [{}, {}]
        self.hb_R = [[Buf(True) for _ in range(4)] for _ in range(2)]
        self.ws_pos = 0
        self.ws_issued = 0
        self.ws_total = self.npp * len(self.segs)

        self.dma('sp', self.vec.f(), self.vec_d, w=[self.vec])
        self.dma('sp', self.zeta.f(), self.c_zeta, w=[self.zeta])
        self.dma('pool', self.ones.r(), self.c_ones, w=[self.ones])
        self.dma('pool', self.ident.r(), self.c_ident, w=[self.ident])
        self.dma('pool', self.tri.r(), self.c_tri, w=[self.tri])
        for k in range(4):
            self.dma('sp', self.masks[k].f(), self.c_masks[k], w=[self.masks[k]])
        self.op('dve', lambda h: h.memset(self.epsc.f(), EPS), w=[self.epsc])
        for zt in self.zt_free:
            self.op('dve', lambda h: h.tensor_scalar(out=zt.r(128, 0, 512), in0=self.masks[0].f(128, 0, 512), scalar1=0.0, scalar2=None, op0=ALU.mult), r=[self.masks[0]], w=[zt])
        self.op('act', lambda h: h.activation(out=self.vech.f(), in_=self.vec.f(), func=AF.Copy, scale=0.5), r=[self.vec], w=[self.vech])
        self.op('act', lambda h: h.activation(out=self.m8c.f(), in_=self.vec.t[0:128, VC_LAM:VC_LAM + 8], func=AF.Exp, scale=-1.0), r=[self.vec], w=[self.m8c])
        self.op('act', lambda h: h.activation(out=self.m8c.f(), in_=self.m8c.f(), func=AF.Ln, bias=1.0), r=[self.m8c], w=[self.m8c])
        self.op('act', lambda h: h.activation(out=self.m8c.f(), in_=self.m8c.f(), func=AF.Copy, scale=-8.0), r=[self.m8c], w=[self.m8c])
        for j in range(2):
            self.op('dve', lambda h: h.memset(self.lruh[j].f(), 0.0), w=[self.lruh[j]])
            self.op('dve', lambda h: h.memset(self.carry[j].f(), 0.0), w=[self.carry[j]])

        nprompt = sum(1 for s in self.segs if s['kind'] == 'p')
        for si, seg in enumerate(self.segs):
            T, col0 = seg['T'], seg['col0']
            seg['first'] = (seg['kind'] == 'p' and seg['s0'] == 0)
            seg['grp'] = 0 if seg['kind'] == 'p' else 1
            seg['last'] = (seg['kind'] == 's') or (si == nprompt - 1)
            if seg['kind'] == 's':
                for j in range(2):
                    self.dma('sp', self.lruh[j].f(), self.s_lruh[j], w=[self.lruh[j]])
                    self.dma('sp', self.carry[j].f(), self.s_conv[j], w=[self.carry[j]])
            for c in range(8):
                self.dma('sp', self.x[c].f(128, 0, T), self.xT[c, :, col0:col0 + T], w=[self.x[c]])
            for i in range(self.depth):
                self.mark(f"s{si} L{i} ffn0")
                self.ffn(i, 0, T)
                if i % 2 == 0 and self.en_ab:
                    self.mark(f"s{si} L{i} ab")
                    self.mixer_ab(i // 2, i, seg)
                if i % 2 == 1 and self.en_c:
                    self.mark(f"s{si} L{i} c")
                    self.mixer_c(i // 2, i, seg)
                self.mark(f"s{si} L{i} ffn1")
                self.ffn(i, 1, T)
                self.mark(f"s{si} L{i} ple")
                self.ple(i, seg)
            self.mark(f"s{si} end")
            for c in range(8):
                self.dma('sp', self.o_y[c, :, col0:col0 + T], self.x[c].f(128, 0, T), r=[self.x[c]])
            if seg['last']:
                for j in range(2):
                    self.dma('sp', self.o_lruh[j, seg['grp']], self.lruh[j].f(), r=[self.lruh[j]])
                    self.dma('sp', self.o_conv[j, seg['grp']], self.carry[j].f(), r=[self.carry[j]])
        sp = self.E['sp']
        for q in ('pool', 'sp'):
            ring = self.dring[q]
            for idx in range(ring['size']):
                if ring['val'][idx] > 0:
                    sp.h.wait_ge(self.sem[(q, idx)], ring['val'][idx])
        self.es.close()

    def mixer_ab(self, j, i, seg):
        T, col0, s0 = seg['T'], seg['col0'], seg['s0']
        grp = seg['grp']
        bs = min(128, T)
        nb = (T + 127) // 128
        self.hb_kv_seen[j] = dict(self.hb_kv[j].wm)
        hn = self.rmsnorm(vcol_norm(i, 2, 0), T)
        wv = [self.ws_next(('abv', j, q)) for q in range(4)]
        Vn = []
        for blk in range(nb):
            psv = self.ps_get()

            def fn(h):
                last = None
                for k in range(8):
                    last = h.matmul(psv.t[0:bs, 0:512], lhsT=hn[k].r(128, blk * 128, blk * 128 + bs), rhs=wv[k // 2].r(128, (k % 2) * 512, (k % 2) * 512 + 512),
                                    start=(k == 0), stop=(k == 7))
                return last
            self.op('pe', fn, r=hn + wv, w=[psv])
            vt = self.rget()
            self.evac(psv, vt.r(bs, 0, 512), bs, 512, vt)
            self.ps_put(psv)
            self.dma('sp', self.o_v[j, col0 + blk * 128: col0 + blk * 128 + bs, :], vt.f(bs, 0, 512), r=[vt], w=[self.hb_kv[j]])
            Vn.append(vt)
        wl = self.wl
        self.dma('pool', wl.r(), self.lrupack[j], w=[wl])
        def lru_gen(c):
                wxa = self.ws_next(('abxa', j, c))
                wga = self.ws_next(('abga', j, c))
                psx = self.linear(wxa, hn, T)
                psg = self.linear(wga, hn, T)
                xa = self.fget()
                self.evac(psx, xa.f(128, 0, T), 128, T, xa)
                self.ps_put(psx)
                acc = self.fget()
                xc = self.rget()
                cw = lambda tap: self.vc(VC_CONVW + (j * 4 + tap) * 4 + c)
                car = self.carry[j]
                self.op('dve', lambda h: h.tensor_scalar(out=acc.f(128, 0, T), in0=xa.f(128, 0, T), scalar1=cw(3), scalar2=self.vc(VC_CONVB + j * 4 + c),
                                                         op0=ALU.mult, op1=ALU.add), r=[xa, self.vec], w=[acc])
                for s in (1, 2, 3):
                    last = (s == 3)
                    dst = xc if last else acc
                    o_big = dst.r(128, s, T) if last else dst.f(128, s, T)
                    o_small = dst.r(128, 0, s) if last else dst.f(128, 0, s)
                    self.op('dve', lambda h: h.scalar_tensor_tensor(out=o_big, in0=xa.f(128, 0, T - s), scalar=cw(3 - s), in1=acc.f(128, s, T),
                                                                   op0=ALU.mult, op1=ALU.add), r=[xa, acc, self.vec], w=[dst])
                    self.op('dve', lambda h: h.scalar_tensor_tensor(out=o_small, in0=car.t[0:128, c * 3 + 3 - s: c * 3 + 3], scalar=cw(3 - s), in1=acc.f(128, 0, s),
                                                                   op0=ALU.mult, op1=ALU.add), r=[car, acc, self.vec], w=[dst])
                self.op('dve', lambda h: h.tensor_copy(out=car.t[0:128, c * 3: c * 3 + 3], in_=xa.f(128, T - 3, T)), r=[xa], w=[car])
                self.fput(xa, acc)
                yield
                psr = self.ps_get()
                psi = self.ps_get()
                self.op('pe', lambda h: h.matmul(psr.t[0:128, 0:T], lhsT=wl.r(128, c * 128, c * 128 + 128), rhs=xc.r(128, 0, T), start=True, stop=True), r=[wl, xc], w=[psr])
                self.op('pe', lambda h: h.matmul(psi.t[0:128, 0:T], lhsT=wl.r(128, 512 + c * 128, 512 + c * 128 + 128), rhs=xc.r(128, 0, T), start=True, stop=True), r=[wl, xc], w=[psi])
                rg = self.fget()
                ig = self.fget()
                self.op('act', lambda h: h.activation(out=rg.f(128, 0, T), in_=psr.t[0:128, 0:T], func=AF.Sigmoid, bias=self.vc(VC_BR + j * 4 + c)), r=[psr, self.vec], w=[rg])
                self.op('act', lambda h: h.activation(out=ig.f(128, 0, T), in_=psi.t[0:128, 0:T], func=AF.Sigmoid, bias=self.vc(VC_BI + j * 4 + c)), r=[psi, self.vec], w=[ig])
                self.ps_put(psr)
                self.ps_put(psi)
                yield
                a = self.fget()
                self.op('act', lambda h: h.activation(out=a.f(128, 0, T), in_=rg.f(128, 0, T), func=AF.Exp, scale=self.m8c.t[0:128, j * 4 + c: j * 4 + c + 1]), r=[rg, self.m8c], w=[a])
                self.op('act', lambda h: h.activation(out=rg.f(128, 0, T), in_=a.f(128, 0, T), func=AF.Square), r=[a], w=[rg])
                self.op('act', lambda h: h.activation(out=rg.f(128, 0, T), in_=rg.f(128, 0, T), func=AF.Sqrt, scale=-1.0, bias=1.0), r=[rg], w=[rg])
                self.op('dve', lambda h: h.tensor_tensor(out=ig.f(128, 0, T), in0=ig.f(128, 0, T), in1=rg.f(128, 0, T), op=ALU.mult), r=[ig, rg], w=[ig])
                self.op('dve', lambda h: h.tensor_tensor(out=ig.f(128, 0, T), in0=ig.f(128, 0, T), in1=xc.f(128, 0, T), op=ALU.mult), r=[ig, xc], w=[ig])
                yield
                hs = rg
                lh = self.lruh[j]
                self.op('dve', lambda h: h.tensor_tensor_scan(out=hs.f(128, 0, T), data0=a.f(128, 0, T), data1=ig.f(128, 0, T), initial=lh.t[0:128, c:c + 1],
                                                             op0=ALU.mult, op1=ALU.add), r=[a, ig, lh], w=[hs])
                self.op('dve', lambda h: h.tensor_copy(out=lh.t[0:128, c:c + 1], in_=hs.f(128, T - 1, T)), r=[hs], w=[lh])
                self.rput(xc)
                yield
                xg = a
                x2 = ig
                self.op('act', lambda h: h.activation(out=xg.f(128, 0, T), in_=psg.t[0:128, 0:T], func=AF.Copy), r=[psg], w=[xg])
                self.op('act', lambda h: h.activation(out=x2.f(128, 0, T), in_=psg.t[0:128, 0:T], func=AF.Square), r=[psg], w=[x2])
                self.ps_put(psg)
                self.op('dve', lambda h: h.tensor_scalar(out=x2.f(128, 0, T), in0=x2.f(128, 0, T), scalar1=0.044715, scalar2=1.0, op0=ALU.mult, op1=ALU.add), r=[x2], w=[x2])
                self.op('dve', lambda h: h.tensor_tensor(out=x2.f(128, 0, T), in0=x2.f(128, 0, T), in1=xg.f(128, 0, T), op=ALU.mult), r=[x2, xg], w=[x2])
                self.op('act', lambda h: h.activation(out=x2.f(128, 0, T), in_=x2.f(128, 0, T), func=AF.Sigmoid, scale=1.5957691216057308), r=[x2], w=[x2])
                self.op('dve', lambda h: h.tensor_tensor(out=x2.f(128, 0, T), in0=x2.f(128, 0, T), in1=xg.f(128, 0, T), op=ALU.mult), r=[x2, xg], w=[x2])
                ca = self.rget()
                self.op('dve', lambda h: h.tensor_tensor(out=ca.r(128, 0, T), in0=x2.f(128, 0, T), in1=hs.f(128, 0, T), op=ALU.mult), r=[x2, hs], w=[ca])
                cat_a[c] = ca
                self.fput(rg, ig, a)

        cat_a = [None] * 4
        for c0_ in (0,):
            alive = [lru_gen(c_) for c_ in range(4)]
            while alive:
                for g_ in list(alive):
                    try:
                        next(g_)
                    except StopIteration:
                        alive.remove(g_)
        if seg['kind'] == 'p':
            npast = s0 // 128
            pastK = lambda hh, a0, a1: self.o_kT[j, hh, :, a0:a1]
            pastV = lambda a0, a1, hh: self.o_v[j, a0:a1, (hh // 2) * 128:(hh // 2) * 128 + 128]
        else:
            npast = SEQ // 128
            pastK = lambda hh, a0, a1: self.s_kT[j, hh, :, a0:a1]
            pastV = lambda a0, a1, hh: self.s_v[j, a0:a1, (hh // 2) * 128:(hh // 2) * 128 + 128]
        ngrp = (npast + 3) // 4
        hb_past = Buf(True)
        hb_past.wm = dict(self.hb_kv_seen[j])
        KQ = 128 if bs == 128 else 64
        ob = [self.rget() for _ in range(4)]
        H = {}
        items = []
        for hh in range(8):
            blocks = [('n', b_) for b_ in range(nb - 1, -1, -1)]
            for g in range(ngrp - 1, -1, -1):
                for b_ in range(min(4, npast - g * 4) - 1, -1, -1):
                    blocks.append(('p', g, b_))
            for bi, blk in enumerate(blocks):
                items.append(dict(hh=hh, blk=blk, first=(bi == 0), last=(bi == len(blocks) - 1),
                                  pre_next=(bi == min(3, len(blocks) - 1) and hh < 7)))

        def load_group(hh, g):
            nbg = min(4, npast - g * 4)
            pk = self.zget()
            pv = self.rget()
            self.dma('pool', pk.r(64, 0, nbg * 128), pastK(hh, g * 512, g * 512 + nbg * 128), r=[hb_past], w=[pk])
            for b2 in range(nbg):
                self.dma('pool', pv.r(128, b2 * 128, b2 * 128 + 128), pastV(g * 512 + b2 * 128, g * 512 + b2 * 128 + 128, hh), r=[hb_past], w=[pv])
            H[hh]['grp'][g] = (pk, pv)

        def head_pre(hh):
            wqk = self.ws_next(('abqk', j, hh))
            psq = self.linear(wqk, hn, T, kw=64, M=64, coff=0)
            psk = self.linear(wqk, hn, T, kw=64, M=64, coff=512)
            qT = self.zget()
            kT = self.zget()
            nkT = self.zget()
            self.evac(psq, qT.r(64, 0, T), 64, T, qT)
            self.evac(psk, kT.r(64, 0, T), 64, T, kT)
            self.op('act', lambda h: h.activation(out=nkT.r(64, 0, T), in_=psq.t[0:64, 0:T], func=AF.Copy, scale=-0.125), r=[psq], w=[nkT])
            self.ps_put(psq)
            self.ps_put(psk)
            self.dma('sp', self.o_kT[j, hh, :, col0:col0 + T], kT.f(64, 0, T), r=[kT], w=[self.hb_kv[j]])
            H[hh] = dict(qT=qT, kT=kT, nkT=nkT, A=None, pso=None, grp={})
            if npast > 0:
                load_group(hh, ngrp - 1)

        def head_post(hh):
            c_ = H.pop(hh)
            obt = ob[hh // 2]
            r0 = (hh % 2) * 64
            pso_ = c_['pso']
            self.op('act', lambda h: h.activation(out=obt.t[r0:r0 + 64, 0:T].bitcast(F32R), in_=pso_.t[r0:r0 + 64, 0:T], func=AF.Copy), r=[pso_], w=[obt])
            self.ps_put(c_['pso'])
            self.zput(c_['qT'], c_['kT'], c_['nkT'])

        def S1(it):
            hh, blk = it['hh'], it['blk']
            if it['first'] and hh == 0:
                head_pre(hh)
            c_ = H[hh]
            qT = c_['qT']
            if blk[0] == 'n':
                kb = blk[1]
                P_ = bs
                it['k'] = (c_['kT'], c_['kT'].r(KQ, kb * 128, kb * 128 + bs))
                it['v'] = (Vn[kb], Vn[kb].r(bs, (hh // 2) * 128, (hh // 2) * 128 + 128))
                it['mask'] = self.masks[kb]
            else:
                g, b_ = blk[1], blk[2]
                P_ = 128
                nbg = min(4, npast - g * 4)
                if b_ == nbg - 1 and g >= 1:
                    load_group(hh, g - 1)
                pk, pv = c_['grp'][g]
                it['k'] = (pk, pk.r(128, b_ * 128, b_ * 128 + 128))
                it['v'] = (pv, pv.r(128, b_ * 128, b_ * 128 + 128))
                it['mask'] = None
            it['P'] = P_
            it['KQ'] = KQ if blk[0] == 'n' else 128
            KK = it['KQ']
            mask = it['mask']
            psz = self.ps_get()
            self.op('pe', lambda h: h.matmul(psz.t[0:P_, 0:T], lhsT=it['k'][1], rhs=qT.r(KK, 0, T), start=True, stop=True), r=[it['k'][0], qT], w=[psz])
            e = self.fget()
            spt = self.rget()
            self.op('act', lambda h: h.activation(out=e.f(P_, 0, T), in_=psz.t[0:P_, 0:T], func=AF.Exp, scale=0.125), r=[psz], w=[e])
            self.ps_put(psz)
            if mask is None:
                self.op('act', lambda h: h.activation(out=spt.r(P_, 0, T), in_=e.f(P_, 0, T), func=AF.Ln, bias=1.0), r=[e], w=[spt])
            else:
                self.op('act', lambda h: h.activation(out=e.f(P_, 0, T), in_=e.f(P_, 0, T), func=AF.Ln, bias=1.0), r=[e], w=[e])
                self.op('dve', lambda h: h.tensor_tensor(out=spt.r(P_, 0, T), in0=e.f(P_, 0, T), in1=mask.f(P_, 0, T), op=ALU.mult), r=[e, mask], w=[spt])
            it['e'] = e
            it['spt'] = spt
            if it['pre_next']:
                head_pre(hh + 1)

        def S1b(it):
            hh = it['hh']
            c_ = H[hh]
            P_ = it['P']
            spt = it['spt']
            it['A'] = c_['A']
            if not it['last']:
                pst = self.ps_get()
                self.op('pe', lambda h: h.matmul(pst.t[0:128, 0:T], lhsT=self.ones.r(P_, 0, 128), rhs=spt.r(P_, 0, T), start=True, stop=True), r=[self.ones, spt], w=[pst])
                An = self.rget()
                Ap = it['A']
                if Ap is None:
                    self.op('dve', lambda h: h.tensor_copy(out=An.r(128, 0, T), in_=pst.t[0:128, 0:T]), r=[pst], w=[An])
                else:
                    self.op('dve', lambda h: h.tensor_tensor(out=An.r(128, 0, T), in0=pst.t[0:128, 0:T], in1=Ap.f(128, 0, T), op=ALU.add), r=[pst, Ap], w=[An])
                self.ps_put(pst)
                c_['A'] = An

        def S2(it):
            hh = it['hh']
            c_ = H[hh]
            P_ = it['P']
            spt, e, mask = it['spt'], it['e'], it['mask']
            acc, qT = it['A'], c_['qT']
            psc = self.ps_get()

            def fn(h):
                h.matmul(psc.t[0:P_, 0:T], lhsT=self.tri.r(P_, 0, P_), rhs=spt.r(P_, 0, T), start=True, stop=False)
                if not it['first']:
                    h.matmul(psc.t[0:P_, 0:T], lhsT=self.ident.r(P_, 0, P_), rhs=acc.r(P_, 0, T), start=False, stop=False)
                return h.matmul(psc.t[0:P_, 0:T], lhsT=it['k'][1], rhs=c_['nkT'].r(it['KQ'], 0, T), start=False, stop=True)
            self.op('pe', fn, r=[self.tri, spt, self.ident, it['k'][0], c_['nkT']] + ([] if it['first'] else [acc]), w=[psc])
            Pt = self.rget()
            if mask is None:
                self.op('act', lambda h: h.activation(out=Pt.r(P_, 0, T), in_=psc.t[0:P_, 0:T], func=AF.Exp, scale=-1.0), r=[psc], w=[Pt])
            else:
                self.op('act', lambda h: h.activation(out=e.f(P_, 0, T), in_=psc.t[0:P_, 0:T], func=AF.Exp, scale=-1.0), r=[psc], w=[e])
                self.op('dve', lambda h: h.tensor_tensor(out=Pt.r(P_, 0, T), in0=e.f(P_, 0, T), in1=mask.f(P_, 0, T), op=ALU.mult), r=[e, mask], w=[Pt])
            self.ps_put(psc)
            if acc is not None:
                self.rput(acc)
            self.fput(e)
            self.rput(spt)
            it['Pt'] = Pt

        def S3(it):
            hh = it['hh']
            c_ = H[hh]
            P_ = it['P']
            Pt = it['Pt']
            if c_['pso'] is None:
                c_['pso'] = self.ps_get()
            self.op('pe', lambda h: h.matmul(c_['pso'].t[0:128, 0:T], lhsT=it['v'][1], rhs=Pt.r(P_, 0, T), start=it['first'], stop=it['last']), r=[it['v'][0], Pt], w=[c_['pso']])
            self.rput(Pt)
            blk = it['blk']
            if blk[0] == 'p' and blk[2] == 0:
                pk_, pv_ = c_['grp'].pop(blk[1])
                self.zput(pk_)
                self.rput(pv_)
            if it['last']:
                head_post(hh)

        stages = [S1, S1b, S2, S3]
        nit = len(items)
        self.mark(f"ab{j} attn")
        for step in range(nit + (len(stages) - 1) * ATT_SKEW):
            for si_, st in enumerate(stages):
                idx = step - si_ * ATT_SKEW
                if 0 <= idx < nit:
                    st(items[idx])
        self.mark(f"ab{j} attn_end")
        self.rput(*hn)
        self.rput(*Vn)
        y = [self.fget() for _ in range(8)]
        for c in range(8):
            wo = self.ws_next(('abo', j, c))
            psy = self.linear(wo, cat_a + ob, T)
            self.evac(psy, y[c].f(128, 0, T), 128, T, y[c])
            self.ps_put(psy)
        self.rput(*cat_a)
        self.rput(*ob)
        self.postnorm_add(y, vcol_norm(i, 3, 0), False, T)
        self.fput(*y)

    def mixer_c(self, j, i, seg):
        T, col0, s0 = seg['T'], seg['col0'], seg['s0']
        grp = seg['grp']
        bs = min(128, T)
        nb = (T + 127) // 128
        has_state = not seg['first']
        gT = self.C['gT'][T]
        hn = self.rmsnorm(vcol_norm(i, 2, 0), T)
        rot = [self.fget() for _ in range(4)]
        for t_ in range(4):
            self.dma('sp', rot[t_].f(128, 0, T), self.c_rot[t_, :, col0:col0 + T], w=[rot[t_]])
        y = [self.fget() for _ in range(8)]
        def head_gen(hh):
                qr = []
                kr = []
                for which in range(2):
                    for cc in range(2):
                        wq = self.ws_next(('cq' if which == 0 else 'ck', j, hh, cc))
                        ps = self.linear(wq, hn, T)
                        raw = self.fget()
                        self.evac(ps, raw.f(128, 0, T), 128, T, raw)
                        self.ps_put(ps)
                        (qr if which == 0 else kr).append(raw)
                outs = []
                for which, (raws, cs_, sn_) in enumerate(((qr, rot[0], rot[1]), (kr, rot[2], rot[3]))):
                    x1, x2 = raws
                    t1 = self.fget()
                    t2 = self.fget()
                    o1 = self.rget()
                    o2 = self.rget()
                    self.op('dve', lambda h: h.tensor_tensor(out=t1.f(128, 0, T), in0=x1.f(128, 0, T), in1=cs_.f(128, 0, T), op=ALU.mult), r=[x1, cs_], w=[t1])
                    self.op('dve', lambda h: h.tensor_tensor(out=t2.f(128, 0, T), in0=x2.f(128, 0, T), in1=sn_.f(128, 0, T), op=ALU.mult), r=[x2, sn_], w=[t2])
                    self.op('dve', lambda h: h.tensor_tensor(out=o1.r(128, 0, T), in0=t1.f(128, 0, T), in1=t2.f(128, 0, T), op=ALU.subtract), r=[t1, t2], w=[o1])
                    self.op('dve', lambda h: h.tensor_tensor(out=t1.f(128, 0, T), in0=x2.f(128, 0, T), in1=cs_.f(128, 0, T), op=ALU.mult), r=[x2, cs_], w=[t1])
                    self.op('dve', lambda h: h.tensor_tensor(out=t2.f(128, 0, T), in0=x1.f(128, 0, T), in1=sn_.f(128, 0, T), op=ALU.mult), r=[x1, sn_], w=[t2])
                    self.op('dve', lambda h: h.tensor_tensor(out=o2.r(128, 0, T), in0=t1.f(128, 0, T), in1=t2.f(128, 0, T), op=ALU.add), r=[t1, t2], w=[o2])
                    self.fput(t1, t2, x1, x2)
                    outs.append([o1, o2])
                qrr, krr = outs
                Rt = [self.rget() for _ in range(2)]
                if has_state:
                    for cc in range(2):
                        src = self.o_R[j, 0, hh, cc] if seg['kind'] == 'p' else self.s_R[j, hh, cc]
                        self.dma('pool', Rt[cc].r(128, 0, 512), src, r=[self.hb_R[j][hh]], w=[Rt[cc]])
                    xit = self.fget()
                    self.dma('sp', xit.f(128, 0, T), self.c_xi[hh, :, 0:T], w=[xit])
                    qx = [self.rget() for _ in range(2)]
                    for cc in range(2):
                        self.op('dve', lambda h: h.tensor_tensor(out=qx[cc].r(128, 0, T), in0=qrr[cc].f(128, 0, T), in1=xit.f(128, 0, T), op=ALU.mult), r=[qrr[cc], xit], w=[qx[cc]])
                    self.fput(xit)
                wv = [self.ws_next(('cv', j, hh, q)) for q in range(4)]
                V = []
                for blk in range(nb):
                    psv = self.ps_get()

                    def fn(h):
                        last = None
                        for k in range(8):
                            last = h.matmul(psv.t[0:bs, 0:512], lhsT=hn[k].r(128, blk * 128, blk * 128 + bs), rhs=wv[k // 2].r(128, (k % 2) * 512, (k % 2) * 512 + 512),
                                            start=(k == 0), stop=(k == 7))
                        return last
                    self.op('pe', fn, r=hn + wv, w=[psv])
                    vt = self.rget()
                    self.evac(psv, vt.r(bs, 0, 512), bs, 512, vt)
                    self.ps_put(psv)
                    V.append(vt)
                yield 'A'
                kz = []
                for blk in range(nb):
                    kzt = self.rget()
                    zc = hh * 5 + (blk if T == 512 else 4)
                    for cc in range(2):
                        pst = self.ps_get()
                        self.op('pe', lambda h: h.matmul(pst.t[0:bs, 0:128], lhsT=krr[cc].r(128, blk * 128, blk * 128 + bs), rhs=self.ident.r(128, 0, 128), start=True, stop=True),
                                r=[krr[cc], self.ident], w=[pst])
                        self.op('dve', lambda h: h.tensor_scalar(out=kzt.r(bs, cc * 128, cc * 128 + 128), in0=pst.t[0:bs, 0:128], scalar1=self.zeta.t[0:bs, zc:zc + 1], scalar2=None,
                                                                 op0=ALU.mult), r=[pst, self.zeta], w=[kzt])
                        self.ps_put(pst)
                    kz.append(kzt)
                pso = [self.ps_get() for _ in range(4)]
                for mb in range(nb):
                    pss = self.ps_get()

                    def fn(h):
                        last = None
                        for cc in range(2):
                            last = h.matmul(pss.t[0:bs, 0:T], lhsT=krr[cc].r(128, mb * 128, mb * 128 + bs), rhs=qrr[cc].r(128, 0, T), start=(cc == 0), stop=(cc == 1))
                        return last
                    self.op('pe', fn, r=krr + qrr, w=[pss])
                    Dt = self.fget()
                    self.dma('sp', Dt.f(bs, 0, T), self.c_decayT[hh, mb * 128: mb * 128 + bs, 0:T], w=[Dt])
                    Pt = self.rget()
                    self.op('dve', lambda h: h.tensor_tensor(out=Pt.r(bs, 0, T), in0=pss.t[0:bs, 0:T], in1=Dt.f(bs, 0, T), op=ALU.mult), r=[pss, Dt], w=[Pt])
                    self.ps_put(pss)
                    self.fput(Dt)
                    for dc in range(4):
                        self.op('pe', lambda h: h.matmul(pso[dc].t[0:128, 0:T], lhsT=V[mb].r(bs, dc * 128, dc * 128 + 128), rhs=Pt.r(bs, 0, T),
                                                         start=(mb == 0), stop=(mb == nb - 1 and not has_state)), r=[V[mb], Pt], w=[pso[dc]])
                    self.rput(Pt)
                if has_state:
                    for dc in range(4):
                        def fn(h):
                            last = None
                            for cc in range(2):
                                last = h.matmul(pso[dc].t[0:128, 0:T], lhsT=Rt[cc].r(128, dc * 128, dc * 128 + 128), rhs=qx[cc].r(128, 0, T), start=False, stop=(cc == 1))
                            return last
                        self.op('pe', fn, r=Rt + qx, w=[pso[dc]])
                    self.rput(*qx)
                for cc in range(2):
                    psr = self.ps_get()

                    def fn(h):
                        last = None
                        for mb in range(nb):
                            last = h.matmul(psr.t[0:128, 0:512], lhsT=kz[mb].r(bs, cc * 128, cc * 128 + 128), rhs=V[mb].r(bs, 0, 512), start=(mb == 0), stop=(mb == nb - 1))
                        return last
                    self.op('pe', fn, r=kz + V, w=[psr])
                    Rn = self.fget()
                    if has_state:
                        self.op('dve', lambda h: h.scalar_tensor_tensor(out=Rn.f(128, 0, 512), in0=Rt[cc].f(128, 0, 512), scalar=gT[hh], in1=psr.t[0:128, 0:512],
                                                                       op0=ALU.mult, op1=ALU.add), r=[Rt[cc], psr], w=[Rn])
                    else:
                        self.op('dve', lambda h: h.tensor_copy(out=Rn.f(128, 0, 512), in_=psr.t[0:128, 0:512]), r=[psr], w=[Rn])
                    self.ps_put(psr)
                    self.dma('sp', self.o_R[j, grp, hh, cc], Rn.f(128, 0, 512), r=[Rn], w=[self.hb_R[j][hh]])
                    self.fput(Rn)
                self.rput(*Rt)
                self.rput(*kz)
                self.rput(*V)
                self.rput(*qrr)
                self.rput(*krr)
                osb = [self.rget() for _ in range(4)]
                osq = [self.rget() for _ in range(4)]
                for dc in range(4):
                    self.op('act', lambda h: h.activation(out=osb[dc].r(128, 0, T), in_=pso[dc].t[0:128, 0:T], func=AF.Copy), r=[pso[dc]], w=[osb[dc]])
                    self.op('act', lambda h: h.activation(out=osq[dc].r(128, 0, T), in_=pso[dc].t[0:128, 0:T], func=AF.Square), r=[pso[dc]], w=[osq[dc]])
                    self.ps_put(pso[dc])
                yield 'Ba'
                ps1 = self.ps_get()
                ps2 = self.ps_get()

                def fn1(h):
                    last = None
                    for dc in range(4):
                        last = h.matmul(ps1.t[0:128, 0:T], lhsT=self.ones.r(128, 0, 128), rhs=osb[dc].r(128, 0, T), start=(dc == 0), stop=(dc == 3))
                    return last

                def fn2(h):
                    last = None
                    for dc in range(4):
                        last = h.matmul(ps2.t[0:128, 0:T], lhsT=self.ones.r(128, 0, 128), rhs=osq[dc].r(128, 0, T), start=(dc == 0), stop=(dc == 3))
                    return last
                self.op('pe', fn1, r=[self.ones] + osb, w=[ps1])
                self.op('pe', fn2, r=[self.ones] + osq, w=[ps2])
                self.rput(*osq)
                mean = self.fget()
                var = self.fget()
                self.op('act', lambda h: h.activation(out=mean.f(128, 0, T), in_=ps1.t[0:128, 0:T], func=AF.Copy, scale=1.0 / 512), r=[ps1], w=[mean])
                self.op('act', lambda h: h.activation(out=var.f(128, 0, T), in_=ps1.t[0:128, 0:T], func=AF.Square, scale=1.0 / 512), r=[ps1], w=[var])
                self.ps_put(ps1)
                self.op('dve', lambda h: h.scalar_tensor_tensor(out=var.f(128, 0, T), in0=ps2.t[0:128, 0:T], scalar=1.0 / 512, in1=var.f(128, 0, T), op0=ALU.mult, op1=ALU.subtract),
                        r=[ps2, var], w=[var])
                self.ps_put(ps2)
                self.op('act', lambda h: h.activation(out=var.f(128, 0, T), in_=var.f(128, 0, T), func=AF.Ln, bias=self.epsc.t[0:128, 0:1]), r=[var, self.epsc], w=[var])
                self.op('act', lambda h: h.activation(out=var.f(128, 0, T), in_=var.f(128, 0, T), func=AF.Exp, scale=-0.5), r=[var], w=[var])
                go = []
                for dc in range(4):
                    wg = self.ws_next(('cg', j, hh, dc))
                    psg = self.linear(wg, hn, T)
                    sg = self.fget()
                    self.op('act', lambda h: h.activation(out=sg.f(128, 0, T), in_=psg.t[0:128, 0:T], func=AF.Silu), r=[psg], w=[sg])
                    self.ps_put(psg)
                    t = self.fget()
                    self.op('dve', lambda h: h.tensor_tensor(out=t.f(128, 0, T), in0=osb[dc].f(128, 0, T), in1=mean.f(128, 0, T), op=ALU.subtract), r=[osb[dc], mean], w=[t])
                    self.op('dve', lambda h: h.tensor_tensor(out=t.f(128, 0, T), in0=t.f(128, 0, T), in1=var.f(128, 0, T), op=ALU.mult), r=[t, var], w=[t])
                    g_ = self.rget()
                    self.op('dve', lambda h: h.tensor_tensor(out=g_.r(128, 0, T), in0=t.f(128, 0, T), in1=sg.f(128, 0, T), op=ALU.mult), r=[t, sg], w=[g_])
                    self.fput(sg, t)
                    go.append(g_)
                self.rput(*osb)
                self.fput(mean, var)
                for c in range(8):
                    wo = self.ws_next(('co', j, hh, c))
                    psy = self.linear(wo, go, T)
                    if hh == 0:
                        self.evac(psy, y[c].f(128, 0, T), 128, T, y[c])
                    else:
                        self.op('dve', lambda h: h.tensor_tensor(out=y[c].f(128, 0, T), in0=psy.t[0:128, 0:T], in1=y[c].f(128, 0, T), op=ALU.add), r=[psy, y[c]], w=[y[c]])
                    self.ps_put(psy)
                self.rput(*go)

        gens = [head_gen(h_) for h_ in range(4)]
        next(gens[0])
        for h_ in range(4):
            next(gens[h_])
            if h_ + 1 < 4:
                next(gens[h_ + 1])
            for _ in gens[h_]:
                pass
        self.rput(*hn)
        self.fput(*rot)
        self.postnorm_add(y, vcol_norm(i, 3, 0), False, T)
        self.fput(*y)


_CACHE = {}


def host_inputs(I, b, wpack, vec, C):
    xT = np.concatenate([I['x_prompt'][b].T, I['x_sample'][b].T], axis=1).reshape(8, 128, NCOL)
    pT = np.concatenate([I['p_prompt'][:, b].transpose(0, 2, 1), I['p_sample'][:, b].transpose(0, 2, 1)], axis=2).reshape(DEPTH, 2, 128, NCOL)
    m = {
        'wpack': wpack, 'lrupack': np.stack([piece_array(('ablru', j), I) for j in range(2)]), 'xT': np.ascontiguousarray(xT), 'pT': np.ascontiguousarray(pT), 'vec': vec,
        'c_ones': C['ones'], 'c_ident': C['ident'], 'c_tri': C['tri'], 'c_masks': C['masks'], 'c_rot': C['rot'],
        'c_decayT': C['decayT'], 'c_xi': C['xi'], 'c_zeta': C['zeta'],
        's_lruh': np.ascontiguousarray(I['state_lru_h'][:, b].reshape(2, 4, 128).transpose(0, 2, 1)),
        's_conv': np.ascontiguousarray(I['state_conv'][:, b].reshape(2, 3, 4, 128).transpose(0, 3, 2, 1).reshape(2, 128, 12)),
        's_kT': np.ascontiguousarray(I['cache_sb_k'][:, b].transpose(0, 2, 3, 1)),
        's_v': np.ascontiguousarray(I['cache_sb_v'][:, b].reshape(2, SEQ, 512)),
        's_R': np.ascontiguousarray(I['state_ret'][:, b].reshape(2, 4, 2, 128, 512)),
    }
    return m


def run(I, depth=DEPTH, en_ab=True, en_c=True, ncores=8, trace=False):
    key = (depth, en_ab, en_c)
    if key not in _CACHE:
        _CACHE[key] = Builder(depth, en_ab, en_c)
    B = _CACHE[key]
    I = {k: np.asarray(v) for k, v in I.items()}
    wpack = np.stack([piece_array(s, I) for s in B.specs])
    vec = build_vec(I)
    in_maps = [host_inputs(I, b, wpack, vec, B.C) for b in range(ncores)]
    res = run_bass_kernel_spmd(B.nc, in_maps, core_ids=list(range(ncores)), trace=trace)
    return res


def assemble(results, nb=8):
    R = results
    y = np.stack([r['o_y'].reshape(D, NCOL).T for r in R])
    y_p, y_s = y[:, :SEQ], y[:, SEQ:]
    lruh = np.stack([r['o_lruh'] for r in R])
    h_all = lruh.transpose(1, 2, 0, 4, 3).reshape(2, 2, nb, 512)
    conv = np.stack([r['o_conv'] for r in R]).reshape(nb, 2, 2, 128, 4, 3)
    conv = conv.transpose(1, 2, 0, 5, 4, 3).reshape(2, 2, nb, 3, 512)
    kT = np.stack([r['o_kT'] for r in R])
    k = kT.transpose(1, 0, 4, 2, 3)
    v = np.stack([r['o_v'] for r in R]).reshape(nb, 2, NCOL, 8, 64).transpose(1, 0, 2, 3, 4)
    Rr = np.stack([r['o_R'] for r in R]).reshape(nb, 2, 2, 4, 256, 512)
    Rr = Rr.transpose(1, 2, 0, 3, 4, 5)
    c = np.ascontiguousarray
    return (c(y_p), c(y_s), c(h_all[:, 0]), c(conv[:, 0]), c(k[:, :, :SEQ]), c(v[:, :, :SEQ]), c(Rr[:, 0]),
            c(h_all[:, 1]), c(conv[:, 1]), c(k[:, :, SEQ:]), c(v[:, :, SEQ:]), c(Rr[:, 1]))


def kernel(**inputs):
    res = run(inputs)
    return assemble(res.results)
```

```python
import numpy as np
from contextlib import ExitStack
import concourse.bass as bass
import concourse.mybir as mybir
from concourse.bass_utils import run_bass_kernel_spmd

F32 = mybir.dt.float32
F32R = mybir.dt.float32r
AF = mybir.ActivationFunctionType
ALU = mybir.AluOpType

D = 1024
DEPTH = 4
SEQ = 2048
DSEQ = 64
NCOL = SEQ + DSEQ
DFF = 2816
NFC = 22
EPS = 1e-6
SAME_ENGINE_SYNC = True
NSLOT = 8
PE_DRAIN = True
ATT_SKEW = 1
NRT = 36
NZT = 10
NFT = 22
PREFETCH = 4
SEG_T = 512


def pass_specs(depth=DEPTH, en_ab=True, en_c=True):
    sp = []
    for i in range(depth):
        j = i // 2
        for w in range(2):
            if w == 1:
                if i % 2 == 0 and en_ab:
                    for q in range(4):
                        sp.append(('abv', j, q))
                    for c in range(4):
                        sp.append(('abxa', j, c))
                        sp.append(('abga', j, c))
                    for h in range(8):
                        sp.append(('abqk', j, h))
                    for c in range(8):
                        sp.append(('abo', j, c))
                if i % 2 == 1 and en_c:
                    def _A(h):
                        for cc in range(2):
                            sp.append(('cq', j, h, cc))
                        for cc in range(2):
                            sp.append(('ck', j, h, cc))
                        for q in range(4):
                            sp.append(('cv', j, h, q))
                    _A(0)
                    for h in range(4):
                        if h + 1 < 4:
                            _A(h + 1)
                        for dc in range(4):
                            sp.append(('cg', j, h, dc))
                        for c in range(8):
                            sp.append(('co', j, h, c))
            for f in range(NFC):
                sp.append(('gate', i, w, f))
                sp.append(('up', i, w, f))
            for c in range(8):
                for part in range(3):
                    sp.append(('down', i, w, c, part))
        for c in range(8):
            sp.append(('pleg', i, c))
            sp.append(('plep', i, c))
    return sp


def kpiece(W, r0, nk, c0, ncol):
    a = W[r0:r0 + nk * 128, c0:c0 + ncol].reshape(nk, 128, ncol).transpose(1, 0, 2).reshape(128, nk * ncol)
    return a


def piece_array(spec, I):
    out = np.zeros((128, 1024), np.float32)
    k = spec[0]
    if k in ('gate', 'up'):
        W = I['ffn_w_gate' if k == 'gate' else 'ffn_w_up'][spec[1], spec[2]]
        a = kpiece(W, 0, 8, spec[3] * 128, 128)
    elif k == 'down':
        W = I['ffn_w_down'][spec[1], spec[2]]
        f0 = spec[4] * 8
        nf = min(8, NFC - f0)
        a = kpiece(W, f0 * 128, nf, spec[3] * 128, 128)
    elif k == 'pleg':
        a = kpiece(I['ple_w_gate'][spec[1]], 0, 8, spec[2] * 128, 128)
    elif k == 'plep':
        a = kpiece(I['ple_w_in'][spec[1]], 0, 2, spec[2] * 128, 128)
    elif k == 'abv':
        a = kpiece(I['ab_w_in'][spec[1]], spec[2] * 256, 2, 2048, 512)
    elif k == 'ablru':
        j = spec[1]
        a = np.zeros((128, 1024), np.float32)
        for gi, nm in enumerate(('lru_w_r', 'lru_w_i')):
            for c in range(4):
                for hh in range(2):
                    a[hh * 64:(hh + 1) * 64, gi * 512 + c * 128 + hh * 64: gi * 512 + c * 128 + (hh + 1) * 64] = I[nm][j, 2 * c + hh]
    elif k == 'abxa':
        a = kpiece(I['ab_w_in'][spec[1]], 0, 8, spec[2] * 128, 128)
    elif k == 'abga':
        a = kpiece(I['ab_w_in'][spec[1]], 0, 8, 512 + spec[2] * 128, 128)
    elif k == 'abqk':
        W = I['ab_w_in'][spec[1]]
        a = np.concatenate([kpiece(W, 0, 8, 1024 + spec[2] * 64, 64), kpiece(W, 0, 8, 1536 + spec[2] * 64, 64)], axis=1)
    elif k == 'abo':
        a = kpiece(I['ab_w_out'][spec[1]], 0, 8, spec[2] * 128, 128)
    elif k == 'cq':
        a = kpiece(I['ret_w_in'][spec[1]], 0, 8, spec[2] * 256 + spec[3] * 128, 128)
    elif k == 'ck':
        a = kpiece(I['ret_w_in'][spec[1]], 0, 8, 1024 + spec[2] * 256 + spec[3] * 128, 128)
    elif k == 'cv':
        a = kpiece(I['ret_w_in'][spec[1]], spec[3] * 256, 2, 2048 + spec[2] * 512, 512)
    elif k == 'cg':
        a = kpiece(I['ret_w_in'][spec[1]], 0, 8, 4096 + spec[2] * 512 + spec[3] * 128, 128)
    elif k == 'co':
        a = kpiece(I['ret_w_out'][spec[1]], spec[2] * 512, 4, spec[3] * 128, 128)
    else:
        raise ValueError(spec)
    out[:a.shape[0], :a.shape[1]] = a
    return out


def piece_cols(spec):
    k = spec[0]
    if k == 'down':
        return min(8, NFC - spec[4] * 8) * 128
    if k == 'plep':
        return 256
    if k in ('co',):
        return 512
    return 1024


def vcol_norm(i, n, c):
    return (i * 8 + n) * 8 + c
VC_CONVW = 256
VC_CONVB = 288
VC_BR = 296
VC_BI = 304
VC_LAM = 312
NVEC = 320


def build_vec(I):
    v = np.zeros((128, NVEC), np.float32)
    ng = I['norm_g']
    for i in range(DEPTH):
        for n in range(8):
            v[:, vcol_norm(i, n, 0):vcol_norm(i, n, 0) + 8] = ng[i, n].reshape(8, 128).T
    for j in range(2):
        for tap in range(4):
            v[:, VC_CONVW + (j * 4 + tap) * 4: VC_CONVW + (j * 4 + tap) * 4 + 4] = I['lru_conv_w'][j, tap].reshape(4, 128).T
        v[:, VC_CONVB + j * 4: VC_CONVB + j * 4 + 4] = I['lru_conv_b'][j].reshape(4, 128).T
        v[:, VC_BR + j * 4: VC_BR + j * 4 + 4] = I['lru_b_r'][j].reshape(4, 128).T
        v[:, VC_BI + j * 4: VC_BI + j * 4 + 4] = I['lru_b_i'][j].reshape(4, 128).T
        v[:, VC_LAM + j * 4: VC_LAM + j * 4 + 4] = I['lru_lambda'][j].reshape(4, 128).T
    return v


def build_consts():
    C = {}
    C['ones'] = np.ones((128, 128), np.float32)
    C['ident'] = np.eye(128, dtype=np.float32)
    jj = np.arange(128)
    C['tri'] = (jj[:, None] >= jj[None, :]).astype(np.float32)
    q = np.arange(512)
    C['masks'] = np.stack([((128 * kb + jj)[:, None] < q[None, :]).astype(np.float32) for kb in range(4)])
    half = 128
    freq = (np.float32(10000.0) ** (-np.arange(half, dtype=np.float32) / np.float32(half))).astype(np.float32)
    pos = np.arange(NCOL, dtype=np.float32)
    ang = (pos[None, :] * freq[:, None]).astype(np.float32)
    cs = np.cos(ang).astype(np.float32)
    sn = np.sin(ang).astype(np.float32)
    C['rot'] = np.stack([cs, sn, cs * np.float32(1.0 / 16), sn * np.float32(1.0 / 16)]).astype(np.float32)
    log_g = np.log(np.float32(1.0) - np.float32(2.0) ** (-5.0 - np.arange(4, dtype=np.float32))).astype(np.float32)
    m = np.arange(512)
    l = np.arange(512)
    dT = np.zeros((4, 512, 512), np.float32)
    cm = m[:, None] // 64
    cl = l[None, :] // 64
    for h in range(4):
        same = np.exp(log_g[h] * np.abs(l[None, :] - m[:, None]).astype(np.float32))
        prev = np.exp(log_g[h] * (l[None, :] - m[:, None]).astype(np.float32))
        dT[h] = np.where(cm == cl, same, np.where(cm < cl, prev, 0.0))
    C['decayT'] = dT.astype(np.float32)
    xi = np.stack([np.exp(log_g[h] * (l.astype(np.float32) + 1.0)) for h in range(4)]).astype(np.float32)
    C['xi'] = np.broadcast_to(xi[:, None, :], (4, 128, 512)).copy()
    z = np.zeros((128, 20), np.float32)
    for h in range(4):
        for blk in range(4):
            z[:, h * 5 + blk] = np.exp(log_g[h] * (511.0 - (blk * 128 + jj)).astype(np.float32))
        z[:64, h * 5 + 4] = np.exp(log_g[h] * (63.0 - jj[:64]).astype(np.float32))
    C['zeta'] = z
    C['gT'] = {512: [float(np.exp(log_g[h] * np.float32(512.0))) for h in range(4)],
               64: [float(np.exp(log_g[h] * np.float32(64.0))) for h in range(4)]}
    return C


class Buf:
    __slots__ = ('w', 'r', 'wm')

    def __init__(self, multi=False):
        self.w = None
        self.r = {}
        self.wm = {} if multi else None


class Tile:
    def __init__(self, t, shape):
        self.t = t
        self.b = Buf()
        self.shape = shape

    def f(self, p=None, a=0, b=None):
        p = self.shape[0] if p is None else p
        b = self.shape[1] if b is None else b
        return self.t[0:p, a:b]

    def r(self, p=None, a=0, b=None):
        return self.f(p, a, b).bitcast(F32R)


def _cls(n):
    return 32 if n <= 32 else (64 if n <= 64 else 128)


class PEProxy:
    def __init__(self, h):
        self.h = h
        self.last = None
        self.ndrain = 0

    def matmul(self, out, lhsT, rhs, **kw):
        c = (_cls(lhsT.shape[0]), _cls(lhsT.shape[-1]))
        if PE_DRAIN and self.last is not None and c != self.last:
            self.h.drain()
            self.ndrain += 1
        self.last = c
        return self.h.matmul(out, lhsT=lhsT, rhs=rhs, **kw)

    def wait_ge(self, *a, **k):
        return self.h.wait_ge(*a, **k)


class Eng:
    def __init__(self, name, h, sem):
        self.name = name
        self.h = h
        self.sem = sem
        self.cnt = 0
        self.seen = {}


class Builder:
    def __init__(self, depth=DEPTH, en_ab=True, en_c=True, segs=None):
        self.depth = depth
        self.en_ab = en_ab
        self.en_c = en_c
        self.C = build_consts()
        self.specs = pass_specs(depth, en_ab, en_c)
        self.npp = len(self.specs)
        if segs is None:
            segs = [dict(kind='p', s0=s, T=SEG_T, col0=s) for s in range(0, SEQ, SEG_T)] + [dict(kind='s', s0=SEQ, T=DSEQ, col0=SEQ)]
        self.segs = segs
        self.nc = bass.Bass("TRN2", target_bir_lowering=False)
        self.es = ExitStack()
        self.build()

    def dram_in(self, name, shape):
        return self.nc.dram_tensor(name, list(shape), F32, kind="ExternalInput").ap()

    def dram_out(self, name, shape):
        return self.nc.dram_tensor(name, list(shape), F32, kind="ExternalOutput").ap()

    def sb(self, name, shape):
        t = self.es.enter_context(self.nc.sbuf_tensor("sb_" + name, list(shape), F32))
        return Tile(t, shape)

    def setup(self):
        nc, es = self.nc, self.es
        self.E = {}
        self.sem = {}
        for name, h in (('pe', PEProxy(nc.tensor)), ('act', nc.scalar), ('dve', nc.vector), ('pool', nc.gpsimd), ('sp', nc.sync)):
            s = es.enter_context(nc.semaphore("s_" + name))
            self.E[name] = Eng(name, h, s)
            self.sem[name] = s
        self.dring = {}
        for q, n in (('pool', 12), ('sp', 24)):
            sems = []
            for i in range(n):
                s = es.enter_context(nc.semaphore(f"d_{q}{i}"))
                self.sem[(q, i)] = s
                sems.append(s)
            self.dring[q] = dict(n=0, val=[0] * n, size=n)
        self.psb = []
        for i in range(8):
            t = es.enter_context(nc.psum_tensor(f"ps{i}", [128, 512], F32))
            self.psb.append(Tile(t, [128, 512]))
        self.ps_free = list(self.psb)
        self.rt_free = [self.sb(f"rt{i}", [128, 512]) for i in range(NRT)]
        self.ft_free = [self.sb(f"ft{i}", [128, 512]) for i in range(NFT)]
        self.zt_free = [self.sb(f"zt{i}", [128, 512]) for i in range(NZT)]
        self.slots = [self.sb(f"ws{i}", [128, 1024]) for i in range(NSLOT)]

    def mark(self, label):
        if not hasattr(self, 'marks'):
            self.marks = []
        self.marks.append((label, self.E['dve'].cnt))

    def ps_get(self):
        assert self.ps_free, "out of PSUM banks"
        return self.ps_free.pop(0)

    def ps_put(self, p):
        self.ps_free.append(p)

    def rget(self):
        assert self.rt_free, "out of R tiles"
        return self.rt_free.pop(0)

    def rput(self, *ts):
        for t in ts:
            self.rt_free.append(t)

    def zget(self):
        assert self.zt_free, "out of Z tiles"
        return self.zt_free.pop(0)

    def zput(self, *ts):
        for t in ts:
            self.zt_free.append(t)

    def fget(self):
        assert self.ft_free, "out of F tiles"
        return self.ft_free.pop(0)

    def fput(self, *ts):
        for t in ts:
            self.ft_free.append(t)

    def _waits(self, eng, r, w):
        need = {}
        for b in list(r) + list(w):
            if b.wm:
                for k, c in b.wm.items():
                    if need.get(k, 0) < c:
                        need[k] = c
        for b in r:
            if b.w is not None:
                k, c = b.w
                if need.get(k, 0) < c:
                    need[k] = c
        for b in w:
            if b.w is not None:
                k, c = b.w
                if need.get(k, 0) < c:
                    need[k] = c
            for k, c in b.r.items():
                if need.get(k, 0) < c:
                    need[k] = c
        for k, c in need.items():
            if k == eng.name and (not SAME_ENGINE_SYNC or k == 'pe'):
                continue
            if eng.seen.get(k, 0) < c:
                eng.h.wait_ge(self.sem[k], c)
                eng.seen[k] = c

    def op(self, e, fn, r=(), w=()):
        eng = self.E[e]
        r = [x.b if isinstance(x, Tile) else x for x in r]
        w = [x.b if isinstance(x, Tile) else x for x in w]
        self._waits(eng, r, w)
        ins = fn(eng.h)
        eng.cnt += 1
        ins.then_inc(eng.sem, 1)
        for b in r:
            if b.r.get(e, 0) < eng.cnt:
                b.r[e] = eng.cnt
        for b in w:
            b.w = (e, eng.cnt)
            b.r = {}
        return ins

    def dma(self, q, out_ap, in_ap, r=(), w=()):
        eng = self.E[q]
        ring = self.dring[q]
        idx = ring['n'] % ring['size']
        ring['n'] += 1
        key = (q, idx)
        prev = ring['val'][idx]
        if prev > 0 and eng.seen.get(key, 0) < prev:
            eng.h.wait_ge(self.sem[key], prev)
            eng.seen[key] = prev
        r = [x.b if isinstance(x, Tile) else x for x in r]
        w = [x.b if isinstance(x, Tile) else x for x in w]
        self._waits(eng, r, w)
        ins = eng.h.dma_start(out=out_ap, in_=in_ap)
        val = prev + 16
        ins.then_inc(self.sem[key], 16)
        ring['val'][idx] = val
        for b in r:
            b.r[key] = val
        for b in w:
            if b.wm is not None:
                b.wm[key] = val
            else:
                b.w = (key, val)
                b.r = {}

    def ws_issue(self, gidx):
        pidx = gidx % self.npp
        spec = self.specs[pidx]
        slot = self.slots[gidx % NSLOT]
        ncol = piece_cols(spec)
        self.dma('pool', slot.r(128, 0, ncol), self.wpack[pidx, :, 0:ncol], r=(), w=[slot])

    def ws_next(self, spec):
        g = self.ws_pos
        assert self.specs[g % self.npp] == spec, (self.specs[g % self.npp], spec)
        while self.ws_issued < min(g + PREFETCH, self.ws_total):
            self.ws_issue(self.ws_issued)
            self.ws_issued += 1
        self.ws_pos += 1
        return self.slots[g % NSLOT]

    def vc(self, col, p=128):
        return self.vec.t[0:p, col:col + 1]

    def linear(self, slot, ins, T, kw=128, M=128, coff=0, ps=None, start=True, stop=True, extra_r=(), per_k=False):
        if ps is None:
            ps = self.ps_get()
        n = len(ins)
        if per_k:
            for k in range(n):
                self.op('pe', lambda h: h.matmul(ps.t[0:M, 0:T], lhsT=slot.r(128, coff + k * kw, coff + k * kw + M), rhs=ins[k].r(128, 0, T),
                                                 start=(start and k == 0), stop=(stop and k == n - 1)), r=[slot, ins[k]], w=[ps])
            return ps

        def fn(h):
            last = None
            for k in range(n):
                last = h.matmul(ps.t[0:M, 0:T], lhsT=slot.r(128, coff + k * kw, coff + k * kw + M), rhs=ins[k].r(128, 0, T),
                                start=(start and k == 0), stop=(stop and k == n - 1))
            return last
        self.op('pe', fn, r=[slot] + list(ins) + list(extra_r), w=[ps])
        return ps

    def evac(self, ps, dst_ap, P, T, dst_tile, eng='act', scale=None):
        if eng == 'act':
            if scale is None:
                self.op('act', lambda h: h.activation(out=dst_ap, in_=ps.t[0:P, 0:T], func=AF.Copy), r=[ps], w=[dst_tile])
            else:
                self.op('act', lambda h: h.activation(out=dst_ap, in_=ps.t[0:P, 0:T], func=AF.Copy, scale=scale), r=[ps], w=[dst_tile])
        else:
            self.op('dve', lambda h: h.tensor_copy(out=dst_ap, in_=ps.t[0:P, 0:T]), r=[ps], w=[dst_tile])

    def rstd_from(self, tiles, T, nfeat, P=128):
        ps = self.ps_get()
        n = len(tiles)
        for c in range(n):
            sq = self.rget()
            self.op('act', lambda h: h.activation(out=sq.r(P, 0, T), in_=tiles[c].f(P, 0, T), func=AF.Square), r=[tiles[c]], w=[sq])
            self.op('pe', lambda h: h.matmul(ps.t[0:128, 0:T], lhsT=self.ones.r(P, 0, 128), rhs=sq.r(P, 0, T), start=(c == 0), stop=(c == n - 1)),
                    r=[sq, self.ones], w=[ps])
            self.rput(sq)
        rstd = self.fget()
        self.op('act', lambda h: h.activation(out=rstd.f(128, 0, T), in_=ps.t[0:128, 0:T], func=AF.Ln, scale=1.0 / nfeat, bias=self.epsc.t[0:128, 0:1]),
                r=[ps, self.epsc], w=[rstd])
        self.ps_put(ps)
        self.op('act', lambda h: h.activation(out=rstd.f(128, 0, T), in_=rstd.f(128, 0, T), func=AF.Exp, scale=-0.5), r=[rstd], w=[rstd])
        return rstd

    def rmsnorm(self, gcol, T):
        rstd = self.rstd_from(self.x, T, D)
        hn = [self.rget() for _ in range(8)]
        for c in range(8):
            self.op('dve', lambda h: h.scalar_tensor_tensor(out=hn[c].r(128, 0, T), in0=self.x[c].f(128, 0, T), scalar=self.vc(gcol + c),
                                                           in1=rstd.f(128, 0, T), op0=ALU.mult, op1=ALU.mult),
                    r=[self.x[c], rstd, self.vec], w=[hn[c]])
        self.fput(rstd)
        return hn

    def postnorm_add(self, y, gcol, half, T):
        rstd = self.rstd_from(y, T, D)
        vt = self.vech if half else self.vec
        for c in range(8):
            self.op('dve', lambda h: h.scalar_tensor_tensor(out=y[c].f(128, 0, T), in0=y[c].f(128, 0, T), scalar=vt.t[0:128, gcol + c:gcol + c + 1],
                                                           in1=rstd.f(128, 0, T), op0=ALU.mult, op1=ALU.mult),
                    r=[y[c], rstd, vt], w=[y[c]])
            self.op('dve', lambda h: h.tensor_tensor(out=self.x[c].f(128, 0, T), in0=self.x[c].f(128, 0, T), in1=y[c].f(128, 0, T), op=ALU.add),
                    r=[self.x[c], y[c]], w=[self.x[c]])
        self.fput(rstd)

    def ffn(self, i, w, T):
        hn = self.rmsnorm(vcol_norm(i, 0 if w == 0 else 4, 0), T)
        hh = []
        for f in range(NFC):
            wg = self.ws_next(('gate', i, w, f))
            wu = self.ws_next(('up', i, w, f))
            psg = self.linear(wg, hn, T, per_k=(f == 0))
            psu = self.linear(wu, hn, T)
            s = self.fget()
            self.op('act', lambda h: h.activation(out=s.f(128, 0, T), in_=psg.t[0:128, 0:T], func=AF.Silu), r=[psg], w=[s])
            self.ps_put(psg)
            ht = self.rget()
            self.op('dve', lambda h: h.tensor_tensor(out=ht.r(128, 0, T), in0=psu.t[0:128, 0:T], in1=s.f(128, 0, T), op=ALU.mult), r=[psu, s], w=[ht])
            self.ps_put(psu)
            self.fput(s)
            hh.append(ht)
        self.rput(*hn)
        y = [self.fget() for _ in range(8)]
        for c in range(8):
            psy = self.ps_get()
            for part in range(3):
                wd = self.ws_next(('down', i, w, c, part))
                f0 = part * 8
                f1 = min(NFC, f0 + 8)
                self.linear(wd, hh[f0:f1], T, ps=psy, start=(part == 0), stop=(part == 2))
            self.evac(psy, y[c].f(128, 0, T), 128, T, y[c])
            self.ps_put(psy)
        self.rput(*hh)
        self.postnorm_add(y, vcol_norm(i, 1 if w == 0 else 5, 0), True, T)
        self.fput(*y)

    def ple(self, i, seg):
        T, col0 = seg['T'], seg['col0']
        hn = self.rmsnorm(vcol_norm(i, 6, 0), T)
        pt = [self.rget() for _ in range(2)]
        for k in range(2):
            self.dma('pool', pt[k].r(128, 0, T), self.pT[i, k, :, col0:col0 + T], w=[pt[k]])
        z = [self.fget() for _ in range(8)]
        for c in range(8):
            wg = self.ws_next(('pleg', i, c))
            wp = self.ws_next(('plep', i, c))
            psg = self.linear(wg, hn, T)
            psp = self.linear(wp, pt, T)
            sg = self.fget()
            self.op('act', lambda h: h.activation(out=sg.f(128, 0, T), in_=psg.t[0:128, 0:T], func=AF.Sigmoid), r=[psg], w=[sg])
            self.ps_put(psg)
            self.op('dve', lambda h: h.tensor_tensor(out=z[c].f(128, 0, T), in0=psp.t[0:128, 0:T], in1=sg.f(128, 0, T), op=ALU.mult), r=[psp, sg], w=[z[c]])
            self.ps_put(psp)
            self.fput(sg)
        self.rput(*hn)
        self.rput(*pt)
        self.postnorm_add(z, vcol_norm(i, 7, 0), False, T)
        self.fput(*z)

    def build(self):
        nc = self.nc
        self.wpack = self.dram_in("wpack", [self.npp, 128, 1024])
        self.xT = self.dram_in("xT", [8, 128, NCOL])
        self.pT = self.dram_in("pT", [DEPTH, 2, 128, NCOL])
        self.vec_d = self.dram_in("vec", [128, NVEC])
        self.lrupack = self.dram_in("lrupack", [2, 128, 1024])
        self.c_ones = self.dram_in("c_ones", [128, 128])
        self.c_ident = self.dram_in("c_ident", [128, 128])
        self.c_tri = self.dram_in("c_tri", [128, 128])
        self.c_masks = self.dram_in("c_masks", [4, 128, 512])
        self.c_rot = self.dram_in("c_rot", [4, 128, NCOL])
        self.c_decayT = self.dram_in("c_decayT", [4, 512, 512])
        self.c_xi = self.dram_in("c_xi", [4, 128, 512])
        self.c_zeta = self.dram_in("c_zeta", [128, 20])
        self.s_lruh = self.dram_in("s_lruh", [2, 128, 4])
        self.s_conv = self.dram_in("s_conv", [2, 128, 12])
        self.s_kT = self.dram_in("s_kT", [2, 8, 64, SEQ])
        self.s_v = self.dram_in("s_v", [2, SEQ, 512])
        self.s_R = self.dram_in("s_R", [2, 4, 2, 128, 512])
        self.o_y = self.dram_out("o_y", [8, 128, NCOL])
        self.o_lruh = self.dram_out("o_lruh", [2, 2, 128, 4])
        self.o_conv = self.dram_out("o_conv", [2, 2, 128, 12])
        self.o_kT = self.dram_out("o_kT", [2, 8, 64, NCOL])
        self.o_v = self.dram_out("o_v", [2, NCOL, 512])
        self.o_R = self.dram_out("o_R", [2, 2, 4, 2, 128, 512])
        self.setup()
        self.x = [self.sb(f"x{c}", [128, 512]) for c in range(8)]
        self.vec = self.sb("vec", [128, NVEC])
        self.vech = self.sb("vech", [128, NVEC])
        self.ones = self.sb("ones", [128, 128])
        self.ident = self.sb("ident", [128, 128])
        self.tri = self.sb("tri", [128, 128])
        self.masks = [self.sb(f"mask{k}", [128, 512]) for k in range(4)]
        self.zeta = self.sb("zeta", [128, 20])
        self.epsc = self.sb("epsc", [128, 1])
        self.lruh = [self.sb(f"lruh{j}", [128, 4]) for j in range(2)]
        self.carry = [self.sb(f"carry{j}", [128, 12]) for j in range(2)]
        self.m8c = self.sb("m8c", [128, 8])
        self.wl = self.sb("wl", [128, 1024])
        self.hb_kv = [Buf(True) for _ in range(2)]
        self.hb_kv_seen = [{}, {}]
        self.hb_R = [[Buf(True) for _ in range(4)] for _ in range(2)]
        self.ws_pos = 0
        self.ws_issued = 0
        self.ws_total = self.npp * len(self.segs)

        self.dma('sp', self.vec.f(), self.vec_d, w=[self.vec])
        self.dma('sp', self.zeta.f(), self.c_zeta, w=[self.zeta])
        self.dma('pool', self.ones.r(), self.c_ones, w=[self.ones])
        self.dma('pool', self.ident.r(), self.c_ident, w=[self.ident])
        self.dma('pool', self.tri.r(), self.c_tri, w=[self.tri])
        for k in range(4):
            self.dma('sp', self.masks[k].f(), self.c_masks[k], w=[self.masks[k]])
        self.op('dve', lambda h: h.memset(self.epsc.f(), EPS), w=[self.epsc])
        for zt in self.zt_free:
            self.op('dve', lambda h: h.tensor_scalar(out=zt.r(128, 0, 512), in0=self.masks[0].f(128, 0, 512), scalar1=0.0, scalar2=None, op0=ALU.mult), r=[self.masks[0]], w=[zt])
        self.op('act', lambda h: h.activation(out=self.vech.f(), in_=self.vec.f(), func=AF.Copy, scale=0.5), r=[self.vec], w=[self.vech])
        self.op('act', lambda h: h.activation(out=self.m8c.f(), in_=self.vec.t[0:128, VC_LAM:VC_LAM + 8], func=AF.Exp, scale=-1.0), r=[self.vec], w=[self.m8c])
        self.op('act', lambda h: h.activation(out=self.m8c.f(), in_=self.m8c.f(), func=AF.Ln, bias=1.0), r=[self.m8c], w=[self.m8c])
        self.op('act', lambda h: h.activation(out=self.m8c.f(), in_=self.m8c.f(), func=AF.Copy, scale=-8.0), r=[self.m8c], w=[self.m8c])
        for j in range(2):
            self.op('dve', lambda h: h.memset(self.lruh[j].f(), 0.0), w=[self.lruh[j]])
            self.op('dve', lambda h: h.memset(self.carry[j].f(), 0.0), w=[self.carry[j]])

        nprompt = sum(1 for s in self.segs if s['kind'] == 'p')
        for si, seg in enumerate(self.segs):
            T, col0 = seg['T'], seg['col0']
            seg['first'] = (seg['kind'] == 'p' and seg['s0'] == 0)
            seg['grp'] = 0 if seg['kind'] == 'p' else 1
            seg['last'] = (seg['kind'] == 's') or (si == nprompt - 1)
            if seg['kind'] == 's':
                for j in range(2):
                    self.dma('sp', self.lruh[j].f(), self.s_lruh[j], w=[self.lruh[j]])
                    self.dma('sp', self.carry[j].f(), self.s_conv[j], w=[self.carry[j]])
            for c in range(8):
                self.dma('sp', self.x[c].f(128, 0, T), self.xT[c, :, col0:col0 + T], w=[self.x[c]])
            for i in range(self.depth):
                self.mark(f"s{si} L{i} ffn0")
                self.ffn(i, 0, T)
                if i % 2 == 0 and self.en_ab:
                    self.mark(f"s{si} L{i} ab")
                    self.mixer_ab(i // 2, i, seg)
                if i % 2 == 1 and self.en_c:
                    self.mark(f"s{si} L{i} c")
                    self.mixer_c(i // 2, i, seg)
                self.mark(f"s{si} L{i} ffn1")
                self.ffn(i, 1, T)
                self.mark(f"s{si} L{i} ple")
                self.ple(i, seg)
            self.mark(f"s{si} end")
            for c in range(8):
                self.dma('sp', self.o_y[c, :, col0:col0 + T], self.x[c].f(128, 0, T), r=[self.x[c]])
            if seg['last']:
                for j in range(2):
                    self.dma('sp', self.o_lruh[j, seg['grp']], self.lruh[j].f(), r=[self.lruh[j]])
                    self.dma('sp', self.o_conv[j, seg['grp']], self.carry[j].f(), r=[self.carry[j]])
        sp = self.E['sp']
        for q in ('pool', 'sp'):
            ring = self.dring[q]
            for idx in range(ring['size']):
                if ring['val'][idx] > 0:
                    sp.h.wait_ge(self.sem[(q, idx)], ring['val'][idx])
        self.es.close()

    def mixer_ab(self, j, i, seg):
        T, col0, s0 = seg['T'], seg['col0'], seg['s0']
        grp = seg['grp']
        bs = min(128, T)
        nb = (T + 127) // 128
        self.hb_kv_seen[j] = dict(self.hb_kv[j].wm)
        hn = self.rmsnorm(vcol_norm(i, 2, 0), T)
        wv = [self.ws_next(('abv', j, q)) for q in range(4)]
        Vn = []
        for blk in range(nb):
            psv = self.ps_get()

            def fn(h):
                last = None
                for k in range(8):
                    last = h.matmul(psv.t[0:bs, 0:512], lhsT=hn[k].r(128, blk * 128, blk * 128 + bs), rhs=wv[k // 2].r(128, (k % 2) * 512, (k % 2) * 512 + 512),
                                    start=(k == 0), stop=(k == 7))
                return last
            self.op('pe', fn, r=hn + wv, w=[psv])
            vt = self.rget()
            self.evac(psv, vt.r(bs, 0, 512), bs, 512, vt)
            self.ps_put(psv)
            self.dma('sp', self.o_v[j, col0 + blk * 128: col0 + blk * 128 + bs, :], vt.f(bs, 0, 512), r=[vt], w=[self.hb_kv[j]])
            Vn.append(vt)
        wl = self.wl
        self.dma('pool', wl.r(), self.lrupack[j], w=[wl])
        def lru_gen(c):
                wxa = self.ws_next(('abxa', j, c))
                wga = self.ws_next(('abga', j, c))
                psx = self.linear(wxa, hn, T)
                psg = self.linear(wga, hn, T)
                xa = self.fget()
                self.evac(psx, xa.f(128, 0, T), 128, T, xa)
                self.ps_put(psx)
                acc = self.fget()
                xc = self.rget()
                cw = lambda tap: self.vc(VC_CONVW + (j * 4 + tap) * 4 + c)
                car = self.carry[j]
                self.op('dve', lambda h: h.tensor_scalar(out=acc.f(128, 0, T), in0=xa.f(128, 0, T), scalar1=cw(3), scalar2=self.vc(VC_CONVB + j * 4 + c),
                                                         op0=ALU.mult, op1=ALU.add), r=[xa, self.vec], w=[acc])
                for s in (1, 2, 3):
                    last = (s == 3)
                    dst = xc if last else acc
                    o_big = dst.r(128, s, T) if last else dst.f(128, s, T)
                    o_small = dst.r(128, 0, s) if last else dst.f(128, 0, s)
                    self.op('dve', lambda h: h.scalar_tensor_tensor(out=o_big, in0=xa.f(128, 0, T - s), scalar=cw(3 - s), in1=acc.f(128, s, T),
                                                                   op0=ALU.mult, op1=ALU.add), r=[xa, acc, self.vec], w=[dst])
                    self.op('dve', lambda h: h.scalar_tensor_tensor(out=o_small, in0=car.t[0:128, c * 3 + 3 - s: c * 3 + 3], scalar=cw(3 - s), in1=acc.f(128, 0, s),
                                                                   op0=ALU.mult, op1=ALU.add), r=[car, acc, self.vec], w=[dst])
                self.op('dve', lambda h: h.tensor_copy(out=car.t[0:128, c * 3: c * 3 + 3], in_=xa.f(128, T - 3, T)), r=[xa], w=[car])
                self.fput(xa, acc)
                yield
                psr = self.ps_get()
                psi = self.ps_get()
                self.op('pe', lambda h: h.matmul(psr.t[0:128, 0:T], lhsT=wl.r(128, c * 128, c * 128 + 128), rhs=xc.r(128, 0, T), start=True, stop=True), r=[wl, xc], w=[psr])
                self.op('pe', lambda h: h.matmul(psi.t[0:128, 0:T], lhsT=wl.r(128, 512 + c * 128, 512 + c * 128 + 128), rhs=xc.r(128, 0, T), start=True, stop=True), r=[wl, xc], w=[psi])
                rg = self.fget()
                ig = self.fget()
                self.op('act', lambda h: h.activation(out=rg.f(128, 0, T), in_=psr.t[0:128, 0:T], func=AF.Sigmoid, bias=self.vc(VC_BR + j * 4 + c)), r=[psr, self.vec], w=[rg])
                self.op('act', lambda h: h.activation(out=ig.f(128, 0, T), in_=psi.t[0:128, 0:T], func=AF.Sigmoid, bias=self.vc(VC_BI + j * 4 + c)), r=[psi, self.vec], w=[ig])
                self.ps_put(psr)
                self.ps_put(psi)
                yield
                a = self.fget()
                self.op('act', lambda h: h.activation(out=a.f(128, 0, T), in_=rg.f(128, 0, T), func=AF.Exp, scale=self.m8c.t[0:128, j * 4 + c: j * 4 + c + 1]), r=[rg, self.m8c], w=[a])
                self.op('act', lambda h: h.activation(out=rg.f(128, 0, T), in_=a.f(128, 0, T), func=AF.Square), r=[a], w=[rg])
                self.op('act', lambda h: h.activation(out=rg.f(128, 0, T), in_=rg.f(128, 0, T), func=AF.Sqrt, scale=-1.0, bias=1.0), r=[rg], w=[rg])
                self.op('dve', lambda h: h.tensor_tensor(out=ig.f(128, 0, T), in0=ig.f(128, 0, T), in1=rg.f(128, 0, T), op=ALU.mult), r=[ig, rg], w=[ig])
                self.op('dve', lambda h: h.tensor_tensor(out=ig.f(128, 0, T), in0=ig.f(128, 0, T), in1=xc.f(128, 0, T), op=ALU.mult), r=[ig, xc], w=[ig])
                yield
                hs = rg
                lh = self.lruh[j]
                self.op('dve', lambda h: h.tensor_tensor_scan(out=hs.f(128, 0, T), data0=a.f(128, 0, T), data1=ig.f(128, 0, T), initial=lh.t[0:128, c:c + 1],
                                                             op0=ALU.mult, op1=ALU.add), r=[a, ig, lh], w=[hs])
                self.op('dve', lambda h: h.tensor_copy(out=lh.t[0:128, c:c + 1], in_=hs.f(128, T - 1, T)), r=[hs], w=[lh])
                self.rput(xc)
                yield
                xg = a
                x2 = ig
                self.op('act', lambda h: h.activation(out=xg.f(128, 0, T), in_=psg.t[0:128, 0:T], func=AF.Copy), r=[psg], w=[xg])
                self.op('act', lambda h: h.activation(out=x2.f(128, 0, T), in_=psg.t[0:128, 0:T], func=AF.Square), r=[psg], w=[x2])
                self.ps_put(psg)
                self.op('dve', lambda h: h.tensor_scalar(out=x2.f(128, 0, T), in0=x2.f(128, 0, T), scalar1=0.044715, scalar2=1.0, op0=ALU.mult, op1=ALU.add), r=[x2], w=[x2])
                self.op('dve', lambda h: h.tensor_tensor(out=x2.f(128, 0, T), in0=x2.f(128, 0, T), in1=xg.f(128, 0, T), op=ALU.mult), r=[x2, xg], w=[x2])
                self.op('act', lambda h: h.activation(out=x2.f(128, 0, T), in_=x2.f(128, 0, T), func=AF.Sigmoid, scale=1.5957691216057308), r=[x2], w=[x2])
                self.op('dve', lambda h: h.tensor_tensor(out=x2.f(128, 0, T), in0=x2.f(128, 0, T), in1=xg.f(128, 0, T), op=ALU.mult), r=[x2, xg], w=[x2])
                ca = self.rget()
                self.op('dve', lambda h: h.tensor_tensor(out=ca.r(128, 0, T), in0=x2.f(128, 0, T), in1=hs.f(128, 0, T), op=ALU.mult), r=[x2, hs], w=[ca])
                cat_a[c] = ca
                self.fput(rg, ig, a)

        cat_a = [None] * 4
        for c0_ in (0,):
            alive = [lru_gen(c_) for c_ in range(4)]
            while alive:
                for g_ in list(alive):
                    try:
                        next(g_)
                    except StopIteration:
                        alive.remove(g_)
        if seg['kind'] == 'p':
            npast = s0 // 128
            pastK = lambda hh, a0, a1: self.o_kT[j, hh, :, a0:a1]
            pastV = lambda a0, a1, hh: self.o_v[j, a0:a1, (hh // 2) * 128:(hh // 2) * 128 + 128]
        else:
            npast = SEQ // 128
            pastK = lambda hh, a0, a1: self.s_kT[j, hh, :, a0:a1]
            pastV = lambda a0, a1, hh: self.s_v[j, a0:a1, (hh // 2) * 128:(hh // 2) * 128 + 128]
        ngrp = (npast + 3) // 4
        hb_past = Buf(True)
        hb_past.wm = dict(self.hb_kv_seen[j])
        KQ = 128 if bs == 128 else 64
        ob = [self.rget() for _ in range(4)]
        H = {}
        items = []
        for hh in range(8):
            blocks = [('n', b_) for b_ in range(nb - 1, -1, -1)]
            for g in range(ngrp - 1, -1, -1):
                for b_ in range(min(4, npast - g * 4) - 1, -1, -1):
                    blocks.append(('p', g, b_))
            for bi, blk in enumerate(blocks):
                items.append(dict(hh=hh, blk=blk, first=(bi == 0), last=(bi == len(blocks) - 1),
                                  pre_next=(bi == min(3, len(blocks) - 1) and hh < 7)))

        def load_group(hh, g):
            nbg = min(4, npast - g * 4)
            pk = self.zget()
            pv = self.rget()
            self.dma('pool', pk.r(64, 0, nbg * 128), pastK(hh, g * 512, g * 512 + nbg * 128), r=[hb_past], w=[pk])
            for b2 in range(nbg):
                self.dma('pool', pv.r(128, b2 * 128, b2 * 128 + 128), pastV(g * 512 + b2 * 128, g * 512 + b2 * 128 + 128, hh), r=[hb_past], w=[pv])
            H[hh]['grp'][g] = (pk, pv)

        def head_pre(hh):
            wqk = self.ws_next(('abqk', j, hh))
            psq = self.linear(wqk, hn, T, kw=64, M=64, coff=0)
            psk = self.linear(wqk, hn, T, kw=64, M=64, coff=512)
            qT = self.zget()
            kT = self.zget()
            nkT = self.zget()
            self.evac(psq, qT.r(64, 0, T), 64, T, qT)
            self.evac(psk, kT.r(64, 0, T), 64, T, kT)
            self.op('act', lambda h: h.activation(out=nkT.r(64, 0, T), in_=psq.t[0:64, 0:T], func=AF.Copy, scale=-0.125), r=[psq], w=[nkT])
            self.ps_put(psq)
            self.ps_put(psk)
            self.dma('sp', self.o_kT[j, hh, :, col0:col0 + T], kT.f(64, 0, T), r=[kT], w=[self.hb_kv[j]])
            H[hh] = dict(qT=qT, kT=kT, nkT=nkT, A=None, pso=None, grp={})
            if npast > 0:
                load_group(hh, ngrp - 1)

        def head_post(hh):
            c_ = H.pop(hh)
            obt = ob[hh // 2]
            r0 = (hh % 2) * 64
            pso_ = c_['pso']
            self.op('act', lambda h: h.activation(out=obt.t[r0:r0 + 64, 0:T].bitcast(F32R), in_=pso_.t[r0:r0 + 64, 0:T], func=AF.Copy), r=[pso_], w=[obt])
            self.ps_put(c_['pso'])
            self.zput(c_['qT'], c_['kT'], c_['nkT'])

        def S1(it):
            hh, blk = it['hh'], it['blk']
            if it['first'] and hh == 0:
                head_pre(hh)
            c_ = H[hh]
            qT = c_['qT']
            if blk[0] == 'n':
                kb = blk[1]
                P_ = bs
                it['k'] = (c_['kT'], c_['kT'].r(KQ, kb * 128, kb * 128 + bs))
                it['v'] = (Vn[kb], Vn[kb].r(bs, (hh // 2) * 128, (hh // 2) * 128 + 128))
                it['mask'] = self.masks[kb]
            else:
                g, b_ = blk[1], blk[2]
                P_ = 128
                nbg = min(4, npast - g * 4)
                if b_ == nbg - 1 and g >= 1:
                    load_group(hh, g - 1)
                pk, pv = c_['grp'][g]
                it['k'] = (pk, pk.r(128, b_ * 128, b_ * 128 + 128))
                it['v'] = (pv, pv.r(128, b_ * 128, b_ * 128 + 128))
                it['mask'] = None
            it['P'] = P_
            it['KQ'] = KQ if blk[0] == 'n' else 128
            KK = it['KQ']
            mask = it['mask']
            psz = self.ps_get()
            self.op('pe', lambda h: h.matmul(psz.t[0:P_, 0:T], lhsT=it['k'][1], rhs=qT.r(KK, 0, T), start=True, stop=True), r=[it['k'][0], qT], w=[psz])
            e = self.fget()
            spt = self.rget()
            self.op('act', lambda h: h.activation(out=e.f(P_, 0, T), in_=psz.t[0:P_, 0:T], func=AF.Exp, scale=0.125), r=[psz], w=[e])
            self.ps_put(psz)
            if mask is None:
                self.op('act', lambda h: h.activation(out=spt.r(P_, 0, T), in_=e.f(P_, 0, T), func=AF.Ln, bias=1.0), r=[e], w=[spt])
            else:
                self.op('act', lambda h: h.activation(out=e.f(P_, 0, T), in_=e.f(P_, 0, T), func=AF.Ln, bias=1.0), r=[e], w=[e])
                self.op('dve', lambda h: h.tensor_tensor(out=spt.r(P_, 0, T), in0=e.f(P_, 0, T), in1=mask.f(P_, 0, T), op=ALU.mult), r=[e, mask], w=[spt])
            it['e'] = e
            it['spt'] = spt
            if it['pre_next']:
                head_pre(hh + 1)

        def S1b(it):
            hh = it['hh']
            c_ = H[hh]
            P_ = it['P']
            spt = it['spt']
            it['A'] = c_['A']
            if not it['last']:
                pst = self.ps_get()
                self.op('pe', lambda h: h.matmul(pst.t[0:128, 0:T], lhsT=self.ones.r(P_, 0, 128), rhs=spt.r(P_, 0, T), start=True, stop=True), r=[self.ones, spt], w=[pst])
                An = self.rget()
                Ap = it['A']
                if Ap is None:
                    self.op('dve', lambda h: h.tensor_copy(out=An.r(128, 0, T), in_=pst.t[0:128, 0:T]), r=[pst], w=[An])
                else:
                    self.op('dve', lambda h: h.tensor_tensor(out=An.r(128, 0, T), in0=pst.t[0:128, 0:T], in1=Ap.f(128, 0, T), op=ALU.add), r=[pst, Ap], w=[An])
                self.ps_put(pst)
                c_['A'] = An

        def S2(it):
            hh = it['hh']
            c_ = H[hh]
            P_ = it['P']
            spt, e, mask = it['spt'], it['e'], it['mask']
            acc, qT = it['A'], c_['qT']
            psc = self.ps_get()

            def fn(h):
                h.matmul(psc.t[0:P_, 0:T], lhsT=self.tri.r(P_, 0, P_), rhs=spt.r(P_, 0, T), start=True, stop=False)
                if not it['first']:
                    h.matmul(psc.t[0:P_, 0:T], lhsT=self.ident.r(P_, 0, P_), rhs=acc.r(P_, 0, T), start=False, stop=False)
                return h.matmul(psc.t[0:P_, 0:T], lhsT=it['k'][1], rhs=c_['nkT'].r(it['KQ'], 0, T), start=False, stop=True)
            self.op('pe', fn, r=[self.tri, spt, self.ident, it['k'][0], c_['nkT']] + ([] if it['first'] else [acc]), w=[psc])
            Pt = self.rget()
            if mask is None:
                self.op('act', lambda h: h.activation(out=Pt.r(P_, 0, T), in_=psc.t[0:P_, 0:T], func=AF.Exp, scale=-1.0), r=[psc], w=[Pt])
            else:
                self.op('act', lambda h: h.activation(out=e.f(P_, 0, T), in_=psc.t[0:P_, 0:T], func=AF.Exp, scale=-1.0), r=[psc], w=[e])
                self.op('dve', lambda h: h.tensor_tensor(out=Pt.r(P_, 0, T), in0=e.f(P_, 0, T), in1=mask.f(P_, 0, T), op=ALU.mult), r=[e, mask], w=[Pt])
            self.ps_put(psc)
            if acc is not None:
                self.rput(acc)
            self.fput(e)
            self.rput(spt)
            it['Pt'] = Pt

        def S3(it):
            hh = it['hh']
            c_ = H[hh]
            P_ = it['P']
            Pt = it['Pt']
            if c_['pso'] is None:
                c_['pso'] = self.ps_get()
            self.op('pe', lambda h: h.matmul(c_['pso'].t[0:128, 0:T], lhsT=it['v'][1], rhs=Pt.r(P_, 0, T), start=it['first'], stop=it['last']), r=[it['v'][0], Pt], w=[c_['pso']])
            self.rput(Pt)
            blk = it['blk']
            if blk[0] == 'p' and blk[2] == 0:
                pk_, pv_ = c_['grp'].pop(blk[1])
                self.zput(pk_)
                self.rput(pv_)
            if it['last']:
                head_post(hh)

        stages = [S1, S1b, S2, S3]
        nit = len(items)
        self.mark(f"ab{j} attn")
        for step in range(nit + (len(stages) - 1) * ATT_SKEW):
            for si_, st in enumerate(stages):
                idx = step - si_ * ATT_SKEW
                if 0 <= idx < nit:
                    st(items[idx])
        self.mark(f"ab{j} attn_end")
        self.rput(*hn)
        self.rput(*Vn)
        y = [self.fget() for _ in range(8)]
        for c in range(8):
            wo = self.ws_next(('abo', j, c))
            psy = self.linear(wo, cat_a + ob, T)
            self.evac(psy, y[c].f(128, 0, T), 128, T, y[c])
            self.ps_put(psy)
        self.rput(*cat_a)
        self.rput(*ob)
        self.postnorm_add(y, vcol_norm(i, 3, 0), False, T)
        self.fput(*y)

    def mixer_c(self, j, i, seg):
        T, col0, s0 = seg['T'], seg['col0'], seg['s0']
        grp = seg['grp']
        bs = min(128, T)
        nb = (T + 127) // 128
        has_state = not seg['first']
        gT = self.C['gT'][T]
        hn = self.rmsnorm(vcol_norm(i, 2, 0), T)
        rot = [self.fget() for _ in range(4)]
        for t_ in range(4):
            self.dma('sp', rot[t_].f(128, 0, T), self.c_rot[t_, :, col0:col0 + T], w=[rot[t_]])
        y = [self.fget() for _ in range(8)]
        def head_gen(hh):
                if has_state:
                    xit = self.fget()
                    self.dma('sp', xit.f(128, 0, T), self.c_xi[hh, :, 0:T], w=[xit])
                qr = []
                kr = []
                for which in range(2):
                    for cc in range(2):
                        wq = self.ws_next(('cq' if which == 0 else 'ck', j, hh, cc))
                        ps = self.linear(wq, hn, T)
                        raw = self.fget()
                        self.evac(ps, raw.f(128, 0, T), 128, T, raw)
                        self.ps_put(ps)
                        (qr if which == 0 else kr).append(raw)
                outs = []
                for which, (raws, cs_, sn_) in enumerate(((qr, rot[0], rot[1]), (kr, rot[2], rot[3]))):
                    x1, x2 = raws
                    t1 = self.fget()
                    t2 = self.fget()
                    o1 = self.rget()
                    o2 = self.rget()
                    self.op('dve', lambda h: h.tensor_tensor(out=t1.f(128, 0, T), in0=x1.f(128, 0, T), in1=cs_.f(128, 0, T), op=ALU.mult), r=[x1, cs_], w=[t1])
                    self.op('dve', lambda h: h.tensor_tensor(out=t2.f(128, 0, T), in0=x2.f(128, 0, T), in1=sn_.f(128, 0, T), op=ALU.mult), r=[x2, sn_], w=[t2])
                    self.op('dve', lambda h: h.tensor_tensor(out=o1.r(128, 0, T), in0=t1.f(128, 0, T), in1=t2.f(128, 0, T), op=ALU.subtract), r=[t1, t2], w=[o1])
                    self.op('dve', lambda h: h.tensor_tensor(out=t1.f(128, 0, T), in0=x2.f(128, 0, T), in1=cs_.f(128, 0, T), op=ALU.mult), r=[x2, cs_], w=[t1])
                    self.op('dve', lambda h: h.tensor_tensor(out=t2.f(128, 0, T), in0=x1.f(128, 0, T), in1=sn_.f(128, 0, T), op=ALU.mult), r=[x1, sn_], w=[t2])
                    self.op('dve', lambda h: h.tensor_tensor(out=o2.r(128, 0, T), in0=t1.f(128, 0, T), in1=t2.f(128, 0, T), op=ALU.add), r=[t1, t2], w=[o2])
                    self.fput(t1, t2, x1, x2)
                    outs.append([o1, o2])
                qrr, krr = outs
                Rt = [self.rget() for _ in range(2)]
                if has_state:
                    for cc in range(2):
                        src = self.o_R[j, 0, hh, cc] if seg['kind'] == 'p' else self.s_R[j, hh, cc]
                        self.dma('pool', Rt[cc].r(128, 0, 512), src, r=[self.hb_R[j][hh]], w=[Rt[cc]])
                    qx = [self.rget() for _ in range(2)]
                    for cc in range(2):
                        self.op('dve', lambda h: h.tensor_tensor(out=qx[cc].r(128, 0, T), in0=qrr[cc].f(128, 0, T), in1=xit.f(128, 0, T), op=ALU.mult), r=[qrr[cc], xit], w=[qx[cc]])
                    self.fput(xit)
                wv = [self.ws_next(('cv', j, hh, q)) for q in range(4)]
                V = []
                for blk in range(nb):
                    psv = self.ps_get()

                    def fn(h):
                        last = None
                        for k in range(8):
                            last = h.matmul(psv.t[0:bs, 0:512], lhsT=hn[k].r(128, blk * 128, blk * 128 + bs), rhs=wv[k // 2].r(128, (k % 2) * 512, (k % 2) * 512 + 512),
                                            start=(k == 0), stop=(k == 7))
                        return last
                    self.op('pe', fn, r=hn + wv, w=[psv])
                    vt = self.rget()
                    self.evac(psv, vt.r(bs, 0, 512), bs, 512, vt)
                    self.ps_put(psv)
                    V.append(vt)
                yield 'A'
                Dts = []
                for mb in range(nb):
                    Dt_ = self.fget()
                    self.dma('sp', Dt_.f(bs, 0, T), self.c_decayT[hh, mb * 128: mb * 128 + bs, 0:T], w=[Dt_])
                    Dts.append(Dt_)
                kz = []
                for blk in range(nb):
                    kzt = self.rget()
                    zc = hh * 5 + (blk if T == 512 else 4)
                    for cc in range(2):
                        pst = self.ps_get()
                        self.op('pe', lambda h: h.matmul(pst.t[0:bs, 0:128], lhsT=krr[cc].r(128, blk * 128, blk * 128 + bs), rhs=self.ident.r(128, 0, 128), start=True, stop=True),
                                r=[krr[cc], self.ident], w=[pst])
                        self.op('dve', lambda h: h.tensor_scalar(out=kzt.r(bs, cc * 128, cc * 128 + 128), in0=pst.t[0:bs, 0:128], scalar1=self.zeta.t[0:bs, zc:zc + 1], scalar2=None,
                                                                 op0=ALU.mult), r=[pst, self.zeta], w=[kzt])
                        self.ps_put(pst)
                    kz.append(kzt)
                pso = [self.ps_get() for _ in range(4)]
                Pts = {}

                def sc1(mb):
                    pss = self.ps_get()

                    def fn(h):
                        last = None
                        for cc in range(2):
                            last = h.matmul(pss.t[0:bs, 0:T], lhsT=krr[cc].r(128, mb * 128, mb * 128 + bs), rhs=qrr[cc].r(128, 0, T), start=(cc == 0), stop=(cc == 1))
                        return last
                    self.op('pe', fn, r=krr + qrr, w=[pss])
                    Dt = Dts[mb]
                    Pt = self.rget()
                    self.op('dve', lambda h: h.tensor_tensor(out=Pt.r(bs, 0, T), in0=pss.t[0:bs, 0:T], in1=Dt.f(bs, 0, T), op=ALU.mult), r=[pss, Dt], w=[Pt])
                    self.ps_put(pss)
                    self.fput(Dt)
                    Pts[mb] = Pt

                def sc2(mb):
                    Pt = Pts.pop(mb)
                    for dc in range(4):
                        self.op('pe', lambda h: h.matmul(pso[dc].t[0:128, 0:T], lhsT=V[mb].r(bs, dc * 128, dc * 128 + 128), rhs=Pt.r(bs, 0, T),
                                                         start=(mb == 0), stop=(mb == nb - 1 and not has_state)), r=[V[mb], Pt], w=[pso[dc]])
                    self.rput(Pt)
                sc1(0)
                for mb in range(1, nb):
                    sc1(mb)
                    sc2(mb - 1)
                sc2(nb - 1)
                if has_state:
                    for dc in range(4):
                        def fn(h):
                            last = None
                            for cc in range(2):
                                last = h.matmul(pso[dc].t[0:128, 0:T], lhsT=Rt[cc].r(128, dc * 128, dc * 128 + 128), rhs=qx[cc].r(128, 0, T), start=False, stop=(cc == 1))
                            return last
                        self.op('pe', fn, r=Rt + qx, w=[pso[dc]])
                    self.rput(*qx)
                for cc in range(2):
                    psr = self.ps_get()

                    def fn(h):
                        last = None
                        for mb in range(nb):
                            last = h.matmul(psr.t[0:128, 0:512], lhsT=kz[mb].r(bs, cc * 128, cc * 128 + 128), rhs=V[mb].r(bs, 0, 512), start=(mb == 0), stop=(mb == nb - 1))
                        return last
                    self.op('pe', fn, r=kz + V, w=[psr])
                    Rn = self.fget()
                    if has_state:
                        self.op('dve', lambda h: h.scalar_tensor_tensor(out=Rn.f(128, 0, 512), in0=Rt[cc].f(128, 0, 512), scalar=gT[hh], in1=psr.t[0:128, 0:512],
                                                                       op0=ALU.mult, op1=ALU.add), r=[Rt[cc], psr], w=[Rn])
                    else:
                        self.op('dve', lambda h: h.tensor_copy(out=Rn.f(128, 0, 512), in_=psr.t[0:128, 0:512]), r=[psr], w=[Rn])
                    self.ps_put(psr)
                    self.dma('sp', self.o_R[j, grp, hh, cc], Rn.f(128, 0, 512), r=[Rn], w=[self.hb_R[j][hh]])
                    self.fput(Rn)
                self.rput(*Rt)
                self.rput(*kz)
                self.rput(*V)
                self.rput(*qrr)
                self.rput(*krr)
                osb = [self.rget() for _ in range(4)]
                osq = [self.rget() for _ in range(4)]
                for dc in range(4):
                    self.op('act', lambda h: h.activation(out=osb[dc].r(128, 0, T), in_=pso[dc].t[0:128, 0:T], func=AF.Copy), r=[pso[dc]], w=[osb[dc]])
                    self.op('act', lambda h: h.activation(out=osq[dc].r(128, 0, T), in_=pso[dc].t[0:128, 0:T], func=AF.Square), r=[pso[dc]], w=[osq[dc]])
                    self.ps_put(pso[dc])
                yield 'Ba'
                ps1 = self.ps_get()
                ps2 = self.ps_get()

                def fn1(h):
                    last = None
                    for dc in range(4):
                        last = h.matmul(ps1.t[0:128, 0:T], lhsT=self.ones.r(128, 0, 128), rhs=osb[dc].r(128, 0, T), start=(dc == 0), stop=(dc == 3))
                    return last

                def fn2(h):
                    last = None
                    for dc in range(4):
                        last = h.matmul(ps2.t[0:128, 0:T], lhsT=self.ones.r(128, 0, 128), rhs=osq[dc].r(128, 0, T), start=(dc == 0), stop=(dc == 3))
                    return last
                self.op('pe', fn1, r=[self.ones] + osb, w=[ps1])
                self.op('pe', fn2, r=[self.ones] + osq, w=[ps2])
                self.rput(*osq)
                mean = self.fget()
                var = self.fget()
                self.op('act', lambda h: h.activation(out=mean.f(128, 0, T), in_=ps1.t[0:128, 0:T], func=AF.Copy, scale=1.0 / 512), r=[ps1], w=[mean])
                self.op('act', lambda h: h.activation(out=var.f(128, 0, T), in_=ps1.t[0:128, 0:T], func=AF.Square, scale=1.0 / 512), r=[ps1], w=[var])
                self.ps_put(ps1)
                self.op('dve', lambda h: h.scalar_tensor_tensor(out=var.f(128, 0, T), in0=ps2.t[0:128, 0:T], scalar=1.0 / 512, in1=var.f(128, 0, T), op0=ALU.mult, op1=ALU.subtract),
                        r=[ps2, var], w=[var])
                self.ps_put(ps2)
                self.op('act', lambda h: h.activation(out=var.f(128, 0, T), in_=var.f(128, 0, T), func=AF.Ln, bias=self.epsc.t[0:128, 0:1]), r=[var, self.epsc], w=[var])
                self.op('act', lambda h: h.activation(out=var.f(128, 0, T), in_=var.f(128, 0, T), func=AF.Exp, scale=-0.5), r=[var], w=[var])
                go = []
                for dc in range(4):
                    wg = self.ws_next(('cg', j, hh, dc))
                    psg = self.linear(wg, hn, T)
                    sg = self.fget()
                    self.op('act', lambda h: h.activation(out=sg.f(128, 0, T), in_=psg.t[0:128, 0:T], func=AF.Silu), r=[psg], w=[sg])
                    self.ps_put(psg)
                    t = self.fget()
                    self.op('dve', lambda h: h.tensor_tensor(out=t.f(128, 0, T), in0=osb[dc].f(128, 0, T), in1=mean.f(128, 0, T), op=ALU.subtract), r=[osb[dc], mean], w=[t])
                    self.op('dve', lambda h: h.tensor_tensor(out=t.f(128, 0, T), in0=t.f(128, 0, T), in1=var.f(128, 0, T), op=ALU.mult), r=[t, var], w=[t])
                    g_ = self.rget()
                    self.op('dve', lambda h: h.tensor_tensor(out=g_.r(128, 0, T), in0=t.f(128, 0, T), in1=sg.f(128, 0, T), op=ALU.mult), r=[t, sg], w=[g_])
                    self.fput(sg, t)
                    go.append(g_)
                self.rput(*osb)
                self.fput(mean, var)
                for c in range(8):
                    wo = self.ws_next(('co', j, hh, c))
                    psy = self.linear(wo, go, T)
                    if hh == 0:
                        self.evac(psy, y[c].f(128, 0, T), 128, T, y[c])
                    else:
                        self.op('dve', lambda h: h.tensor_tensor(out=y[c].f(128, 0, T), in0=psy.t[0:128, 0:T], in1=y[c].f(128, 0, T), op=ALU.add), r=[psy, y[c]], w=[y[c]])
                    self.ps_put(psy)
                self.rput(*go)

        gens = [head_gen(h_) for h_ in range(4)]
        next(gens[0])
        for h_ in range(4):
            next(gens[h_])
            if h_ + 1 < 4:
                next(gens[h_ + 1])
            for _ in gens[h_]:
                pass
        self.rput(*hn)
        self.fput(*rot)
        self.postnorm_add(y, vcol_norm(i, 3, 0), False, T)
        self.fput(*y)


_CACHE = {}


def host_inputs(I, b, wpack, vec, C):
    xT = np.concatenate([I['x_prompt'][b].T, I['x_sample'][b].T], axis=1).reshape(8, 128, NCOL)
    pT = np.concatenate([I['p_prompt'][:, b].transpose(0, 2, 1), I['p_sample'][:, b].transpose(0, 2, 1)], axis=2).reshape(DEPTH, 2, 128, NCOL)
    m = {
        'wpack': wpack, 'lrupack': np.stack([piece_array(('ablru', j), I) for j in range(2)]), 'xT': np.ascontiguousarray(xT), 'pT': np.ascontiguousarray(pT), 'vec': vec,
        'c_ones': C['ones'], 'c_ident': C['ident'], 'c_tri': C['tri'], 'c_masks': C['masks'], 'c_rot': C['rot'],
        'c_decayT': C['decayT'], 'c_xi': C['xi'], 'c_zeta': C['zeta'],
        's_lruh': np.ascontiguousarray(I['state_lru_h'][:, b].reshape(2, 4, 128).transpose(0, 2, 1)),
        's_conv': np.ascontiguousarray(I['state_conv'][:, b].reshape(2, 3, 4, 128).transpose(0, 3, 2, 1).reshape(2, 128, 12)),
        's_kT': np.ascontiguousarray(I['cache_sb_k'][:, b].transpose(0, 2, 3, 1)),
        's_v': np.ascontiguousarray(I['cache_sb_v'][:, b].reshape(2, SEQ, 512)),
        's_R': np.ascontiguousarray(I['state_ret'][:, b].reshape(2, 4, 2, 128, 512)),
    }
    return m


def run(I, depth=DEPTH, en_ab=True, en_c=True, ncores=8, trace=False):
    key = (depth, en_ab, en_c)
    if key not in _CACHE:
        _CACHE[key] = Builder(depth, en_ab, en_c)
    B = _CACHE[key]
    I = {k: np.asarray(v) for k, v in I.items()}
    wpack = np.stack([piece_array(s, I) for s in B.specs])
    vec = build_vec(I)
    in_maps = [host_inputs(I, b, wpack, vec, B.C) for b in range(ncores)]
    res = run_bass_kernel_spmd(B.nc, in_maps, core_ids=list(range(ncores)), trace=trace)
    return res


def assemble(results, nb=8):
    R = results
    y = np.stack([r['o_y'].reshape(D, NCOL).T for r in R])
    y_p, y_s = y[:, :SEQ], y[:, SEQ:]
    lruh = np.stack([r['o_lruh'] for r in R])
    h_all = lruh.transpose(1, 2, 0, 4, 3).reshape(2, 2, nb, 512)
    conv = np.stack([r['o_conv'] for r in R]).reshape(nb, 2, 2, 128, 4, 3)
    conv = conv.transpose(1, 2, 0, 5, 4, 3).reshape(2, 2, nb, 3, 512)
    kT = np.stack([r['o_kT'] for r in R])
    k = kT.transpose(1, 0, 4, 2, 3)
    v = np.stack([r['o_v'] for r in R]).reshape(nb, 2, NCOL, 8, 64).transpose(1, 0, 2, 3, 4)
    Rr = np.stack([r['o_R'] for r in R]).reshape(nb, 2, 2, 4, 256, 512)
    Rr = Rr.transpose(1, 2, 0, 3, 4, 5)
    c = np.ascontiguousarray
    return (c(y_p), c(y_s), c(h_all[:, 0]), c(conv[:, 0]), c(k[:, :, :SEQ]), c(v[:, :, :SEQ]), c(Rr[:, 0]),
            c(h_all[:, 1]), c(conv[:, 1]), c(k[:, :, SEQ:]), c(v[:, :, SEQ:]), c(Rr[:, 1]))


def kernel(**inputs):
    res = run(inputs)
    return assemble(res.results)
```

```python
import numpy as np
from contextlib import ExitStack
import concourse.bass as bass
import concourse.mybir as mybir
from concourse.bass_utils import run_bass_kernel_spmd

F32 = mybir.dt.float32
F32R = mybir.dt.float32r
AF = mybir.ActivationFunctionType
ALU = mybir.AluOpType

D = 1024
DEPTH = 4
SEQ = 2048
DSEQ = 64
NCOL = SEQ + DSEQ
DFF = 2816
NFC = 22
EPS = 1e-6
SAME_ENGINE_SYNC = True
NSLOT = 8
PE_DRAIN = True
ATT_SKEW = 1
NRT = 36
NZT = 10
NFT = 22
PREFETCH = 4
SEG_T = 512


def pass_specs(depth=DEPTH, en_ab=True, en_c=True):
    sp = []
    for i in range(depth):
        j = i // 2
        for w in range(2):
            if w == 1:
                if i % 2 == 0 and en_ab:
                    for q in range(4):
                        sp.append(('abv', j, q))
                    for c in range(4):
                        sp.append(('abxa', j, c))
                        sp.append(('abga', j, c))
                    for h in range(8):
                        sp.append(('abqk', j, h))
                    for c in range(8):
                        sp.append(('abo', j, c))
                if i % 2 == 1 and en_c:
                    def _A(h):
                        for cc in range(2):
                            sp.append(('cq', j, h, cc))
                        for cc in range(2):
                            sp.append(('ck', j, h, cc))
                        for q in range(4):
                            sp.append(('cv', j, h, q))
                    _A(0)
                    for h in range(4):
                        if h + 1 < 4:
                            _A(h + 1)
                        for dc in range(4):
                            sp.append(('cg', j, h, dc))
                        for c in range(8):
                            sp.append(('co', j, h, c))
            for f in range(NFC):
                sp.append(('gate', i, w, f))
                sp.append(('up', i, w, f))
            for c in range(8):
                for part in range(3):
                    sp.append(('down', i, w, c, part))
        for c in range(8):
            sp.append(('pleg', i, c))
            sp.append(('plep', i, c))
    return sp


def kpiece(W, r0, nk, c0, ncol):
    a = W[r0:r0 + nk * 128, c0:c0 + ncol].reshape(nk, 128, ncol).transpose(1, 0, 2).reshape(128, nk * ncol)
    return a


def piece_array(spec, I):
    out = np.zeros((128, 1024), np.float32)
    k = spec[0]
    if k in ('gate', 'up'):
        W = I['ffn_w_gate' if k == 'gate' else 'ffn_w_up'][spec[1], spec[2]]
        a = kpiece(W, 0, 8, spec[3] * 128, 128)
    elif k == 'down':
        W = I['ffn_w_down'][spec[1], spec[2]]
        f0 = spec[4] * 8
        nf = min(8, NFC - f0)
        a = kpiece(W, f0 * 128, nf, spec[3] * 128, 128)
    elif k == 'pleg':
        a = kpiece(I['ple_w_gate'][spec[1]], 0, 8, spec[2] * 128, 128)
    elif k == 'plep':
        a = kpiece(I['ple_w_in'][spec[1]], 0, 2, spec[2] * 128, 128)
    elif k == 'abv':
        a = kpiece(I['ab_w_in'][spec[1]], spec[2] * 256, 2, 2048, 512)
    elif k == 'ablru':
        j = spec[1]
        a = np.zeros((128, 1024), np.float32)
        for gi, nm in enumerate(('lru_w_r', 'lru_w_i')):
            for c in range(4):
                for hh in range(2):
                    a[hh * 64:(hh + 1) * 64, gi * 512 + c * 128 + hh * 64: gi * 512 + c * 128 + (hh + 1) * 64] = I[nm][j, 2 * c + hh]
    elif k == 'abxa':
        a = kpiece(I['ab_w_in'][spec[1]], 0, 8, spec[2] * 128, 128)
    elif k == 'abga':
        a = kpiece(I['ab_w_in'][spec[1]], 0, 8, 512 + spec[2] * 128, 128)
    elif k == 'abqk':
        W = I['ab_w_in'][spec[1]]
        a = np.concatenate([kpiece(W, 0, 8, 1024 + spec[2] * 64, 64), kpiece(W, 0, 8, 1536 + spec[2] * 64, 64)], axis=1)
    elif k == 'abo':
        a = kpiece(I['ab_w_out'][spec[1]], 0, 8, spec[2] * 128, 128)
    elif k == 'cq':
        a = kpiece(I['ret_w_in'][spec[1]], 0, 8, spec[2] * 256 + spec[3] * 128, 128)
    elif k == 'ck':
        a = kpiece(I['ret_w_in'][spec[1]], 0, 8, 1024 + spec[2] * 256 + spec[3] * 128, 128)
    elif k == 'cv':
        a = kpiece(I['ret_w_in'][spec[1]], spec[3] * 256, 2, 2048 + spec[2] * 512, 512)
    elif k == 'cg':
        a = kpiece(I['ret_w_in'][spec[1]], 0, 8, 4096 + spec[2] * 512 + spec[3] * 128, 128)
    elif k == 'co':
        a = kpiece(I['ret_w_out'][spec[1]], spec[2] * 512, 4, spec[3] * 128, 128)
    else:
        raise ValueError(spec)
    out[:a.shape[0], :a.shape[1]] = a
    return out


def piece_cols(spec):
    k = spec[0]
    if k == 'down':
        return min(8, NFC - spec[4] * 8) * 128
    if k == 'plep':
        return 256
    if k in ('co',):
        return 512
    return 1024


def vcol_norm(i, n, c):
    return (i * 8 + n) * 8 + c
VC_CONVW = 256
VC_CONVB = 288
VC_BR = 296
VC_BI = 304
VC_LAM = 312
NVEC = 320


def build_vec(I):
    v = np.zeros((128, NVEC), np.float32)
    ng = I['norm_g']
    for i in range(DEPTH):
        for n in range(8):
            v[:, vcol_norm(i, n, 0):vcol_norm(i, n, 0) + 8] = ng[i, n].reshape(8, 128).T
    for j in range(2):
        for tap in range(4):
            v[:, VC_CONVW + (j * 4 + tap) * 4: VC_CONVW + (j * 4 + tap) * 4 + 4] = I['lru_conv_w'][j, tap].reshape(4, 128).T
        v[:, VC_CONVB + j * 4: VC_CONVB + j * 4 + 4] = I['lru_conv_b'][j].reshape(4, 128).T
        v[:, VC_BR + j * 4: VC_BR + j * 4 + 4] = I['lru_b_r'][j].reshape(4, 128).T
        v[:, VC_BI + j * 4: VC_BI + j * 4 + 4] = I['lru_b_i'][j].reshape(4, 128).T
        v[:, VC_LAM + j * 4: VC_LAM + j * 4 + 4] = I['lru_lambda'][j].reshape(4, 128).T
    return v


def build_consts():
    C = {}
    C['ones'] = np.ones((128, 128), np.float32)
    C['ident'] = np.eye(128, dtype=np.float32)
    jj = np.arange(128)
    C['tri'] = (jj[:, None] >= jj[None, :]).astype(np.float32)
    q = np.arange(512)
    C['masks'] = np.stack([((128 * kb + jj)[:, None] < q[None, :]).astype(np.float32) for kb in range(4)])
    half = 128
    freq = (np.float32(10000.0) ** (-np.arange(half, dtype=np.float32) / np.float32(half))).astype(np.float32)
    pos = np.arange(NCOL, dtype=np.float32)
    ang = (pos[None, :] * freq[:, None]).astype(np.float32)
    cs = np.cos(ang).astype(np.float32)
    sn = np.sin(ang).astype(np.float32)
    C['rot'] = np.stack([cs, sn, cs * np.float32(1.0 / 16), sn * np.float32(1.0 / 16)]).astype(np.float32)
    log_g = np.log(np.float32(1.0) - np.float32(2.0) ** (-5.0 - np.arange(4, dtype=np.float32))).astype(np.float32)
    m = np.arange(512)
    l = np.arange(512)
    dT = np.zeros((4, 512, 512), np.float32)
    cm = m[:, None] // 64
    cl = l[None, :] // 64
    for h in range(4):
        same = np.exp(log_g[h] * np.abs(l[None, :] - m[:, None]).astype(np.float32))
        prev = np.exp(log_g[h] * (l[None, :] - m[:, None]).astype(np.float32))
        dT[h] = np.where(cm == cl, same, np.where(cm < cl, prev, 0.0))
    C['decayT'] = dT.astype(np.float32)
    xi = np.stack([np.exp(log_g[h] * (l.astype(np.float32) + 1.0)) for h in range(4)]).astype(np.float32)
    C['xi'] = np.broadcast_to(xi[:, None, :], (4, 128, 512)).copy()
    z = np.zeros((128, 20), np.float32)
    for h in range(4):
        for blk in range(4):
            z[:, h * 5 + blk] = np.exp(log_g[h] * (511.0 - (blk * 128 + jj)).astype(np.float32))
        z[:64, h * 5 + 4] = np.exp(log_g[h] * (63.0 - jj[:64]).astype(np.float32))
    C['zeta'] = z
    C['gT'] = {512: [float(np.exp(log_g[h] * np.float32(512.0))) for h in range(4)],
               64: [float(np.exp(log_g[h] * np.float32(64.0))) for h in range(4)]}
    return C


class Buf:
    __slots__ = ('w', 'r', 'wm')

    def __init__(self, multi=False):
        self.w = None
        self.r = {}
        self.wm = {} if multi else None


class Tile:
    def __init__(self, t, shape):
        self.t = t
        self.b = Buf()
        self.shape = shape

    def f(self, p=None, a=0, b=None):
        p = self.shape[0] if p is None else p
        b = self.shape[1] if b is None else b
        return self.t[0:p, a:b]

    def r(self, p=None, a=0, b=None):
        return self.f(p, a, b).bitcast(F32R)


def _cls(n):
    return 32 if n <= 32 else (64 if n <= 64 else 128)


class PEProxy:
    def __init__(self, h):
        self.h = h
        self.last = None
        self.ndrain = 0

    def matmul(self, out, lhsT, rhs, **kw):
        c = (_cls(lhsT.shape[0]), _cls(lhsT.shape[-1]))
        if PE_DRAIN and self.last is not None and c != self.last:
            self.h.drain()
            self.ndrain += 1
        self.last = c
        return self.h.matmul(out, lhsT=lhsT, rhs=rhs, **kw)

    def wait_ge(self, *a, **k):
        return self.h.wait_ge(*a, **k)


class Eng:
    def __init__(self, name, h, sem):
        self.name = name
        self.h = h
        self.sem = sem
        self.cnt = 0
        self.seen = {}


class Builder:
    def __init__(self, depth=DEPTH, en_ab=True, en_c=True, segs=None):
        self.depth = depth
        self.en_ab = en_ab
        self.en_c = en_c
        self.C = build_consts()
        self.specs = pass_specs(depth, en_ab, en_c)
        self.npp = len(self.specs)
        if segs is None:
            segs = [dict(kind='p', s0=s, T=SEG_T, col0=s) for s in range(0, SEQ, SEG_T)] + [dict(kind='s', s0=SEQ, T=DSEQ, col0=SEQ)]
        self.segs = segs
        self.nc = bass.Bass("TRN2", target_bir_lowering=False)
        self.es = ExitStack()
        self.build()

    def dram_in(self, name, shape):
        return self.nc.dram_tensor(name, list(shape), F32, kind="ExternalInput").ap()

    def dram_out(self, name, shape):
        return self.nc.dram_tensor(name, list(shape), F32, kind="ExternalOutput").ap()

    def sb(self, name, shape):
        t = self.es.enter_context(self.nc.sbuf_tensor("sb_" + name, list(shape), F32))
        return Tile(t, shape)

    def setup(self):
        nc, es = self.nc, self.es
        self.E = {}
        self.sem = {}
        for name, h in (('pe', PEProxy(nc.tensor)), ('act', nc.scalar), ('dve', nc.vector), ('pool', nc.gpsimd), ('sp', nc.sync)):
            s = es.enter_context(nc.semaphore("s_" + name))
            self.E[name] = Eng(name, h, s)
            self.sem[name] = s
        self.dring = {}
        for q, n in (('pool', 12), ('sp', 24)):
            sems = []
            for i in range(n):
                s = es.enter_context(nc.semaphore(f"d_{q}{i}"))
                self.sem[(q, i)] = s
                sems.append(s)
            self.dring[q] = dict(n=0, val=[0] * n, size=n)
        self.psb = []
        for i in range(8):
            t = es.enter_context(nc.psum_tensor(f"ps{i}", [128, 512], F32))
            self.psb.append(Tile(t, [128, 512]))
        self.ps_free = list(self.psb)
        self.rt_free = [self.sb(f"rt{i}", [128, 512]) for i in range(NRT)]
        self.ft_free = [self.sb(f"ft{i}", [128, 512]) for i in range(NFT)]
        self.zt_free = [self.sb(f"zt{i}", [128, 512]) for i in range(NZT)]
        self.slots = [self.sb(f"ws{i}", [128, 1024]) for i in range(NSLOT)]

    def mark(self, label):
        if not hasattr(self, 'marks'):
            self.marks = []
        self.marks.append((label, self.E['dve'].cnt))

    def ps_get(self):
        assert self.ps_free, "out of PSUM banks"
        return self.ps_free.pop(0)

    def ps_put(self, p):
        self.ps_free.append(p)

    def rget(self):
        assert self.rt_free, "out of R tiles"
        return self.rt_free.pop(0)

    def rput(self, *ts):
        for t in ts:
            self.rt_free.append(t)

    def zget(self):
        assert self.zt_free, "out of Z tiles"
        return self.zt_free.pop(0)

    def zput(self, *ts):
        for t in ts:
            self.zt_free.append(t)

    def fget(self):
        assert self.ft_free, "out of F tiles"
        return self.ft_free.pop(0)

    def fput(self, *ts):
        for t in ts:
            self.ft_free.append(t)

    def _waits(self, eng, r, w):
        need = {}
        for b in list(r) + list(w):
            if b.wm:
                for k, c in b.wm.items():
                    if need.get(k, 0) < c:
                        need[k] = c
        for b in r:
            if b.w is not None:
                k, c = b.w
                if need.get(k, 0) < c:
                    need[k] = c
        for b in w:
            if b.w is not None:
                k, c = b.w
                if need.get(k, 0) < c:
                    need[k] = c
            for k, c in b.r.items():
                if need.get(k, 0) < c:
                    need[k] = c
        for k, c in need.items():
            if k == eng.name and (not SAME_ENGINE_SYNC or k == 'pe'):
                continue
            if eng.seen.get(k, 0) < c:
                eng.h.wait_ge(self.sem[k], c)
                eng.seen[k] = c

    def op(self, e, fn, r=(), w=()):
        eng = self.E[e]
        r = [x.b if isinstance(x, Tile) else x for x in r]
        w = [x.b if isinstance(x, Tile) else x for x in w]
        self._waits(eng, r, w)
        ins = fn(eng.h)
        eng.cnt += 1
        ins.then_inc(eng.sem, 1)
        for b in r:
            if b.r.get(e, 0) < eng.cnt:
                b.r[e] = eng.cnt
        for b in w:
            b.w = (e, eng.cnt)
            b.r = {}
        return ins

    def dma(self, q, out_ap, in_ap, r=(), w=()):
        eng = self.E[q]
        ring = self.dring[q]
        idx = ring['n'] % ring['size']
        ring['n'] += 1
        key = (q, idx)
        prev = ring['val'][idx]
        if prev > 0 and eng.seen.get(key, 0) < prev:
            eng.h.wait_ge(self.sem[key], prev)
            eng.seen[key] = prev
        r = [x.b if isinstance(x, Tile) else x for x in r]
        w = [x.b if isinstance(x, Tile) else x for x in w]
        self._waits(eng, r, w)
        ins = eng.h.dma_start(out=out_ap, in_=in_ap)
        val = prev + 16
        ins.then_inc(self.sem[key], 16)
        ring['val'][idx] = val
        for b in r:
            b.r[key] = val
        for b in w:
            if b.wm is not None:
                b.wm[key] = val
            else:
                b.w = (key, val)
                b.r = {}

    def ws_issue(self, gidx):
        pidx = gidx % self.npp
        spec = self.specs[pidx]
        slot = self.slots[gidx % NSLOT]
        ncol = piece_cols(spec)
        self.dma('pool', slot.r(128, 0, ncol), self.wpack[pidx, :, 0:ncol], r=(), w=[slot])

    def ws_next(self, spec):
        g = self.ws_pos
        assert self.specs[g % self.npp] == spec, (self.specs[g % self.npp], spec)
        while self.ws_issued < min(g + PREFETCH, self.ws_total):
            self.ws_issue(self.ws_issued)
            self.ws_issued += 1
        self.ws_pos += 1
        return self.slots[g % NSLOT]

    def act_warm(self):
        self.op('act', lambda h: h.activation(out=self.warm.f(128, 0, 1), in_=self.epsc.t[0:128, 0:1], func=AF.Ln), r=[self.epsc], w=[self.warm])

    def vc(self, col, p=128):
        return self.vec.t[0:p, col:col + 1]

    def linear(self, slot, ins, T, kw=128, M=128, coff=0, ps=None, start=True, stop=True, extra_r=(), per_k=False):
        if ps is None:
            ps = self.ps_get()
        n = len(ins)
        if per_k:
            for k in range(n):
                self.op('pe', lambda h: h.matmul(ps.t[0:M, 0:T], lhsT=slot.r(128, coff + k * kw, coff + k * kw + M), rhs=ins[k].r(128, 0, T),
                                                 start=(start and k == 0), stop=(stop and k == n - 1)), r=[slot, ins[k]], w=[ps])
            return ps

        def fn(h):
            last = None
            for k in range(n):
                last = h.matmul(ps.t[0:M, 0:T], lhsT=slot.r(128, coff + k * kw, coff + k * kw + M), rhs=ins[k].r(128, 0, T),
                                start=(start and k == 0), stop=(stop and k == n - 1))
            return last
        self.op('pe', fn, r=[slot] + list(ins) + list(extra_r), w=[ps])
        return ps

    def evac(self, ps, dst_ap, P, T, dst_tile, eng='act', scale=None):
        if eng == 'act':
            if scale is None:
                self.op('act', lambda h: h.activation(out=dst_ap, in_=ps.t[0:P, 0:T], func=AF.Copy), r=[ps], w=[dst_tile])
            else:
                self.op('act', lambda h: h.activation(out=dst_ap, in_=ps.t[0:P, 0:T], func=AF.Copy, scale=scale), r=[ps], w=[dst_tile])
        else:
            self.op('dve', lambda h: h.tensor_copy(out=dst_ap, in_=ps.t[0:P, 0:T]), r=[ps], w=[dst_tile])

    def rstd_from(self, tiles, T, nfeat, P=128):
        ps = self.ps_get()
        n = len(tiles)
        for c in range(n):
            sq = self.rget()
            self.op('act', lambda h: h.activation(out=sq.r(P, 0, T), in_=tiles[c].f(P, 0, T), func=AF.Square), r=[tiles[c]], w=[sq])
            self.op('pe', lambda h: h.matmul(ps.t[0:128, 0:T], lhsT=self.ones.r(P, 0, 128), rhs=sq.r(P, 0, T), start=(c == 0), stop=(c == n - 1)),
                    r=[sq, self.ones], w=[ps])
            self.rput(sq)
        rstd = self.fget()
        self.op('act', lambda h: h.activation(out=rstd.f(128, 0, T), in_=ps.t[0:128, 0:T], func=AF.Ln, scale=1.0 / nfeat, bias=self.epsc.t[0:128, 0:1]),
                r=[ps, self.epsc], w=[rstd])
        self.ps_put(ps)
        self.op('act', lambda h: h.activation(out=rstd.f(128, 0, T), in_=rstd.f(128, 0, T), func=AF.Exp, scale=-0.5), r=[rstd], w=[rstd])
        return rstd

    def rmsnorm(self, gcol, T):
        rstd = self.rstd_from(self.x, T, D)
        hn = [self.rget() for _ in range(8)]
        for c in range(8):
            self.op('dve', lambda h: h.scalar_tensor_tensor(out=hn[c].r(128, 0, T), in0=self.x[c].f(128, 0, T), scalar=self.vc(gcol + c),
                                                           in1=rstd.f(128, 0, T), op0=ALU.mult, op1=ALU.mult),
                    r=[self.x[c], rstd, self.vec], w=[hn[c]])
        self.fput(rstd)
        return hn

    def postnorm_add(self, y, gcol, half, T):
        rstd = self.rstd_from(y, T, D)
        vt = self.vech if half else self.vec
        for c in range(8):
            self.op('dve', lambda h: h.scalar_tensor_tensor(out=y[c].f(128, 0, T), in0=y[c].f(128, 0, T), scalar=vt.t[0:128, gcol + c:gcol + c + 1],
                                                           in1=rstd.f(128, 0, T), op0=ALU.mult, op1=ALU.mult),
                    r=[y[c], rstd, vt], w=[y[c]])
            self.op('dve', lambda h: h.tensor_tensor(out=self.x[c].f(128, 0, T), in0=self.x[c].f(128, 0, T), in1=y[c].f(128, 0, T), op=ALU.add),
                    r=[self.x[c], y[c]], w=[self.x[c]])
        self.fput(rstd)

    def ffn(self, i, w, T):
        hn = self.rmsnorm(vcol_norm(i, 0 if w == 0 else 4, 0), T)
        hh = []
        for f in range(NFC):
            wg = self.ws_next(('gate', i, w, f))
            wu = self.ws_next(('up', i, w, f))
            psg = self.linear(wg, hn, T, per_k=(f == 0))
            psu = self.linear(wu, hn, T)
            s = self.fget()
            self.op('act', lambda h: h.activation(out=s.f(128, 0, T), in_=psg.t[0:128, 0:T], func=AF.Silu), r=[psg], w=[s])
            self.ps_put(psg)
            ht = self.rget()
            self.op('dve', lambda h: h.tensor_tensor(out=ht.r(128, 0, T), in0=psu.t[0:128, 0:T], in1=s.f(128, 0, T), op=ALU.mult), r=[psu, s], w=[ht])
            self.ps_put(psu)
            self.fput(s)
            hh.append(ht)
        self.rput(*hn)
        self.act_warm()
        y = [self.fget() for _ in range(8)]
        for c in range(8):
            psy = self.ps_get()
            for part in range(3):
                wd = self.ws_next(('down', i, w, c, part))
                f0 = part * 8
                f1 = min(NFC, f0 + 8)
                self.linear(wd, hh[f0:f1], T, ps=psy, start=(part == 0), stop=(part == 2))
            self.evac(psy, y[c].f(128, 0, T), 128, T, y[c])
            self.ps_put(psy)
        self.rput(*hh)
        self.postnorm_add(y, vcol_norm(i, 1 if w == 0 else 5, 0), True, T)
        self.fput(*y)

    def ple(self, i, seg):
        T, col0 = seg['T'], seg['col0']
        hn = self.rmsnorm(vcol_norm(i, 6, 0), T)
        pt = [self.rget() for _ in range(2)]
        for k in range(2):
            self.dma('pool', pt[k].r(128, 0, T), self.pT[i, k, :, col0:col0 + T], w=[pt[k]])
        z = [self.fget() for _ in range(8)]
        for c in range(8):
            wg = self.ws_next(('pleg', i, c))
            wp = self.ws_next(('plep', i, c))
            psg = self.linear(wg, hn, T)
            psp = self.linear(wp, pt, T)
            sg = self.fget()
            self.op('act', lambda h: h.activation(out=sg.f(128, 0, T), in_=psg.t[0:128, 0:T], func=AF.Sigmoid), r=[psg], w=[sg])
            self.ps_put(psg)
            self.op('dve', lambda h: h.tensor_tensor(out=z[c].f(128, 0, T), in0=psp.t[0:128, 0:T], in1=sg.f(128, 0, T), op=ALU.mult), r=[psp, sg], w=[z[c]])
            self.ps_put(psp)
            self.fput(sg)
        self.rput(*hn)
        self.rput(*pt)
        self.act_warm()
        self.postnorm_add(z, vcol_norm(i, 7, 0), False, T)
        self.fput(*z)

    def build(self):
        nc = self.nc
        self.wpack = self.dram_in("wpack", [self.npp, 128, 1024])
        self.xT = self.dram_in("xT", [8, 128, NCOL])
        self.pT = self.dram_in("pT", [DEPTH, 2, 128, NCOL])
        self.vec_d = self.dram_in("vec", [128, NVEC])
        self.lrupack = self.dram_in("lrupack", [2, 128, 1024])
        self.c_ones = self.dram_in("c_ones", [128, 128])
        self.c_ident = self.dram_in("c_ident", [128, 128])
        self.c_tri = self.dram_in("c_tri", [128, 128])
        self.c_masks = self.dram_in("c_masks", [4, 128, 512])
        self.c_rot = self.dram_in("c_rot", [4, 128, NCOL])
        self.c_decayT = self.dram_in("c_decayT", [4, 512, 512])
        self.c_xi = self.dram_in("c_xi", [4, 128, 512])
        self.c_zeta = self.dram_in("c_zeta", [128, 20])
        self.s_lruh = self.dram_in("s_lruh", [2, 128, 4])
        self.s_conv = self.dram_in("s_conv", [2, 128, 12])
        self.s_kT = self.dram_in("s_kT", [2, 8, 64, SEQ])
        self.s_v = self.dram_in("s_v", [2, SEQ, 512])
        self.s_R = self.dram_in("s_R", [2, 4, 2, 128, 512])
        self.o_y = self.dram_out("o_y", [8, 128, NCOL])
        self.o_lruh = self.dram_out("o_lruh", [2, 2, 128, 4])
        self.o_conv = self.dram_out("o_conv", [2, 2, 128, 12])
        self.o_kT = self.dram_out("o_kT", [2, 8, 64, NCOL])
        self.o_v = self.dram_out("o_v", [2, NCOL, 512])
        self.o_R = self.dram_out("o_R", [2, 2, 4, 2, 128, 512])
        self.setup()
        self.x = [self.sb(f"x{c}", [128, 512]) for c in range(8)]
        self.vec = self.sb("vec", [128, NVEC])
        self.vech = self.sb("vech", [128, NVEC])
        self.ones = self.sb("ones", [128, 128])
        self.ident = self.sb("ident", [128, 128])
        self.tri = self.sb("tri", [128, 128])
        self.masks = [self.sb(f"mask{k}", [128, 512]) for k in range(4)]
        self.zeta = self.sb("zeta", [128, 20])
        self.epsc = self.sb("epsc", [128, 1])
        self.warm = self.sb("warm", [128, 1])
        self.lruh = [self.sb(f"lruh{j}", [128, 4]) for j in range(2)]
        self.carry = [self.sb(f"carry{j}", [128, 12]) for j in range(2)]
        self.m8c = self.sb("m8c", [128, 8])
        self.wl = self.sb("wl", [128, 1024])
        self.hb_kv = [Buf(True) for _ in range(2)]
        self.hb_kv_seen = [{}, {}]
        self.hb_R = [[Buf(True) for _ in range(4)] for _ in range(2)]
        self.ws_pos = 0
        self.ws_issued = 0
        self.ws_total = self.npp * len(self.segs)

        self.dma('sp', self.vec.f(), self.vec_d, w=[self.vec])
        self.dma('sp', self.zeta.f(), self.c_zeta, w=[self.zeta])
        self.dma('pool', self.ones.r(), self.c_ones, w=[self.ones])
        self.dma('pool', self.ident.r(), self.c_ident, w=[self.ident])
        self.dma('pool', self.tri.r(), self.c_tri, w=[self.tri])
        for k in range(4):
            self.dma('sp', self.masks[k].f(), self.c_masks[k], w=[self.masks[k]])
        self.op('dve', lambda h: h.memset(self.epsc.f(), EPS), w=[self.epsc])
        for zt in self.zt_free:
            self.op('dve', lambda h: h.tensor_scalar(out=zt.r(128, 0, 512), in0=self.masks[0].f(128, 0, 512), scalar1=0.0, scalar2=None, op0=ALU.mult), r=[self.masks[0]], w=[zt])
        self.op('act', lambda h: h.activation(out=self.vech.f(), in_=self.vec.f(), func=AF.Copy, scale=0.5), r=[self.vec], w=[self.vech])
        self.op('act', lambda h: h.activation(out=self.m8c.f(), in_=self.vec.t[0:128, VC_LAM:VC_LAM + 8], func=AF.Exp, scale=-1.0), r=[self.vec], w=[self.m8c])
        self.op('act', lambda h: h.activation(out=self.m8c.f(), in_=self.m8c.f(), func=AF.Ln, bias=1.0), r=[self.m8c], w=[self.m8c])
        self.op('act', lambda h: h.activation(out=self.m8c.f(), in_=self.m8c.f(), func=AF.Copy, scale=-8.0), r=[self.m8c], w=[self.m8c])
        for j in range(2):
            self.op('dve', lambda h: h.memset(self.lruh[j].f(), 0.0), w=[self.lruh[j]])
            self.op('dve', lambda h: h.memset(self.carry[j].f(), 0.0), w=[self.carry[j]])

        nprompt = sum(1 for s in self.segs if s['kind'] == 'p')
        for si, seg in enumerate(self.segs):
            T, col0 = seg['T'], seg['col0']
            seg['first'] = (seg['kind'] == 'p' and seg['s0'] == 0)
            seg['grp'] = 0 if seg['kind'] == 'p' else 1
            seg['last'] = (seg['kind'] == 's') or (si == nprompt - 1)
            if seg['kind'] == 's':
                for j in range(2):
                    self.dma('sp', self.lruh[j].f(), self.s_lruh[j], w=[self.lruh[j]])
                    self.dma('sp', self.carry[j].f(), self.s_conv[j], w=[self.carry[j]])
            for c in range(8):
                self.dma('sp', self.x[c].f(128, 0, T), self.xT[c, :, col0:col0 + T], w=[self.x[c]])
            for i in range(self.depth):
                self.mark(f"s{si} L{i} ffn0")
                self.ffn(i, 0, T)
                if i % 2 == 0 and self.en_ab:
                    self.mark(f"s{si} L{i} ab")
                    self.mixer_ab(i // 2, i, seg)
                if i % 2 == 1 and self.en_c:
                    self.mark(f"s{si} L{i} c")
                    self.mixer_c(i // 2, i, seg)
                self.mark(f"s{si} L{i} ffn1")
                self.ffn(i, 1, T)
                self.mark(f"s{si} L{i} ple")
                self.ple(i, seg)
            self.mark(f"s{si} end")
            for c in range(8):
                self.dma('sp', self.o_y[c, :, col0:col0 + T], self.x[c].f(128, 0, T), r=[self.x[c]])
            if seg['last']:
                for j in range(2):
                    self.dma('sp', self.o_lruh[j, seg['grp']], self.lruh[j].f(), r=[self.lruh[j]])
                    self.dma('sp', self.o_conv[j, seg['grp']], self.carry[j].f(), r=[self.carry[j]])
        sp = self.E['sp']
        for q in ('pool', 'sp'):
            ring = self.dring[q]
            for idx in range(ring['size']):
                if ring['val'][idx] > 0:
                    sp.h.wait_ge(self.sem[(q, idx)], ring['val'][idx])
        self.es.close()

    def mixer_ab(self, j, i, seg):
        T, col0, s0 = seg['T'], seg['col0'], seg['s0']
        grp = seg['grp']
        bs = min(128, T)
        nb = (T + 127) // 128
        self.hb_kv_seen[j] = dict(self.hb_kv[j].wm)
        hn = self.rmsnorm(vcol_norm(i, 2, 0), T)
        wv = [self.ws_next(('abv', j, q)) for q in range(4)]
        Vn = []
        for blk in range(nb):
            psv = self.ps_get()

            def fn(h):
                last = None
                for k in range(8):
                    last = h.matmul(psv.t[0:bs, 0:512], lhsT=hn[k].r(128, blk * 128, blk * 128 + bs), rhs=wv[k // 2].r(128, (k % 2) * 512, (k % 2) * 512 + 512),
                                    start=(k == 0), stop=(k == 7))
                return last
            self.op('pe', fn, r=hn + wv, w=[psv])
            vt = self.rget()
            self.evac(psv, vt.r(bs, 0, 512), bs, 512, vt)
            self.ps_put(psv)
            self.dma('sp', self.o_v[j, col0 + blk * 128: col0 + blk * 128 + bs, :], vt.f(bs, 0, 512), r=[vt], w=[self.hb_kv[j]])
            Vn.append(vt)
        wl = self.wl
        self.dma('pool', wl.r(), self.lrupack[j], w=[wl])
        def lru_gen(c):
                wxa = self.ws_next(('abxa', j, c))
                wga = self.ws_next(('abga', j, c))
                psx = self.linear(wxa, hn, T)
                psg = self.linear(wga, hn, T)
                xa = self.fget()
                self.evac(psx, xa.f(128, 0, T), 128, T, xa)
                self.ps_put(psx)
                acc = self.fget()
                xc = self.rget()
                cw = lambda tap: self.vc(VC_CONVW + (j * 4 + tap) * 4 + c)
                car = self.carry[j]
                self.op('dve', lambda h: h.tensor_scalar(out=acc.f(128, 0, T), in0=xa.f(128, 0, T), scalar1=cw(3), scalar2=self.vc(VC_CONVB + j * 4 + c),
                                                         op0=ALU.mult, op1=ALU.add), r=[xa, self.vec], w=[acc])
                for s in (1, 2, 3):
                    last = (s == 3)
                    dst = xc if last else acc
                    o_big = dst.r(128, s, T) if last else dst.f(128, s, T)
                    o_small = dst.r(128, 0, s) if last else dst.f(128, 0, s)
                    self.op('dve', lambda h: h.scalar_tensor_tensor(out=o_big, in0=xa.f(128, 0, T - s), scalar=cw(3 - s), in1=acc.f(128, s, T),
                                                                   op0=ALU.mult, op1=ALU.add), r=[xa, acc, self.vec], w=[dst])
                    self.op('dve', lambda h: h.scalar_tensor_tensor(out=o_small, in0=car.t[0:128, c * 3 + 3 - s: c * 3 + 3], scalar=cw(3 - s), in1=acc.f(128, 0, s),
                                                                   op0=ALU.mult, op1=ALU.add), r=[car, acc, self.vec], w=[dst])
                self.op('dve', lambda h: h.tensor_copy(out=car.t[0:128, c * 3: c * 3 + 3], in_=xa.f(128, T - 3, T)), r=[xa], w=[car])
                self.fput(xa, acc)
                yield
                psr = self.ps_get()
                psi = self.ps_get()
                self.op('pe', lambda h: h.matmul(psr.t[0:128, 0:T], lhsT=wl.r(128, c * 128, c * 128 + 128), rhs=xc.r(128, 0, T), start=True, stop=True), r=[wl, xc], w=[psr])
                self.op('pe', lambda h: h.matmul(psi.t[0:128, 0:T], lhsT=wl.r(128, 512 + c * 128, 512 + c * 128 + 128), rhs=xc.r(128, 0, T), start=True, stop=True), r=[wl, xc], w=[psi])
                rg = self.fget()
                ig = self.fget()
                self.op('act', lambda h: h.activation(out=rg.f(128, 0, T), in_=psr.t[0:128, 0:T], func=AF.Sigmoid, bias=self.vc(VC_BR + j * 4 + c)), r=[psr, self.vec], w=[rg])
                self.op('act', lambda h: h.activation(out=ig.f(128, 0, T), in_=psi.t[0:128, 0:T], func=AF.Sigmoid, bias=self.vc(VC_BI + j * 4 + c)), r=[psi, self.vec], w=[ig])
                self.ps_put(psr)
                self.ps_put(psi)
                yield
                a = self.fget()
                self.op('act', lambda h: h.activation(out=a.f(128, 0, T), in_=rg.f(128, 0, T), func=AF.Exp, scale=self.m8c.t[0:128, j * 4 + c: j * 4 + c + 1]), r=[rg, self.m8c], w=[a])
                self.op('act', lambda h: h.activation(out=rg.f(128, 0, T), in_=a.f(128, 0, T), func=AF.Square), r=[a], w=[rg])
                self.op('act', lambda h: h.activation(out=rg.f(128, 0, T), in_=rg.f(128, 0, T), func=AF.Sqrt, scale=-1.0, bias=1.0), r=[rg], w=[rg])
                self.op('dve', lambda h: h.tensor_tensor(out=ig.f(128, 0, T), in0=ig.f(128, 0, T), in1=rg.f(128, 0, T), op=ALU.mult), r=[ig, rg], w=[ig])
                self.op('dve', lambda h: h.tensor_tensor(out=ig.f(128, 0, T), in0=ig.f(128, 0, T), in1=xc.f(128, 0, T), op=ALU.mult), r=[ig, xc], w=[ig])
                yield
                hs = rg
                lh = self.lruh[j]
                self.op('dve', lambda h: h.tensor_tensor_scan(out=hs.f(128, 0, T), data0=a.f(128, 0, T), data1=ig.f(128, 0, T), initial=lh.t[0:128, c:c + 1],
                                                             op0=ALU.mult, op1=ALU.add), r=[a, ig, lh], w=[hs])
                self.op('dve', lambda h: h.tensor_copy(out=lh.t[0:128, c:c + 1], in_=hs.f(128, T - 1, T)), r=[hs], w=[lh])
                self.rput(xc)
                yield
                xg = a
                x2 = ig
                self.op('act', lambda h: h.activation(out=xg.f(128, 0, T), in_=psg.t[0:128, 0:T], func=AF.Copy), r=[psg], w=[xg])
                self.op('act', lambda h: h.activation(out=x2.f(128, 0, T), in_=psg.t[0:128, 0:T], func=AF.Square), r=[psg], w=[x2])
                self.ps_put(psg)
                self.op('dve', lambda h: h.tensor_scalar(out=x2.f(128, 0, T), in0=x2.f(128, 0, T), scalar1=0.044715, scalar2=1.0, op0=ALU.mult, op1=ALU.add), r=[x2], w=[x2])
                self.op('dve', lambda h: h.tensor_tensor(out=x2.f(128, 0, T), in0=x2.f(128, 0, T), in1=xg.f(128, 0, T), op=ALU.mult), r=[x2, xg], w=[x2])
                self.op('act', lambda h: h.activation(out=x2.f(128, 0, T), in_=x2.f(128, 0, T), func=AF.Sigmoid, scale=1.5957691216057308), r=[x2], w=[x2])
                self.op('dve', lambda h: h.tensor_tensor(out=x2.f(128, 0, T), in0=x2.f(128, 0, T), in1=xg.f(128, 0, T), op=ALU.mult), r=[x2, xg], w=[x2])
                ca = self.rget()
                self.op('dve', lambda h: h.tensor_tensor(out=ca.r(128, 0, T), in0=x2.f(128, 0, T), in1=hs.f(128, 0, T), op=ALU.mult), r=[x2, hs], w=[ca])
                cat_a[c] = ca
                self.fput(rg, ig, a)

        cat_a = [None] * 4
        for c0_ in (0,):
            alive = [lru_gen(c_) for c_ in range(4)]
            while alive:
                for g_ in list(alive):
                    try:
                        next(g_)
                    except StopIteration:
                        alive.remove(g_)
        if seg['kind'] == 'p':
            npast = s0 // 128
            pastK = lambda hh, a0, a1: self.o_kT[j, hh, :, a0:a1]
            pastV = lambda a0, a1, hh: self.o_v[j, a0:a1, (hh // 2) * 128:(hh // 2) * 128 + 128]
        else:
            npast = SEQ // 128
            pastK = lambda hh, a0, a1: self.s_kT[j, hh, :, a0:a1]
            pastV = lambda a0, a1, hh: self.s_v[j, a0:a1, (hh // 2) * 128:(hh // 2) * 128 + 128]
        ngrp = (npast + 3) // 4
        hb_past = Buf(True)
        hb_past.wm = dict(self.hb_kv_seen[j])
        KQ = 128 if bs == 128 else 64
        ob = [self.rget() for _ in range(4)]
        H = {}
        items = []
        for hh in range(8):
            blocks = [('n', b_) for b_ in range(nb - 1, -1, -1)]
            for g in range(ngrp - 1, -1, -1):
                for b_ in range(min(4, npast - g * 4) - 1, -1, -1):
                    blocks.append(('p', g, b_))
            for bi, blk in enumerate(blocks):
                items.append(dict(hh=hh, blk=blk, first=(bi == 0), last=(bi == len(blocks) - 1),
                                  pre_next=(bi == min(3, len(blocks) - 1) and hh < 7)))

        def load_group(hh, g):
            nbg = min(4, npast - g * 4)
            pk = self.zget()
            pv = self.rget()
            self.dma('pool', pk.r(64, 0, nbg * 128), pastK(hh, g * 512, g * 512 + nbg * 128), r=[hb_past], w=[pk])
            for b2 in range(nbg):
                self.dma('pool', pv.r(128, b2 * 128, b2 * 128 + 128), pastV(g * 512 + b2 * 128, g * 512 + b2 * 128 + 128, hh), r=[hb_past], w=[pv])
            H[hh]['grp'][g] = (pk, pv)

        def head_pre(hh):
            wqk = self.ws_next(('abqk', j, hh))
            psq = self.linear(wqk, hn, T, kw=64, M=64, coff=0)
            psk = self.linear(wqk, hn, T, kw=64, M=64, coff=512)
            qT = self.zget()
            kT = self.zget()
            nkT = self.zget()
            self.evac(psq, qT.r(64, 0, T), 64, T, qT)
            self.evac(psk, kT.r(64, 0, T), 64, T, kT)
            self.op('act', lambda h: h.activation(out=nkT.r(64, 0, T), in_=psq.t[0:64, 0:T], func=AF.Copy, scale=-0.125), r=[psq], w=[nkT])
            self.ps_put(psq)
            self.ps_put(psk)
            self.dma('sp', self.o_kT[j, hh, :, col0:col0 + T], kT.f(64, 0, T), r=[kT], w=[self.hb_kv[j]])
            H[hh] = dict(qT=qT, kT=kT, nkT=nkT, A=None, pso=None, grp={})
            if npast > 0:
                load_group(hh, ngrp - 1)

        def head_post(hh):
            c_ = H.pop(hh)
            obt = ob[hh // 2]
            r0 = (hh % 2) * 64
            pso_ = c_['pso']
            self.op('act', lambda h: h.activation(out=obt.t[r0:r0 + 64, 0:T].bitcast(F32R), in_=pso_.t[r0:r0 + 64, 0:T], func=AF.Copy), r=[pso_], w=[obt])
            self.ps_put(c_['pso'])
            self.zput(c_['qT'], c_['kT'], c_['nkT'])

        def S1(it):
            hh, blk = it['hh'], it['blk']
            if it['first'] and hh == 0:
                head_pre(hh)
            c_ = H[hh]
            qT = c_['qT']
            if blk[0] == 'n':
                kb = blk[1]
                P_ = bs
                it['k'] = (c_['kT'], c_['kT'].r(KQ, kb * 128, kb * 128 + bs))
                it['v'] = (Vn[kb], Vn[kb].r(bs, (hh // 2) * 128, (hh // 2) * 128 + 128))
                it['mask'] = self.masks[kb]
            else:
                g, b_ = blk[1], blk[2]
                P_ = 128
                nbg = min(4, npast - g * 4)
                if b_ == nbg - 1 and g >= 1:
                    load_group(hh, g - 1)
                pk, pv = c_['grp'][g]
                it['k'] = (pk, pk.r(128, b_ * 128, b_ * 128 + 128))
                it['v'] = (pv, pv.r(128, b_ * 128, b_ * 128 + 128))
                it['mask'] = None
            it['P'] = P_
            it['KQ'] = KQ if blk[0] == 'n' else 128
            KK = it['KQ']
            mask = it['mask']
            psz = self.ps_get()
            self.op('pe', lambda h: h.matmul(psz.t[0:P_, 0:T], lhsT=it['k'][1], rhs=qT.r(KK, 0, T), start=True, stop=True), r=[it['k'][0], qT], w=[psz])
            e = self.fget()
            spt = self.rget()
            self.op('act', lambda h: h.activation(out=e.f(P_, 0, T), in_=psz.t[0:P_, 0:T], func=AF.Exp, scale=0.125), r=[psz], w=[e])
            self.ps_put(psz)
            if mask is None:
                self.op('act', lambda h: h.activation(out=spt.r(P_, 0, T), in_=e.f(P_, 0, T), func=AF.Ln, bias=1.0), r=[e], w=[spt])
            else:
                self.op('act', lambda h: h.activation(out=e.f(P_, 0, T), in_=e.f(P_, 0, T), func=AF.Ln, bias=1.0), r=[e], w=[e])
                self.op('dve', lambda h: h.tensor_tensor(out=spt.r(P_, 0, T), in0=e.f(P_, 0, T), in1=mask.f(P_, 0, T), op=ALU.mult), r=[e, mask], w=[spt])
            it['e'] = e
            it['spt'] = spt
            if it['pre_next']:
                head_pre(hh + 1)

        def S1b(it):
            hh = it['hh']
            c_ = H[hh]
            P_ = it['P']
            spt = it['spt']
            it['A'] = c_['A']
            if not it['last']:
                pst = self.ps_get()
                self.op('pe', lambda h: h.matmul(pst.t[0:128, 0:T], lhsT=self.ones.r(P_, 0, 128), rhs=spt.r(P_, 0, T), start=True, stop=True), r=[self.ones, spt], w=[pst])
                An = self.rget()
                Ap = it['A']
                if Ap is None:
                    self.op('dve', lambda h: h.tensor_copy(out=An.r(128, 0, T), in_=pst.t[0:128, 0:T]), r=[pst], w=[An])
                else:
                    self.op('dve', lambda h: h.tensor_tensor(out=An.r(128, 0, T), in0=pst.t[0:128, 0:T], in1=Ap.f(128, 0, T), op=ALU.add), r=[pst, Ap], w=[An])
                self.ps_put(pst)
                c_['A'] = An

        def S2(it):
            hh = it['hh']
            c_ = H[hh]
            P_ = it['P']
            spt, e, mask = it['spt'], it['e'], it['mask']
            acc, qT = it['A'], c_['qT']
            psc = self.ps_get()

            def fn(h):
                h.matmul(psc.t[0:P_, 0:T], lhsT=self.tri.r(P_, 0, P_), rhs=spt.r(P_, 0, T), start=True, stop=False)
                if not it['first']:
                    h.matmul(psc.t[0:P_, 0:T], lhsT=self.ident.r(P_, 0, P_), rhs=acc.r(P_, 0, T), start=False, stop=False)
                return h.matmul(psc.t[0:P_, 0:T], lhsT=it['k'][1], rhs=c_['nkT'].r(it['KQ'], 0, T), start=False, stop=True)
            self.op('pe', fn, r=[self.tri, spt, self.ident, it['k'][0], c_['nkT']] + ([] if it['first'] else [acc]), w=[psc])
            Pt = self.rget()
            if mask is None:
                self.op('act', lambda h: h.activation(out=Pt.r(P_, 0, T), in_=psc.t[0:P_, 0:T], func=AF.Exp, scale=-1.0), r=[psc], w=[Pt])
            else:
                self.op('act', lambda h: h.activation(out=e.f(P_, 0, T), in_=psc.t[0:P_, 0:T], func=AF.Exp, scale=-1.0), r=[psc], w=[e])
                self.op('dve', lambda h: h.tensor_tensor(out=Pt.r(P_, 0, T), in0=e.f(P_, 0, T), in1=mask.f(P_, 0, T), op=ALU.mult), r=[e, mask], w=[Pt])
            self.ps_put(psc)
            if acc is not None:
                self.rput(acc)
            self.fput(e)
            self.rput(spt)
            it['Pt'] = Pt

        def S3(it):
            hh = it['hh']
            c_ = H[hh]
            P_ = it['P']
            Pt = it['Pt']
            if c_['pso'] is None:
                c_['pso'] = self.ps_get()
            self.op('pe', lambda h: h.matmul(c_['pso'].t[0:128, 0:T], lhsT=it['v'][1], rhs=Pt.r(P_, 0, T), start=it['first'], stop=it['last']), r=[it['v'][0], Pt], w=[c_['pso']])
            self.rput(Pt)
            blk = it['blk']
            if blk[0] == 'p' and blk[2] == 0:
                pk_, pv_ = c_['grp'].pop(blk[1])
                self.zput(pk_)
                self.rput(pv_)
            if it['last']:
                head_post(hh)

        stages = [S1, S1b, S2, S3]
        nit = len(items)
        self.mark(f"ab{j} attn")
        for step in range(nit + (len(stages) - 1) * ATT_SKEW):
            for si_, st in enumerate(stages):
                idx = step - si_ * ATT_SKEW
                if 0 <= idx < nit:
                    st(items[idx])
        self.mark(f"ab{j} attn_end")
        self.rput(*hn)
        self.rput(*Vn)
        y = [self.fget() for _ in range(8)]
        for c in range(8):
            wo = self.ws_next(('abo', j, c))
            psy = self.linear(wo, cat_a + ob, T)
            self.evac(psy, y[c].f(128, 0, T), 128, T, y[c])
            self.ps_put(psy)
        self.rput(*cat_a)
        self.rput(*ob)
        self.postnorm_add(y, vcol_norm(i, 3, 0), False, T)
        self.fput(*y)

    def mixer_c(self, j, i, seg):
        T, col0, s0 = seg['T'], seg['col0'], seg['s0']
        grp = seg['grp']
        bs = min(128, T)
        nb = (T + 127) // 128
        has_state = not seg['first']
        gT = self.C['gT'][T]
        hn = self.rmsnorm(vcol_norm(i, 2, 0), T)
        rot = [self.fget() for _ in range(4)]
        for t_ in range(4):
            self.dma('sp', rot[t_].f(128, 0, T), self.c_rot[t_, :, col0:col0 + T], w=[rot[t_]])
        y = [self.fget() for _ in range(8)]
        def head_gen(hh):
                if has_state:
                    xit = self.fget()
                    self.dma('sp', xit.f(128, 0, T), self.c_xi[hh, :, 0:T], w=[xit])
                qr = []
                kr = []
                for which in range(2):
                    for cc in range(2):
                        wq = self.ws_next(('cq' if which == 0 else 'ck', j, hh, cc))
                        ps = self.linear(wq, hn, T)
                        raw = self.fget()
                        self.evac(ps, raw.f(128, 0, T), 128, T, raw)
                        self.ps_put(ps)
                        (qr if which == 0 else kr).append(raw)
                outs = []
                for which, (raws, cs_, sn_) in enumerate(((qr, rot[0], rot[1]), (kr, rot[2], rot[3]))):
                    x1, x2 = raws
                    t1 = self.fget()
                    t2 = self.fget()
                    o1 = self.rget()
                    o2 = self.rget()
                    self.op('dve', lambda h: h.tensor_tensor(out=t1.f(128, 0, T), in0=x1.f(128, 0, T), in1=cs_.f(128, 0, T), op=ALU.mult), r=[x1, cs_], w=[t1])
                    self.op('dve', lambda h: h.tensor_tensor(out=t2.f(128, 0, T), in0=x2.f(128, 0, T), in1=sn_.f(128, 0, T), op=ALU.mult), r=[x2, sn_], w=[t2])
                    self.op('dve', lambda h: h.tensor_tensor(out=o1.r(128, 0, T), in0=t1.f(128, 0, T), in1=t2.f(128, 0, T), op=ALU.subtract), r=[t1, t2], w=[o1])
                    self.op('dve', lambda h: h.tensor_tensor(out=t1.f(128, 0, T), in0=x2.f(128, 0, T), in1=cs_.f(128, 0, T), op=ALU.mult), r=[x2, cs_], w=[t1])
                    self.op('dve', lambda h: h.tensor_tensor(out=t2.f(128, 0, T), in0=x1.f(128, 0, T), in1=sn_.f(128, 0, T), op=ALU.mult), r=[x1, sn_], w=[t2])
                    self.op('dve', lambda h: h.tensor_tensor(out=o2.r(128, 0, T), in0=t1.f(128, 0, T), in1=t2.f(128, 0, T), op=ALU.add), r=[t1, t2], w=[o2])
                    self.fput(t1, t2, x1, x2)
                    outs.append([o1, o2])
                qrr, krr = outs
                Rt = [self.rget() for _ in range(2)]
                if has_state:
                    for cc in range(2):
                        src = self.o_R[j, 0, hh, cc] if seg['kind'] == 'p' else self.s_R[j, hh, cc]
                        self.dma('pool', Rt[cc].r(128, 0, 512), src, r=[self.hb_R[j][hh]], w=[Rt[cc]])
                    qx = [self.rget() for _ in range(2)]
                    for cc in range(2):
                        self.op('dve', lambda h: h.tensor_tensor(out=qx[cc].r(128, 0, T), in0=qrr[cc].f(128, 0, T), in1=xit.f(128, 0, T), op=ALU.mult), r=[qrr[cc], xit], w=[qx[cc]])
                    self.fput(xit)
                wv = [self.ws_next(('cv', j, hh, q)) for q in range(4)]
                V = []
                for blk in range(nb):
                    psv = self.ps_get()

                    def fn(h):
                        last = None
                        for k in range(8):
                            last = h.matmul(psv.t[0:bs, 0:512], lhsT=hn[k].r(128, blk * 128, blk * 128 + bs), rhs=wv[k // 2].r(128, (k % 2) * 512, (k % 2) * 512 + 512),
                                            start=(k == 0), stop=(k == 7))
                        return last
                    self.op('pe', fn, r=hn + wv, w=[psv])
                    vt = self.rget()
                    self.evac(psv, vt.r(bs, 0, 512), bs, 512, vt)
                    self.ps_put(psv)
                    V.append(vt)
                yield 'A'
                Dts = []
                for mb in range(nb):
                    Dt_ = self.fget()
                    self.dma('sp', Dt_.f(bs, 0, T), self.c_decayT[hh, mb * 128: mb * 128 + bs, 0:T], w=[Dt_])
                    Dts.append(Dt_)
                kz = []
                for blk in range(nb):
                    kzt = self.rget()
                    zc = hh * 5 + (blk if T == 512 else 4)
                    for cc in range(2):
                        pst = self.ps_get()
                        self.op('pe', lambda h: h.matmul(pst.t[0:bs, 0:128], lhsT=krr[cc].r(128, blk * 128, blk * 128 + bs), rhs=self.ident.r(128, 0, 128), start=True, stop=True),
                                r=[krr[cc], self.ident], w=[pst])
                        self.op('dve', lambda h: h.tensor_scalar(out=kzt.r(bs, cc * 128, cc * 128 + 128), in0=pst.t[0:bs, 0:128], scalar1=self.zeta.t[0:bs, zc:zc + 1], scalar2=None,
                                                                 op0=ALU.mult), r=[pst, self.zeta], w=[kzt])
                        self.ps_put(pst)
                    kz.append(kzt)
                pso = [self.ps_get() for _ in range(4)]
                Pts = {}

                def sc1(mb):
                    pss = self.ps_get()

                    def fn(h):
                        last = None
                        for cc in range(2):
                            last = h.matmul(pss.t[0:bs, 0:T], lhsT=krr[cc].r(128, mb * 128, mb * 128 + bs), rhs=qrr[cc].r(128, 0, T), start=(cc == 0), stop=(cc == 1))
                        return last
                    self.op('pe', fn, r=krr + qrr, w=[pss])
                    Dt = Dts[mb]
                    Pt = self.rget()
                    self.op('dve', lambda h: h.tensor_tensor(out=Pt.r(bs, 0, T), in0=pss.t[0:bs, 0:T], in1=Dt.f(bs, 0, T), op=ALU.mult), r=[pss, Dt], w=[Pt])
                    self.ps_put(pss)
                    self.fput(Dt)
                    Pts[mb] = Pt

                def sc2(mb):
                    Pt = Pts.pop(mb)
                    for dc in range(4):
                        self.op('pe', lambda h: h.matmul(pso[dc].t[0:128, 0:T], lhsT=V[mb].r(bs, dc * 128, dc * 128 + 128), rhs=Pt.r(bs, 0, T),
                                                         start=(mb == 0), stop=(mb == nb - 1 and not has_state)), r=[V[mb], Pt], w=[pso[dc]])
                    self.rput(Pt)
                sc1(0)
                for mb in range(1, nb):
                    sc1(mb)
                    sc2(mb - 1)
                sc2(nb - 1)
                if has_state:
                    for dc in range(4):
                        def fn(h):
                            last = None
                            for cc in range(2):
                                last = h.matmul(pso[dc].t[0:128, 0:T], lhsT=Rt[cc].r(128, dc * 128, dc * 128 + 128), rhs=qx[cc].r(128, 0, T), start=False, stop=(cc == 1))
                            return last
                        self.op('pe', fn, r=Rt + qx, w=[pso[dc]])
                    self.rput(*qx)
                for cc in range(2):
                    psr = self.ps_get()

                    def fn(h):
                        last = None
                        for mb in range(nb):
                            last = h.matmul(psr.t[0:128, 0:512], lhsT=kz[mb].r(bs, cc * 128, cc * 128 + 128), rhs=V[mb].r(bs, 0, 512), start=(mb == 0), stop=(mb == nb - 1))
                        return last
                    self.op('pe', fn, r=kz + V, w=[psr])
                    Rn = self.fget()
                    if has_state:
                        self.op('dve', lambda h: h.scalar_tensor_tensor(out=Rn.f(128, 0, 512), in0=Rt[cc].f(128, 0, 512), scalar=gT[hh], in1=psr.t[0:128, 0:512],
                                                                       op0=ALU.mult, op1=ALU.add), r=[Rt[cc], psr], w=[Rn])
                    else:
                        self.op('dve', lambda h: h.tensor_copy(out=Rn.f(128, 0, 512), in_=psr.t[0:128, 0:512]), r=[psr], w=[Rn])
                    self.ps_put(psr)
                    self.dma('sp', self.o_R[j, grp, hh, cc], Rn.f(128, 0, 512), r=[Rn], w=[self.hb_R[j][hh]])
                    self.fput(Rn)
                self.rput(*Rt)
                self.rput(*kz)
                self.rput(*V)
                self.rput(*qrr)
                self.rput(*krr)
                osb = [self.rget() for _ in range(4)]
                osq = [self.rget() for _ in range(4)]
                for dc in range(4):
                    self.op('act', lambda h: h.activation(out=osb[dc].r(128, 0, T), in_=pso[dc].t[0:128, 0:T], func=AF.Copy), r=[pso[dc]], w=[osb[dc]])
                    self.op('act', lambda h: h.activation(out=osq[dc].r(128, 0, T), in_=pso[dc].t[0:128, 0:T], func=AF.Square), r=[pso[dc]], w=[osq[dc]])
                    self.ps_put(pso[dc])
                yield 'Ba'
                ps1 = self.ps_get()
                ps2 = self.ps_get()

                def fn1(h):
                    last = None
                    for dc in range(4):
                        last = h.matmul(ps1.t[0:128, 0:T], lhsT=self.ones.r(128, 0, 128), rhs=osb[dc].r(128, 0, T), start=(dc == 0), stop=(dc == 3))
                    return last

                def fn2(h):
                    last = None
                    for dc in range(4):
                        last = h.matmul(ps2.t[0:128, 0:T], lhsT=self.ones.r(128, 0, 128), rhs=osq[dc].r(128, 0, T), start=(dc == 0), stop=(dc == 3))
                    return last
                self.op('pe', fn1, r=[self.ones] + osb, w=[ps1])
                self.op('pe', fn2, r=[self.ones] + osq, w=[ps2])
                self.rput(*osq)
                mean = self.fget()
                var = self.fget()
                self.op('act', lambda h: h.activation(out=mean.f(128, 0, T), in_=ps1.t[0:128, 0:T], func=AF.Copy, scale=1.0 / 512), r=[ps1], w=[mean])
                self.op('act', lambda h: h.activation(out=var.f(128, 0, T), in_=ps1.t[0:128, 0:T], func=AF.Square, scale=1.0 / 512), r=[ps1], w=[var])
                self.ps_put(ps1)
                self.op('dve', lambda h: h.scalar_tensor_tensor(out=var.f(128, 0, T), in0=ps2.t[0:128, 0:T], scalar=1.0 / 512, in1=var.f(128, 0, T), op0=ALU.mult, op1=ALU.subtract),
                        r=[ps2, var], w=[var])
                self.ps_put(ps2)
                self.op('act', lambda h: h.activation(out=var.f(128, 0, T), in_=var.f(128, 0, T), func=AF.Ln, bias=self.epsc.t[0:128, 0:1]), r=[var, self.epsc], w=[var])
                self.op('act', lambda h: h.activation(out=var.f(128, 0, T), in_=var.f(128, 0, T), func=AF.Exp, scale=-0.5), r=[var], w=[var])
                go = []
                for dc in range(4):
                    wg = self.ws_next(('cg', j, hh, dc))
                    psg = self.linear(wg, hn, T)
                    sg = self.fget()
                    self.op('act', lambda h: h.activation(out=sg.f(128, 0, T), in_=psg.t[0:128, 0:T], func=AF.Silu), r=[psg], w=[sg])
                    self.ps_put(psg)
                    t = self.fget()
                    self.op('dve', lambda h: h.tensor_tensor(out=t.f(128, 0, T), in0=osb[dc].f(128, 0, T), in1=mean.f(128, 0, T), op=ALU.subtract), r=[osb[dc], mean], w=[t])
                    self.op('dve', lambda h: h.tensor_tensor(out=t.f(128, 0, T), in0=t.f(128, 0, T), in1=var.f(128, 0, T), op=ALU.mult), r=[t, var], w=[t])
                    g_ = self.rget()
                    self.op('dve', lambda h: h.tensor_tensor(out=g_.r(128, 0, T), in0=t.f(128, 0, T), in1=sg.f(128, 0, T), op=ALU.mult), r=[t, sg], w=[g_])
                    self.fput(sg, t)
                    go.append(g_)
                self.rput(*osb)
                self.fput(mean, var)
                self.act_warm()
                for c in range(8):
                    wo = self.ws_next(('co', j, hh, c))
                    psy = self.linear(wo, go, T)
                    if hh == 0:
                        self.evac(psy, y[c].f(128, 0, T), 128, T, y[c])
                    else:
                        self.op('dve', lambda h: h.tensor_tensor(out=y[c].f(128, 0, T), in0=psy.t[0:128, 0:T], in1=y[c].f(128, 0, T), op=ALU.add), r=[psy, y[c]], w=[y[c]])
                    self.ps_put(psy)
                self.rput(*go)

        gens = [head_gen(h_) for h_ in range(4)]
        next(gens[0])
        for h_ in range(4):
            next(gens[h_])
            if h_ + 1 < 4:
                next(gens[h_ + 1])
            for _ in gens[h_]:
                pass
        self.rput(*hn)
        self.fput(*rot)
        self.postnorm_add(y, vcol_norm(i, 3, 0), False, T)
        self.fput(*y)


_CACHE = {}


def host_inputs(I, b, wpack, vec, C):
    xT = np.concatenate([I['x_prompt'][b].T, I['x_sample'][b].T], axis=1).reshape(8, 128, NCOL)
    pT = np.concatenate([I['p_prompt'][:, b].transpose(0, 2, 1), I['p_sample'][:, b].transpose(0, 2, 1)], axis=2).reshape(DEPTH, 2, 128, NCOL)
    m = {
        'wpack': wpack, 'lrupack': np.stack([piece_array(('ablru', j), I) for j in range(2)]), 'xT': np.ascontiguousarray(xT), 'pT': np.ascontiguousarray(pT), 'vec': vec,
        'c_ones': C['ones'], 'c_ident': C['ident'], 'c_tri': C['tri'], 'c_masks': C['masks'], 'c_rot': C['rot'],
        'c_decayT': C['decayT'], 'c_xi': C['xi'], 'c_zeta': C['zeta'],
        's_lruh': np.ascontiguousarray(I['state_lru_h'][:, b].reshape(2, 4, 128).transpose(0, 2, 1)),
        's_conv': np.ascontiguousarray(I['state_conv'][:, b].reshape(2, 3, 4, 128).transpose(0, 3, 2, 1).reshape(2, 128, 12)),
        's_kT': np.ascontiguousarray(I['cache_sb_k'][:, b].transpose(0, 2, 3, 1)),
        's_v': np.ascontiguousarray(I['cache_sb_v'][:, b].reshape(2, SEQ, 512)),
        's_R': np.ascontiguousarray(I['state_ret'][:, b].reshape(2, 4, 2, 128, 512)),
    }
    return m


def run(I, depth=DEPTH, en_ab=True, en_c=True, ncores=8, trace=False):
    key = (depth, en_ab, en_c)
    if key not in _CACHE:
        _CACHE[key] = Builder(depth, en_ab, en_c)
    B = _CACHE[key]
    I = {k: np.asarray(v) for k, v in I.items()}
    wpack = np.stack([piece_array(s, I) for s in B.specs])
    vec = build_vec(I)
    in_maps = [host_inputs(I, b, wpack, vec, B.C) for b in range(ncores)]
    res = run_bass_kernel_spmd(B.nc, in_maps, core_ids=list(range(ncores)), trace=trace)
    return res


def assemble(results, nb=8):
    R = results
    y = np.stack([r['o_y'].reshape(D, NCOL).T for r in R])
    y_p, y_s = y[:, :SEQ], y[:, SEQ:]
    lruh = np.stack([r['o_lruh'] for r in R])
    h_all = lruh.transpose(1, 2, 0, 4, 3).reshape(2, 2, nb, 512)
    conv = np.stack([r['o_conv'] for r in R]).reshape(nb, 2, 2, 128, 4, 3)
    conv = conv.transpose(1, 2, 0, 5, 4, 3).reshape(2, 2, nb, 3, 512)
    kT = np.stack([r['o_kT'] for r in R])
    k = kT.transpose(1, 0, 4, 2, 3)
    v = np.stack([r['o_v'] for r in R]).reshape(nb, 2, NCOL, 8, 64).transpose(1, 0, 2, 3, 4)
    Rr = np.stack([r['o_R'] for r in R]).reshape(nb, 2, 2, 4, 256, 512)
    Rr = Rr.transpose(1, 2, 0, 3, 4, 5)
    c = np.ascontiguousarray
    return (c(y_p), c(y_s), c(h_all[:, 0]), c(conv[:, 0]), c(k[:, :, :SEQ]), c(v[:, :, :SEQ]), c(Rr[:, 0]),
            c(h_all[:, 1]), c(conv[:, 1]), c(k[:, :, SEQ:]), c(v[:, :, SEQ:]), c(Rr[:, 1]))


def kernel(**inputs):
    res = run(inputs)
    return assemble(res.results)
```
